# Optimizing a Trainium2 kernel written in Bass

```python
import math
import jax, jax.numpy as jnp
from jax import lax
import numpy as np

D_MODEL = 2048
BATCH = 2
SEQ = 4096
DEPTH = 2

HEAD_DIM = 128
DIL_HEADS = 6
DIL_BRANCHES = ((128, 1), (512, 4), (2048, 16))
DIL_BLOCK = 64
DIFF_HEADS = 4
DIFF_QK_DIM = HEAD_DIM // 2
DIFF_BLOCK = 128
SWA_Q_HEADS = 6
SWA_KV_HEADS = 2
SWA_RADIUS = 128
SWA_BLOCK = 128
D_FF = 5504

N_ALIBI_HEADS = SWA_Q_HEADS + DIL_HEADS + DIFF_HEADS
MIX_WIDTH = (DIL_HEADS + DIFF_HEADS + SWA_Q_HEADS) * HEAD_DIM
A_QKV = DIL_HEADS * HEAD_DIM
B_QK = DIFF_HEADS * 2 * DIFF_QK_DIM
B_V = DIFF_HEADS * HEAD_DIM
C_Q = SWA_Q_HEADS * HEAD_DIM
C_KV = SWA_KV_HEADS * HEAD_DIM
IN_SPLITS = (A_QKV, A_QKV, A_QKV, B_QK, B_QK, B_V, C_Q, C_KV, C_KV)
IN_WIDTH = sum(IN_SPLITS)
RMS_EPS = 1e-6
NEG = -1e30

kernel_name = "hybrid_dilated_diff_swa_macaron_encoder"


def rms_norm(x, g):
    xf = x.astype(jnp.float32)
    y = xf * lax.rsqrt(jnp.mean(xf * xf, axis=-1, keepdims=True) + RMS_EPS)
    return (y * g.astype(jnp.float32)).astype(x.dtype)


def swiglu(h, w_gate, w_up, w_down):
    return (jax.nn.silu(h @ w_gate) * (h @ w_up)) @ w_down


def alibi_slopes():
    n = N_ALIBI_HEADS
    return jnp.asarray(2.0 ** (-8.0 * np.arange(1, n + 1) / n), dtype=jnp.float32)


def banded_attention(q, k, v, radius, blk, slopes, dist_scale, sink=None):
    B, L, H, Dh = q.shape
    G = k.shape[2]
    R = H // G
    nb = -(-L // blk)
    Lp = nb * blk
    pad = Lp - L
    qb = jnp.pad(q, ((0, 0), (0, pad), (0, 0), (0, 0))).reshape(B, nb, blk, G, R, Dh)
    kp = jnp.pad(k, ((0, 0), (blk, pad + blk), (0, 0), (0, 0))).reshape(B, nb + 2, blk, G, Dh)
    vp = jnp.pad(v, ((0, 0), (blk, pad + blk), (0, 0), (0, 0))).reshape(B, nb + 2, blk, G, Dh)
    kb = jnp.concatenate([kp[:, :-2], kp[:, 1:-1], kp[:, 2:]], axis=2)
    vb = jnp.concatenate([vp[:, :-2], vp[:, 1:-1], vp[:, 2:]], axis=2)
    s = jnp.einsum('bnqgrd,bnkgd->bngrqk', qb, kb,
                   preferred_element_type=jnp.float32) * (Dh ** -0.5)
    a = jnp.arange(blk)
    c = jnp.arange(3 * blk)
    rel = c[None, :] - blk - a[:, None]
    kpos = jnp.arange(nb)[:, None] * blk + c[None, :] - blk
    valid = (jnp.abs(rel) <= radius)[None] & ((kpos >= 0) & (kpos < L))[:, None, :]
    bias = slopes.reshape(G, R)[:, :, None, None] * (dist_scale * jnp.abs(rel).astype(jnp.float32))
    s = jnp.where(valid[None, :, None, None], s - bias[None, None], NEG)
    m = jnp.max(s, axis=-1)
    if sink is not None:
        sink_b = sink.astype(jnp.float32).reshape(G, R)[None, None, :, :, None]
        m = jnp.maximum(m, sink_b)
    p = jnp.exp(s - m[..., None])
    den = jnp.sum(p, axis=-1)
    if sink is not None:
        den = den + jnp.exp(sink_b - m)
    o = jnp.einsum('bngrqk,bnkgd->bnqgrd', p.astype(v.dtype), vb,
                   preferred_element_type=jnp.float32)
    o = o / den.transpose(0, 1, 4, 2, 3)[..., None]
    lse = (m + jnp.log(den)).transpose(0, 1, 4, 2, 3)
    return o.reshape(B, Lp, H, Dh)[:, :L], lse.reshape(B, Lp, H)[:, :L]


def dilated_attention(q, k, v, slopes):
    B, S, H, Dh = q.shape
    outs, lses = [], []
    for window, dil in DIL_BRANCHES:
        L = S // dil

        def to_sub(t):
            return t.reshape(B, L, dil, H, Dh).transpose(0, 2, 1, 3, 4).reshape(B * dil, L, H, Dh)

        o, lse = banded_attention(to_sub(q), to_sub(k), to_sub(v), window // (2 * dil),
                                  DIL_BLOCK, slopes, float(dil))
        outs.append(o.reshape(B, dil, L, H, Dh).transpose(0, 2, 1, 3, 4).reshape(B, S, H, Dh))
        lses.append(lse.reshape(B, dil, L, H).transpose(0, 2, 1, 3).reshape(B, S, H))
    w = jax.nn.softmax(jnp.stack(lses, axis=0), axis=0)
    return jnp.einsum('ibsh,ibshd->bshd', w, jnp.stack(outs, axis=0))


def diff_attention(q, k, v, lam, slopes):
    B, S, H, _, Dq = q.shape
    Dv = v.shape[-1]
    nblk = S // DIFF_BLOCK
    qb = q.reshape(B, nblk, DIFF_BLOCK, H, 2, Dq).transpose(1, 0, 2, 3, 4, 5)
    kpos = jnp.arange(S)

    def block(args):
        qi, i = args
        s = jnp.einsum('bqhcd,bkhcd->bhcqk', qi, k,
                       preferred_element_type=jnp.float32) * (Dq ** -0.5)
        qpos = i * DIFF_BLOCK + jnp.arange(DIFF_BLOCK)
        dist = jnp.abs(qpos[:, None] - kpos[None, :]).astype(jnp.float32)
        s = s - slopes[None, :, None, None, None] * dist
        p = jax.nn.softmax(s, axis=-1)
        a = p[:, :, 0] - lam * p[:, :, 1]
        return jnp.einsum('bhqk,bkhd->bqhd', a.astype(v.dtype), v,
                          preferred_element_type=jnp.float32)

    o = lax.map(block, (qb, jnp.arange(nblk)))
    return o.transpose(1, 0, 2, 3, 4).reshape(B, S, H, Dv)


def mixing(h, w_in, w_out, lq1, lk1, lq2, lk2, subln_g, sink, lambda_init):
    B, S, _ = h.shape
    slopes = alibi_slopes()
    sl_c = slopes[:SWA_Q_HEADS]
    sl_a = slopes[SWA_Q_HEADS:SWA_Q_HEADS + DIL_HEADS]
    sl_b = slopes[SWA_Q_HEADS + DIL_HEADS:]
    idx = [int(i) for i in np.cumsum(IN_SPLITS)[:-1]]
    aq, ak, av, bq, bk, bv, cq, ck, cv = jnp.split(h @ w_in, idx, axis=-1)
    hd = lambda t, n: t.reshape(B, S, n, HEAD_DIM)
    out_a = dilated_attention(hd(aq, DIL_HEADS), hd(ak, DIL_HEADS), hd(av, DIL_HEADS), sl_a)
    lam = (jnp.exp(jnp.sum(lq1.astype(jnp.float32) * lk1.astype(jnp.float32)))
           - jnp.exp(jnp.sum(lq2.astype(jnp.float32) * lk2.astype(jnp.float32))) + lambda_init)
    out_b = diff_attention(bq.reshape(B, S, DIFF_HEADS, 2, DIFF_QK_DIM),
                           bk.reshape(B, S, DIFF_HEADS, 2, DIFF_QK_DIM),
                           hd(bv, DIFF_HEADS), lam, sl_b)
    out_b = rms_norm(out_b, subln_g) * (1.0 - lambda_init)
    out_c, _ = banded_attention(hd(cq, SWA_Q_HEADS), hd(ck, SWA_KV_HEADS), hd(cv, SWA_KV_HEADS),
                                SWA_RADIUS, SWA_BLOCK, sl_c, 1.0, sink)
    y = jnp.concatenate([out_a.reshape(B, S, -1), out_b.reshape(B, S, -1),
                         out_c.reshape(B, S, -1)], axis=-1).astype(h.dtype)
    return y @ w_out


def setup_inputs(seed: int = 0) -> dict:
    key = jax.random.key(seed)
    ks = jax.random.split(key, 20)
    f32 = jnp.float32

    def nrm(k, shape, scale):
        return jax.random.normal(k, shape, f32) * scale

    def gain(k, shape):
        return 1.0 + 0.05 * jax.random.normal(k, shape, f32)

    return {
        "x": nrm(ks[0], (BATCH, SEQ, D_MODEL), 1.0),
        "ffn1_norm": gain(ks[1], (DEPTH, D_MODEL)),
        "ffn1_w_gate": nrm(ks[2], (DEPTH, D_MODEL, D_FF), D_MODEL ** -0.5),
        "ffn1_w_up": nrm(ks[3], (DEPTH, D_MODEL, D_FF), D_MODEL ** -0.5),
        "ffn1_w_down": nrm(ks[4], (DEPTH, D_FF, D_MODEL), D_FF ** -0.5),
        "mix_norm": gain(ks[5], (DEPTH, D_MODEL)),
        "w_in": nrm(ks[6], (DEPTH, D_MODEL, IN_WIDTH), D_MODEL ** -0.5),
        "w_out": nrm(ks[7], (DEPTH, MIX_WIDTH, D_MODEL), MIX_WIDTH ** -0.5),
        "diff_lambda_q1": nrm(ks[8], (DEPTH, DIFF_QK_DIM), 0.1),
        "diff_lambda_k1": nrm(ks[9], (DEPTH, DIFF_QK_DIM), 0.1),
        "diff_lambda_q2": nrm(ks[10], (DEPTH, DIFF_QK_DIM), 0.1),
        "diff_lambda_k2": nrm(ks[11], (DEPTH, DIFF_QK_DIM), 0.1),
        "diff_subln": gain(ks[12], (DEPTH, HEAD_DIM)),
        "swa_sink": nrm(ks[13], (DEPTH, SWA_Q_HEADS), 1.0),
        "ffn2_norm": gain(ks[14], (DEPTH, D_MODEL)),
        "ffn2_w_gate": nrm(ks[15], (DEPTH, D_MODEL, D_FF), D_MODEL ** -0.5),
        "ffn2_w_up": nrm(ks[16], (DEPTH, D_MODEL, D_FF), D_MODEL ** -0.5),
        "ffn2_w_down": nrm(ks[17], (DEPTH, D_FF, D_MODEL), D_FF ** -0.5),
        "final_norm": gain(ks[18], (D_MODEL,)),
    }


def reference(x, ffn1_norm, ffn1_w_gate, ffn1_w_up, ffn1_w_down, mix_norm, w_in, w_out,
              diff_lambda_q1, diff_lambda_k1, diff_lambda_q2, diff_lambda_k2, diff_subln,
              swa_sink, ffn2_norm, ffn2_w_gate, ffn2_w_up, ffn2_w_down, final_norm):
    for l in range(DEPTH):
        lambda_init = 0.8 - 0.6 * math.exp(-0.3 * l)
        h = rms_norm(x, ffn1_norm[l])
        x = x + 0.5 * swiglu(h, ffn1_w_gate[l], ffn1_w_up[l], ffn1_w_down[l])
        h = rms_norm(x, mix_norm[l])
        x = x + mixing(h, w_in[l], w_out[l], diff_lambda_q1[l], diff_lambda_k1[l],
                       diff_lambda_q2[l], diff_lambda_k2[l], diff_subln[l], swa_sink[l],
                       lambda_init)
        h = rms_norm(x, ffn2_norm[l])
        x = x + 0.5 * swiglu(h, ffn2_w_gate[l], ffn2_w_up[l], ffn2_w_down[l])
    return rms_norm(x, final_norm)
```

```python
import contextlib
import math
import numpy as np
import ml_dtypes
import concourse.bass as bass
import concourse.mybir as mybir
from concourse.bass_utils import run_bass_kernel_spmd

F32 = mybir.dt.float32
BF16 = mybir.dt.bfloat16
ALU = mybir.AluOpType
AF = mybir.ActivationFunctionType
AX = mybir.AxisListType
NPBF = ml_dtypes.bfloat16

D_MODEL = 2048
BATCH = 2
SEQ = 4096
DEPTH = 2
D_FF = 5504
NF = D_FF // 128
NC_ = 16
TL = 1024
NCORES = 8
RMS_EPS = 1e-6
BIG = 30000.0
GSZ = 5
N_DUMMY = 0
DUMMY_N = 128

ENGS = ("pe", "act", "dve", "pool", "sp")


class Buf:
    __slots__ = ("w", "wd", "r", "rd", "name")

    def __init__(self, name=""):
        self.w = {}
        self.wd = []
        self.r = {}
        self.rd = []
        self.name = name

    def alias(self, olds):
        for b in olds:
            for src in (b.w, b.r):
                for e, o in src.items():
                    cur = self.r.get(e)
                    if cur is None or cur.idx < o.idx:
                        self.r[e] = o
            self.rd.extend(b.wd)
            self.rd.extend(b.rd)
        return self


class Op:
    __slots__ = ("eng", "fn", "deps", "needed", "dma", "sem", "semval", "prev_semval", "idx", "inc")

    def __init__(self, eng, fn, dma, idx):
        self.eng = eng
        self.fn = fn
        self.dma = dma
        self.idx = idx
        self.deps = []
        self.needed = False
        self.sem = None
        self.semval = 0
        self.prev_semval = 0


class Sched:
    def __init__(self, n_dma_sems=8):
        self.ops = {e: [] for e in ENGS}
        self.n_dma_sems = n_dma_sems
        self.nops = 0

    def op(self, eng, fn, reads=(), writes=(), dma=False, inc=16):
        o = Op(eng, fn, dma, self.nops)
        o.inc = inc
        deps = {}

        def add(d):
            if (not d.dma) and (not dma) and d.eng == "pe" and eng == "pe":
                return
            deps[id(d)] = d

        for b in reads:
            for d in b.w.values():
                add(d)
            for d in b.wd:
                add(d)
        for b in writes:
            for d in b.w.values():
                add(d)
            for d in b.wd:
                add(d)
            for d in b.r.values():
                add(d)
            for d in b.rd:
                add(d)
        for b in reads:
            if dma:
                b.rd.append(o)
            else:
                b.r[eng] = o
        for b in writes:
            b.r = {}
            b.rd = []
            if dma:
                b.w = {}
                b.wd = [o]
            else:
                b.w = {eng: o}
                b.wd = []
        o.deps = list(deps.values())
        for d in o.deps:
            d.needed = True
        self.ops[eng].append(o)
        self.nops += 1
        return o

    def emit(self, nc, stack):
        esem = {e: stack.enter_context(nc.semaphore("S_" + e)) for e in ENGS}
        dsem = {}
        for e in ENGS:
            for inc in sorted(set(o.inc for o in self.ops[e] if o.dma)):
                dsem[(e, inc)] = [stack.enter_context(nc.semaphore("D_%s%d_%d" % (e, inc, i)))
                                  for i in range(self.n_dma_sems if inc == 16 else 2)]
        for e in ENGS:
            c = 0
            kk = {}
            cn = {}
            for o in self.ops[e]:
                if o.dma:
                    pool = dsem[(e, o.inc)]
                    k = kk.get(o.inc, 0)
                    cnts = cn.setdefault(o.inc, [0] * len(pool))
                    i = k % len(pool)
                    o.sem = pool[i]
                    o.prev_semval = cnts[i]
                    cnts[i] += o.inc
                    o.semval = cnts[i]
                    kk[o.inc] = k + 1
                elif o.needed:
                    c += 1
                    o.sem = esem[e]
                    o.semval = c
        block = stack.enter_context(nc.Block())
        handles = {"pe": block.tensor, "act": block.scalar, "dve": block.vector,
                   "pool": block.gpsimd, "sp": block.sync}
        stats = {}
        for e in ENGS:
            ops = self.ops[e]
            if not ops:
                continue
            nwait = [0]

            def body(eng, ops=ops, nwait=nwait):
                known = {}
                for o in ops:
                    waits = {}
                    for d in o.deps:
                        key = id(d.sem)
                        if known.get(key, 0) >= d.semval:
                            continue
                        if key not in waits or waits[key][1] < d.semval:
                            waits[key] = (d.sem, d.semval)
                    if o.dma and o.prev_semval > 0:
                        key = id(o.sem)
                        if known.get(key, 0) < o.prev_semval:
                            if key not in waits or waits[key][1] < o.prev_semval:
                                waits[key] = (o.sem, o.prev_semval)
                    for key, (s, v) in waits.items():
                        eng.wait_ge(s, v)
                        known[key] = v
                        nwait[0] += 1
                    if o.fn is None:
                        continue
                    inst = o.fn(eng)
                    if o.dma:
                        inst.then_inc(o.sem, o.inc)
                    elif o.needed:
                        inst.then_inc(o.sem, 1)

            handles[e](body)
            stats[e] = (len(ops), nwait[0])
        return stats


O_XT = 0
O_RSTD = 65536
O_GAIN = O_RSTD + 4096
O_CONST = O_GAIN + 512
RB = O_CONST + 512
O_H = RB
O_Q = RB + 32768
O_Z = RB + 65536
O_WGU = O_Q
O_SG = O_Q + 24576
O_SQ_F = O_Q + 28672
O_WD = O_Z
O_ACT = O_Z + 2 * GSZ * 4096
NWR = 6
NWS = 3
O_WIN = O_Z
O_WST = O_Z + NWR * 4096
O_KST = O_WST + NWS * 8192
O_VST = O_KST + 4096
O_SQ_W = O_VST + 12288
O_TB = O_Z
O_TA1 = O_TB + 19968
O_EA1 = O_TA1 + 2048
O_TA4 = O_EA1 + 2048
O_TA16 = O_TA4 + 4096
O_TC = O_TA16 + 4096
O_EC = O_TC + 2304
O_STG = O_EC + 2048
O_SB = O_STG + 16640
O_PT = O_SB + 6144
O_TMP = O_PT + 4096
O_SM = O_TMP + 6144
O_DACC = O_SM + 2048
ATT_END = O_DACC
O_WO = O_Z
ARENA_BYTES = max(ATT_END, O_ACT + 2 * GSZ * 2048, O_WO + 32768)

WIN_ROLE = (["q"] * 6 + ["k"] * 6 + ["v"] * 6 + ["q"] * 4 + ["k"] * 4 + ["v"] * 4
            + ["q"] * 6 + ["k"] * 2 + ["v"] * 2)


def alibi_slopes():
    return [2.0 ** (-8.0 * (i + 1) / 16) for i in range(16)]


class Prog:
    def __init__(self, phases):
        self.phases = phases
        self.nc = nc = bass.Bass("TRN2", target_bir_lowering=False)
        self.S = Sched()
        self.dram = {}
        self.stack = contextlib.ExitStack()
        st = self.stack
        self.arena = st.enter_context(nc.sbuf_tensor("arena", [128, ARENA_BYTES // 2], BF16))
        self.pb_ap = [st.enter_context(nc.psum_tensor("pb%d" % i, [128, 512], F32)) for i in range(8)]
        self.pb = [Buf("pb%d" % i) for i in range(8)]
        self.xT = self.f32(O_XT, 16 * 1024).rearrange("p (c t) -> p c t", c=16)
        self.xb = [[Buf("x%d_%d" % (c, h)) for h in range(2)] for c in range(16)]
        self.rstd = self.f32(O_RSTD, 1024)
        self.rstd_b = Buf("rstd")
        self.gain = self.f32(O_GAIN, 7 * 16).rearrange("p (v c) -> p v c", v=7)
        self.gain_b = Buf("gain")
        self.ones = self.bf(O_CONST, 128)
        self.eps = self.f32(O_CONST + 256, 1)
        self.zeros = self.arena[0:1, (O_CONST + 264) // 2:(O_CONST + 264) // 2 + 120]
        self.const_b = Buf("const")
        self.hT = self.bf(O_H, 16 * 1024).rearrange("p (c t) -> p c t", c=16)
        self.hb = [Buf("h%d" % c) for c in range(16)]
        self.QT = self.bf(O_Q, 16 * 1024).rearrange("p (c t) -> p c t", c=16)
        self.qb = [Buf("q%d" % c) for c in range(16)]
        self.region_bufs = []
        self.out_bufs = []
        self.build()

    def bf(self, off, n):
        return self.arena[:, off // 2: off // 2 + n]

    def f32(self, off, n):
        return self.arena[:, off // 2: off // 2 + 2 * n].bitcast(F32)

    def din(self, name, shape, dt):
        if name not in self.dram:
            self.dram[name] = self.nc.dram_tensor(name, list(shape), dt, kind="ExternalInput").ap()
        return self.dram[name]

    def dout(self, name, shape, dt):
        if name not in self.dram:
            self.dram[name] = self.nc.dram_tensor(name, list(shape), dt, kind="ExternalOutput").ap()
        return self.dram[name]

    def _rank_of(self, e, k):
        if not hasattr(self, "_rk"):
            self._rk = {}
            cc = e.partition_id() % 4
            for kk in (-1, 0, 1):
                self._rk[kk] = e.snap((cc + (kk + 4)) % 4)
        return self._rk[k]

    def dint(self, name, shape, dt):
        if name not in self.dram:
            self.dram[name] = self.nc.dram_tensor(name, list(shape), dt).ap()
        return self.dram[name]

    def newbufs(self, names):
        bs = [Buf(n).alias(self.region_bufs) for n in names]
        return bs

    def mm(self, out, lhsT, rhs, start, stop, reads, writes, skip=False):
        kw = {"skip_group_check": True} if skip else {}
        self.S.op("pe", lambda e: e.matmul(out, lhsT=lhsT, rhs=rhs, start=start, stop=stop, **kw),
                  reads=reads, writes=writes)

    def dma(self, eng, out, in_, reads, writes):
        return self.S.op(eng, lambda e: e.dma_start(out=out, in_=in_), reads=reads, writes=writes, dma=True)

    def act(self, out, in_, func, reads, writes, scale=None, bias=None):
        kw = {}
        if scale is not None:
            kw["scale"] = scale
        if bias is not None:
            kw["bias"] = bias
        self.S.op("act", lambda e: e.activation(out=out, in_=in_, func=func, **kw), reads=reads, writes=writes)

    def stt(self, out, in0, scalar, in1, op0, op1, reads, writes, eng="dve"):
        self.S.op(eng, lambda e: e.scalar_tensor_tensor(out=out, in0=in0, scalar=scalar, in1=in1, op0=op0, op1=op1),
                  reads=reads, writes=writes)

    def tt(self, out, in0, in1, op, reads, writes, eng="dve"):
        self.S.op(eng, lambda e: e.tensor_tensor(out=out, in0=in0, in1=in1, op=op), reads=reads, writes=writes)

    def powact(self, out, in_, reads, writes, power=-1.0, scale_in=None, bias=None):
        self.act(out, in_, AF.Ln, reads, writes, scale=scale_in, bias=bias)
        self.act(out, out, AF.Exp, [], writes, scale=power)

    def recip(self, out, in_, reads, writes):
        self.S.op("dve", lambda e: e.reciprocal(out=out, in_=in_), reads=reads, writes=writes)

    def build(self):
        S = self.S
        S.op("pool", lambda e: e.memset(self.ones, 1.0), writes=[self.const_b])
        S.op("pool", lambda e: e.memset(self.eps, RMS_EPS), writes=[self.const_b])
        S.op("pool", lambda e: e.memset(self.zeros, 0.0), writes=[self.const_b])
        g_in = self.din("gains", [128, 7 * 16], F32)
        self.dma("sp", self.f32(O_GAIN, 7 * 16), g_in, [], [self.gain_b])
        for ph in self.phases:
            kind = ph[0]
            if kind == "load_x":
                self.ph_load_x()
            elif kind == "load_state":
                self.ph_load_state()
            elif kind == "ffn":
                self.ph_ffn(ph[1], ph[2])
            elif kind == "win":
                self.ph_win(ph[1])
            elif kind == "store_state":
                self.ph_store_state()
            elif kind == "attn":
                self.ph_attn(ph[1])
            elif kind == "xchg":
                self.ph_xchg(ph[1])
            elif kind == "wout":
                self.ph_wout(ph[1])
            elif kind == "final":
                self.ph_final()
            elif kind == "dump_h":
                h_out = self.dout("hT_out", [16, 128, 1024], BF16)
                for c in range(16):
                    ob = Buf("ho")
                    self.dma("sp", h_out[c], self.hT[:, c, :], [self.hb[c]], [ob])
                    self.out_bufs.append(ob)
            else:
                raise ValueError(kind)
        S.op("sp", None, reads=self.out_bufs)
        self.stats = S.emit(self.nc, self.stack)

    def ph_load_x(self):
        x_in = self.din("xT_in", [16, 128, 1024], F32)
        for c in range(16):
            self.dma("sp", self.xT[:, c, :], x_in[c], [], [self.xb[c][0], self.xb[c][1]])

    def ph_load_state(self):
        self.ph_load_x()
        q_in = self.din("QT_in", [16, 128, 1024], BF16)
        for b in self.qb:
            b.alias(self.region_bufs)
        self.region_bufs = list(self.qb)
        for c in range(16):
            self.dma("sp", self.QT[:, c, :], q_in[c], [], [self.qb[c]])

    def ph_store_state(self):
        x_out = self.dout("xT_out", [16, 128, 1024], F32)
        q_out = self.dout("QT_out", [16, 128, 1024], BF16)
        for c in range(16):
            ob = Buf("xo")
            self.dma("sp", x_out[c], self.xT[:, c, :], [self.xb[c][0], self.xb[c][1]], [ob])
            self.out_bufs.append(ob)
        for c in range(16):
            ob = Buf("qo")
            self.dma("sp", q_out[c], self.QT[:, c, :], [self.qb[c]], [ob])
            self.out_bufs.append(ob)

    def norm(self, vidx, sq_off, sq_bufs, final_out=None):
        sq = [self.bf(sq_off + i * 2048, 1024) for i in range(2)]
        P6, P7 = self.pb[6], self.pb[7]
        for c in range(16):
            i = c % 2
            if c % 2 == 0:
                self.act(sq[i], self.xT[:, c, :], AF.Square, [self.xb[c][0], self.xb[c][1]], [sq_bufs[i]])
            else:
                self.tt(sq[i], self.xT[:, c, :], self.xT[:, c, :], ALU.mult, [self.xb[c][0], self.xb[c][1]], [sq_bufs[i]])
            for h in range(2):
                self.mm(self.pb_ap[6 + h][:], self.ones, sq[i][:, h * 512:(h + 1) * 512], c == 0, c == 15,
                        [sq_bufs[i], self.const_b], [self.pb[6 + h]])
        for h in range(2):
            r = self.rstd[:, h * 512:(h + 1) * 512]
            self.powact(r, self.pb_ap[6 + h][:], [self.pb[6 + h], self.const_b], [self.rstd_b],
                        power=-0.5, scale_in=1.0 / D_MODEL, bias=self.eps)
        for c in range(16):
            if final_out is None:
                self.stt(self.hT[:, c, :], self.xT[:, c, :], self.gain[:, vidx, c:c + 1], self.rstd,
                         ALU.mult, ALU.mult, [self.xb[c][0], self.xb[c][1], self.gain_b, self.rstd_b], [self.hb[c]])
            else:
                self.stt(self.xT[:, c, :], self.xT[:, c, :], self.gain[:, vidx, c:c + 1], self.rstd,
                         ALU.mult, ALU.mult, [self.gain_b, self.rstd_b], [self.xb[c][0], self.xb[c][1]])
                ob = Buf("fo")
                self.dma("sp", final_out[c], self.xT[:, c, :], [self.xb[c][0], self.xb[c][1]], [ob])
                self.out_bufs.append(ob)

    def ph_final(self):
        out = self.dout("outT", [16, 128, 1024], F32)
        sqb = self.newbufs(["sq0", "sq1"])
        self.region_bufs = sqb
        self.norm(6, O_SQ_F, sqb, final_out=out)

    def ph_ffn(self, l, which):
        S = self.S
        tag = "%d_%d" % (l, which)
        wg_d = self.din("wg_" + tag, [NF, 128, 2048], F32)
        wu_d = self.din("wu_" + tag, [NF, 128, 2048], F32)
        wd_d = self.din("wd_" + tag, [NF, 128, 2048], F32)
        names = (["wgu%d" % i for i in range(3)] + ["wd%d" % i for i in range(2)] + ["sg0", "sg1", "sq0", "sq1"]
                 + ["act%d_%d_%d" % (a, j, h) for a in range(2) for j in range(GSZ) for h in range(2)])
        bl = self.newbufs(names)
        wgu_b = bl[0:3]
        wd_b = bl[3:5]
        sg_b = bl[5:7]
        sq_b = bl[7:9]
        act_b = [[[bl[9 + (a * GSZ + j) * 2 + h] for h in range(2)] for j in range(GSZ)] for a in range(2)]
        self.region_bufs = bl
        wg_sb = [self.bf(O_WGU + s * 8192, 2048).rearrange("p (c f) -> p c f", c=16) for s in range(3)]
        wu_sb = [self.bf(O_WGU + s * 8192 + 4096, 2048).rearrange("p (c f) -> p c f", c=16) for s in range(3)]
        wd_sb = [self.bf(O_WD + s * GSZ * 4096, GSZ * 2048).rearrange("p (g d) -> p g d", g=GSZ) for s in range(2)]
        act_sb = [self.bf(O_ACT + s * GSZ * 2048, GSZ * 1024).rearrange("p (g t) -> p g t", g=GSZ) for s in range(2)]
        sg_sb = [self.f32(O_SG + s * 2048, 512) for s in range(2)]
        vidx = l * 3 + (0 if which == 1 else 2)
        self.norm(vidx, O_SQ_F, sq_b)

        groups = []
        j = 0
        while j < NF:
            groups.append(list(range(j, min(j + GSZ, NF))))
            j += GSZ
        ng = len(groups)

        def issue_gu(jj):
            s = jj % 3
            self.dma("pool", wg_sb[s].rearrange("p c f -> p (c f)"), wg_d[jj], [], [wgu_b[s]])
            self.dma("pool", wu_sb[s].rearrange("p c f -> p (c f)"), wu_d[jj], [], [wgu_b[s]])

        def issue_wd(g):
            js = groups[g]
            s = g % 2
            for gi, jj in enumerate(js):
                self.dma("pool", wd_sb[s][:, gi, :], wd_d[jj], [], [wd_b[s]])

        dma_plan = []
        for g in range(ng):
            js = groups[g]
            for jj in js[:3]:
                dma_plan.append(("gu", jj))
            dma_plan.append(("wd", g))
            for jj in js[3:]:
                dma_plan.append(("gu", jj))
        self._dma_plan = dma_plan
        self._dma_pos = 0

        plan_pos = {item: i for i, item in enumerate(dma_plan)}

        def pump(until_gu=None, until_wd=None):
            target = ("gu", until_gu) if until_gu is not None else ("wd", until_wd)
            end = plan_pos[target] if until_gu is not None or until_wd is not None else len(self._dma_plan) - 1
            while self._dma_pos <= end:
                k, v = self._dma_plan[self._dma_pos]
                if k == "gu":
                    issue_gu(v)
                else:
                    issue_wd(v)
                self._dma_pos += 1

        unit = [0]

        def emit_gu(g):
            a = g % 2
            for gi, jj in enumerate(groups[g]):
                pump(until_gu=min(jj + 2, NF - 1))
                s = jj % 3
                if jj == 0:
                    ks_ = []
                    for h in range(2):
                        ks_.append(unit[0] % 2)
                        unit[0] += 1
                    for c in range(16):
                        for h in range(2):
                            pg, pu = 2 * ks_[h], 2 * ks_[h] + 1
                            self.mm(self.pb_ap[pg][:], wg_sb[s][:, c, :], self.hT[:, c, h * 512:(h + 1) * 512],
                                    c == 0, c == 15, [wgu_b[s], self.hb[c]], [self.pb[pg]])
                            self.mm(self.pb_ap[pu][:], wu_sb[s][:, c, :], self.hT[:, c, h * 512:(h + 1) * 512],
                                    c == 0, c == 15, [wgu_b[s], self.hb[c]], [self.pb[pu]])
                    for h in range(2):
                        k = ks_[h]
                        pg, pu = 2 * k, 2 * k + 1
                        self.act(sg_sb[k], self.pb_ap[pg][:], AF.Silu, [self.pb[pg]], [sg_b[k]])
                        self.tt(act_sb[a][:, gi, h * 512:(h + 1) * 512], sg_sb[k], self.pb_ap[pu][:], ALU.mult,
                                [sg_b[k], self.pb[pu]], [act_b[a][gi][h]])
                    continue
                for h in range(2):
                    k = unit[0] % 2
                    unit[0] += 1
                    pg, pu = 2 * k, 2 * k + 1
                    for c in range(16):
                        self.mm(self.pb_ap[pg][:], wg_sb[s][:, c, :], self.hT[:, c, h * 512:(h + 1) * 512],
                                c == 0, c == 15, [wgu_b[s], self.hb[c]], [self.pb[pg]])
                    for c in range(16):
                        self.mm(self.pb_ap[pu][:], wu_sb[s][:, c, :], self.hT[:, c, h * 512:(h + 1) * 512],
                                c == 0, c == 15, [wgu_b[s], self.hb[c]], [self.pb[pu]])
                    self.act(sg_sb[k], self.pb_ap[pg][:], AF.Silu, [self.pb[pg]], [sg_b[k]])
                    self.tt(act_sb[a][:, gi, h * 512:(h + 1) * 512], sg_sb[k], self.pb_ap[pu][:], ALU.mult,
                            [sg_b[k], self.pb[pu]], [act_b[a][gi][h]])

        dunit = [0]

        def emit_d(g):
            a = g % 2
            s = g % 2
            js = groups[g]
            n = len(js)
            for c in range(16):
                for h in range(2):
                    bk = 4 + dunit[0] % 4
                    dunit[0] += 1
                    for gi in range(n):
                        self.mm(self.pb_ap[bk][:], wd_sb[s][:, gi, c * 128:(c + 1) * 128],
                                act_sb[a][:, gi, h * 512:(h + 1) * 512], gi == 0, gi == n - 1,
                                [wd_b[s], act_b[a][gi][h]], [self.pb[bk]])
                    xs = self.xT[:, c, h * 512:(h + 1) * 512]
                    self.stt(xs, self.pb_ap[bk][:], 0.5, xs, ALU.mult, ALU.add, [self.pb[bk]], [self.xb[c][h]])

        emit_gu(0)
        for g in range(ng):
            if g + 1 < ng:
                emit_gu(g + 1)
            pump(until_wd=g)
            emit_d(g)

    def ph_win(self, l):
        S = self.S
        w_d = self.din("win_%d" % l, [40, 128, 2048], F32)
        kt_out = self.dint("KT_own_%d" % l, [12, 128, 1024], BF16)
        v_out = self.dint("V_own_%d" % l, [12, 1024, 128], BF16)
        k_all = self.dint("KT_all_%d" % l, [4, 4 * 384, 1024], BF16)
        v_all = self.dint("V_all_%d" % l, [4, 4 * 3072, 128], BF16)
        self.dk_own = [Buf("dk%d" % i) for i in range(12)]
        self.dv_own = [Buf("dv%d" % i) for i in range(12)]
        self.dk_all = [Buf("dkall%d" % g) for g in range(4)]
        self.dv_all = [Buf("dvall%d" % g) for g in range(4)]
        cc_groups = [[0, 1, 2, 3], [4, 5, 6, 7]]
        bl = self.newbufs(["win%d" % i for i in range(NWR)] + ["kst0", "kst1", "vst0", "vst1", "sq0", "sq1"]
                          + ["wst%d" % i for i in range(NWS)])
        for b in self.qb:
            b.alias(self.region_bufs)
        win_b = bl[0:NWR]
        kst_b = bl[NWR:NWR + 2]
        vst_b = bl[NWR + 2:NWR + 4]
        sq_b = bl[NWR + 4:NWR + 6]
        wst_b = bl[NWR + 6:NWR + 6 + NWS]
        wst = [self.f32(O_WST + i * 8192, 2048) for i in range(NWS)]
        self.region_bufs = bl + self.qb
        win_sb = [self.bf(O_WIN + s * 4096, 2048).rearrange("p (c f) -> p c f", c=16) for s in range(NWR)]
        kst = [self.bf(O_KST + s * 2048, 1024) for s in range(2)]
        vst = [self.bf(O_VST + s * 6144, 3072).rearrange("p (t g c) -> p t g c", t=8, g=3) for s in range(2)]
        self.norm(l * 3 + 1, O_SQ_W, sq_b)
        role_idx = {}
        cnt = {"q": 0, "k": 0, "v": 0}
        for j in range(40):
            r = WIN_ROLE[j]
            role_idx[j] = (r, cnt[r])
            cnt[r] += 1
        order = (list(range(22, 26)) + [36, 37] + list(range(26, 30)) + [38, 39]
                 + list(range(6, 12)) + list(range(12, 18))
                 + list(range(18, 22)) + list(range(0, 6)) + list(range(30, 36)))
        assert sorted(order) == list(range(40))
        bank = [0]

        def nb():
            b = bank[0] % 8
            bank[0] += 1
            return b

        issued = [0]

        def pump(upto):
            while issued[0] <= min(upto, 39):
                p_ = issued[0]
                j = order[p_]
                s = p_ % NWR
                ws = p_ % NWS
                self.dma("sp", wst[ws], w_d[j], [], [wst_b[ws]])
                self.S.op("dve", (lambda e, o=win_sb[s].rearrange("p c f -> p (c f)"), i=wst[ws]:
                                  e.tensor_copy(out=o, in_=i)), reads=[wst_b[ws]], writes=[win_b[s]])
                issued[0] += 1

        kdone = set()
        vdone = set()
        nk = nv = 0
        for pos, j in enumerate(order):
            pump(pos + 4)
            s = pos % NWR
            role, idx = role_idx[j]
            if role in ("q", "k"):
                if role == "k":
                    ks = nk % 2
                    nk += 1
                for h in range(2):
                    b = nb()
                    for c in range(16):
                        self.mm(self.pb_ap[b][:], win_sb[s][:, c, :], self.hT[:, c, h * 512:(h + 1) * 512],
                                c == 0, c == 15, [win_b[s], self.hb[c]], [self.pb[b]])
                    if role == "q":
                        self.act(self.QT[:, idx, h * 512:(h + 1) * 512], self.pb_ap[b][:], AF.Copy,
                                 [self.pb[b]], [self.qb[idx]])
                    else:
                        self.act(kst[ks][:, h * 512:(h + 1) * 512], self.pb_ap[b][:], AF.Copy,
                                 [self.pb[b]], [kst_b[ks]])
                if role == "k":
                    self.dma("act", kt_out[idx], kst[ks], [kst_b[ks]], [self.dk_own[idx]])
                    kdone.add(idx)
                    g = idx // 3
                    if all((3 * g + i) in kdone for i in range(3)):
                        kin = kt_out[3 * g:3 * g + 3].rearrange("h p t -> (h p) t")
                        S.op("pool", (lambda e, kin=kin, g=g: e.collective_compute(
                            "AllGather", ALU.bypass, replica_groups=cc_groups, ins=[kin.opt()], outs=[k_all[g].opt()])),
                            reads=self.dk_own[3 * g:3 * g + 3], writes=[self.dk_all[g]], dma=True, inc=1)
            else:
                g = idx // 3
                if idx % 3 != 0:
                    continue
                vs = nv % 2
                nv += 1
                s0 = pos % NWR
                assert s0 + 2 < NWR and [role_idx[order[pos + i]] for i in range(3)] == [("v", idx + i) for i in range(3)]
                w3 = self.bf(O_WIN + s0 * 4096, 3 * 2048).rearrange("p (g c f) -> p g c f", g=3, c=16)
                for tt_ in range(8):
                    b = nb()
                    o3 = self.pb_ap[b][:, 0:384].rearrange("p (g f) -> p g f", g=3)
                    for c in range(16):
                        self.mm(o3, self.hT[:, c, tt_ * 128:(tt_ + 1) * 128], w3[:, :, c, :], c == 0, c == 15,
                                [win_b[s0], win_b[s0 + 1], win_b[s0 + 2], self.hb[c]], [self.pb[b]])
                    self.S.op("dve", (lambda e, o=vst[vs][:, tt_, :, :], i=o3: e.tensor_copy(out=o, in_=i)),
                              reads=[self.pb[b]], writes=[vst_b[vs]])
                for i in range(3):
                    self.dma("act", v_out[3 * g + i].rearrange("(t p) c -> p t c", p=128), vst[vs][:, :, i, :],
                             [vst_b[vs]], [self.dv_own[3 * g + i]])
                vin = v_out[3 * g:3 * g + 3].rearrange("h t c -> (h t) c")
                S.op("pool", (lambda e, vin=vin, g=g: e.collective_compute(
                    "AllGather", ALU.bypass, replica_groups=cc_groups, ins=[vin.opt()], outs=[v_all[g].opt()])),
                    reads=self.dv_own[3 * g:3 * g + 3], writes=[self.dv_all[g]], dma=True, inc=1)
        self.kall, self.vall = k_all, v_all

    def ph_wout(self, l):
        w_d = self.din("wout_%d" % l, [4, 128, 16 * 512], F32)
        bl = self.newbufs(["wo0", "wo1"])
        self.region_bufs = bl
        wo_sb = [self.bf(O_WO + s * 16384, 8192).rearrange("p (c d) -> p c d", c=16) for s in range(2)]
        for dg in range(2):
            self.dma("pool", wo_sb[dg].rearrange("p c d -> p (c d)"), w_d[dg], [], [bl[dg]])
        u = 0
        for dg in range(4):
            s = dg % 2
            for dc in range(4):
                c = dg * 4 + dc
                for h in range(2):
                    b = u % 8
                    u += 1
                    for hc in range(16):
                        self.mm(self.pb_ap[b][:], wo_sb[s][:, hc, dc * 128:(dc + 1) * 128],
                                self.hT[:, hc, h * 512:(h + 1) * 512], hc == 0, hc == 15,
                                [bl[s], self.hb[hc]], [self.pb[b]])
                    xs = self.xT[:, c, h * 512:(h + 1) * 512]
                    self.tt(xs, self.pb_ap[b][:], xs, ALU.add, [self.pb[b]], [self.xb[c][h]])
            if dg + 2 < 4:
                self.dma("pool", wo_sb[s].rearrange("p c d -> p (c d)"), w_d[dg + 2], [], [bl[s]])

    def ph_xchg(self, l):
        S = self.S
        k_all, v_all = self.kall, self.vall
        kpad = self.dint("Kpad_%d" % l, [8, 128, 3072], BF16)
        vpad = self.dint("Vpad_%d" % l, [8, 3072, 128], BF16)
        self.dkpad = [[Buf("dkpad") for _ in range(3)] for _ in range(3)]
        self.dvpad = [[Buf("dvpad") for _ in range(3)] for _ in range(3)]
        k_all4 = k_all.rearrange("g (r q) t -> g r q t", r=4)
        v_all4 = v_all.rearrange("g (r q) c -> g r q c", r=4)
        for k in (-1, 0, 1):
            c0 = (k + 1) * 1024
            for di, (g, hl, nh, dh) in enumerate([(0, 0, 3, 0), (1, 0, 3, 3), (3, 1, 2, 6)]):
                def kdma(e, k=k, c0=c0, g=g, hl=hl, nh=nh, dh=dh):
                    rank = self._rank_of(e, k)
                    src = k_all4[g][bass.ds(rank, 1)][0, hl * 128:(hl + nh) * 128, :]
                    dst = kpad[dh:dh + nh, :, c0:c0 + 1024].rearrange("h p t -> (h p) t")
                    return e.dma_start(out=dst, in_=src)

                def vdma(e, k=k, c0=c0, g=g, hl=hl, nh=nh, dh=dh):
                    rank = self._rank_of(e, k)
                    src = v_all4[g][bass.ds(rank, 1)][0, hl * 1024:(hl + nh) * 1024, :].rearrange("(h t) c -> h t c", h=nh)
                    return e.dma_start(out=vpad[dh:dh + nh, c0:c0 + 1024, :], in_=src)

                S.op("sp", kdma, reads=[self.dk_all[g]], writes=[self.dkpad[di][k + 1]], dma=True)
                S.op("sp", vdma, reads=[self.dv_all[g]], writes=[self.dvpad[di][k + 1]], dma=True)
        self.kpad, self.vpad = kpad, vpad

    def ph_attn(self, l):
        S = self.S
        lam_init = 0.8 - 0.6 * math.exp(-0.3 * l)
        slopes = alibi_slopes()
        sl_c, sl_a, sl_b = slopes[0:6], slopes[6:12], slopes[12:16]
        kall5 = self.kall.rearrange("g (r h p) t -> g r h p t", r=4, h=3)
        vall5 = self.vall.rearrange("g (r h t) c -> g r h t c", r=4, h=3)
        TB_d = self.din("TB", [128, 4992], F32)
        TAC_d = self.din("TAC", [128, 8320], BF16)
        lam_d = self.din("lamv_%d" % l, [256], F32)
        sub_d = self.din("subln_%d" % l, [128, 1], F32)
        sink_d = self.din("sink_%d" % l, [6], F32)
        names = ["tb", "tac", "stgk", "stgv", "sb0", "sb1", "sb2", "pt0", "pt1", "pt2", "pt3", "tmp0", "tmp1", "tmp2", "sm", "dacc0", "dacc1", "onesf"]
        bl = self.newbufs(names)
        for b in self.qb:
            b.alias([x for x in self.region_bufs if x not in self.qb])
        B = dict(zip(names, bl))
        self.region_bufs = bl + self.qb
        TB = self.f32(O_TB, 4992)
        TA1 = self.bf(O_TA1, 1024)
        EA1 = self.bf(O_EA1, 1024).rearrange("p (a b) -> p a b", a=2)
        TA4 = self.bf(O_TA4, 2048).rearrange("p (a b) -> p a b", a=4)
        TA16 = self.bf(O_TA16, 2048).rearrange("p (a b) -> p a b", a=4)
        TC = self.bf(O_TC, 1152)
        EC = self.bf(O_EC, 1024).rearrange("p (a b) -> p a b", a=2)
        sbuf = [self.f32(O_SB + i * 2048, 512) for i in range(3)]
        sb_b = [B["sb0"], B["sb1"], B["sb2"]]
        ptb = [self.bf(O_PT + i * 1024, 512) for i in range(4)]
        pt_b = [B["pt0"], B["pt1"], B["pt2"], B["pt3"]]
        tmp = [self.f32(O_TMP + i * 2048, 512) for i in range(3)]
        tmp_b = [B["tmp0"], B["tmp1"], B["tmp2"]]
        sqb16 = self.bf(O_TMP + 2 * 2048, 512)
        lam_sb = self.f32(O_SM, 256)
        prod = self.f32(O_SM + 1024, 128)
        scal = self.f32(O_SM + 1536, 16)
        sink_sb = self.f32(O_SM + 1600, 6)
        esink = self.f32(O_SM + 1632, 6)
        gsub = self.f32(O_SM + 1664, 1)
        sm_b = B["sm"]
        dacc = [None, None]
        dacc_b = [B["dacc0"], B["dacc1"]]
        self.dma("sp", TB, TB_d, [], [B["tb"]])
        self.dma("sp", lam_sb, lam_d.partition_broadcast(128), [], [sm_b])
        self.dma("sp", sink_sb, sink_d.partition_broadcast(128), [], [sm_b])
        self.dma("sp", gsub, sub_d, [], [sm_b])
        self.tt(prod[:, 0:64], lam_sb[:, 0:64], lam_sb[:, 64:128], ALU.mult, [sm_b], [sm_b])
        self.tt(prod[:, 64:128], lam_sb[:, 128:192], lam_sb[:, 192:256], ALU.mult, [sm_b], [sm_b])
        S.op("dve", lambda e: e.reduce_sum(out=scal[:, 0:1], in_=prod[:, 0:64], axis=AX.X), reads=[sm_b], writes=[sm_b])
        S.op("dve", lambda e: e.reduce_sum(out=scal[:, 1:2], in_=prod[:, 64:128], axis=AX.X), reads=[sm_b], writes=[sm_b])
        self.act(scal[:, 2:4], scal[:, 0:2], AF.Exp, [sm_b], [sm_b])
        self.tt(scal[:, 4:5], scal[:, 3:4], scal[:, 2:3], ALU.subtract, [sm_b], [sm_b])
        S.op("dve", lambda e: e.tensor_scalar_add(out=scal[:, 5:6], in0=scal[:, 4:5], scalar1=-lam_init),
             reads=[sm_b], writes=[sm_b])
        S.op("dve", lambda e: e.tensor_scalar_mul(out=gsub, in0=gsub, scalar1=1.0 - lam_init), reads=[sm_b], writes=[sm_b])
        self.act(esink, sink_sb, AF.Exp, [sm_b], [sm_b])
        nlam = scal[:, 5:6]
        onesf = prod
        S.op("pool", lambda e: e.memset(onesf, 1.0), reads=[sm_b], writes=[B["onesf"]])
        onesm = self.ones

        cnt = [0]

        LAG = 3
        pending = []

        def pop_one():
            blk, (sbk, si, pi), fin = pending.pop(0)
            nk = blk["nk"]
            for pv in blk["pvs"]:
                (bank, out, lhsT, c0, c1, st_, sp_) = pv[:7]
                nkk = pv[7] if len(pv) > 7 else nk
                self.mm(out, lhsT, ptb[pi][0:nkk, c0:c1], st_, sp_, [pt_b[pi]] + blk["vreads"] + [self.const_b],
                        [self.pb[bank]], skip=True)
            if fin is not None:
                fin()

        def flush():
            while pending:
                pop_one()

        def push(blk, fin=None):
            k = cnt[0]
            cnt[0] += 1
            sbk = 4 + k % 4
            si = k % 3
            pi = k % 4
            nk, ncol = blk["nk"], blk["ncol"]
            for (c0, c1, lhsT, rhs) in blk["scores"]:
                self.mm(self.pb_ap[sbk][0:nk, c0:c1], lhsT, rhs, True, True,
                        blk["kreads"] + blk["qreads"], [self.pb[sbk]])
            self.stt(sbuf[si][0:nk, 0:ncol], blk["tab"], blk["coef"], self.pb_ap[sbk][0:nk, 0:ncol],
                     ALU.mult, ALU.add, [blk["tabb"], self.pb[sbk]], [sb_b[si]])
            self.act(ptb[pi][0:nk, 0:ncol], sbuf[si][0:nk, 0:ncol], AF.Exp, [sb_b[si]], [pt_b[pi]],
                     scale=blk["scale"])
            pending.append((blk, (sbk, si, pi), fin))
            if len(pending) > LAG:
                pop_one()

        def run_blocks(blocks, fin=None, hook=None):
            n = len(blocks)
            for i, blk in enumerate(blocks):
                push(blk, fin if i == n - 1 else None)
                if hook is not None and i == LAG - 1:
                    hook()

        stg = O_STG
        KBm = [self.bf(stg, 4096), self.bf(stg + 8192, 4096)]
        VBs = self.bf(O_TA1, 4096).rearrange("p (n c) -> p n c", n=32)
        vB_b = B["tac"]
        S.op("pool", lambda e: e.memset(KBm[0][64:128, :], 0.0), writes=[B["stgk"]])
        S.op("pool", lambda e: e.memset(KBm[1][0:64, :], 0.0), writes=[B["stgk"]])
        sc_b = 0.125
        for h in range(4):
            flush()
            gg, hl = (6 + h) // 3, (6 + h) % 3
            for m in range(2):
                rs = slice(64 * m, 64 * m + 64)
                self.dma("sp", KBm[m][rs, :].rearrange("p (r t) -> p r t", r=4),
                         kall5[gg, :, hl, rs, :].rearrange("r p t -> p r t"), [self.dk_all[gg]], [B["stgk"]])
            for r4 in range(4):
                self.dma("sp", VBs[:, r4 * 8:(r4 + 1) * 8, :], vall5[gg, r4, hl].rearrange("(n p) c -> p n c", p=128),
                         [self.dv_all[gg]], [vB_b])
            ch = 6 + h
            for qb in range(2):
                qs = slice(qb * 512, (qb + 1) * 512)
                blocks = []
                for kb in range(32):
                    for m in range(2):
                        u0 = qb * 512 - kb * 128 + 3968
                        blocks.append(dict(
                            nk=128, ncol=512,
                            scores=[(0, 512, KBm[m][:, kb * 128:(kb + 1) * 128], self.QT[:, ch, qs])],
                            kreads=[B["stgk"]], qreads=[self.qb[ch]],
                            tab=TB[:, u0:u0 + 512], tabb=B["tb"], coef=-sl_b[h] / sc_b, scale=sc_b,
                            pvs=[(m, self.pb_ap[m][:], VBs[:, kb, :], 0, 512, kb == 0, kb == 31),
                                 (2 + m, self.pb_ap[2 + m][:], onesm, 0, 512, kb == 0, kb == 31)]
                            + [(m, self.pb_ap[m][0:120, 0:DUMMY_N], self.zeros, 0, DUMMY_N, False, False, 1)] * N_DUMMY,
                            vreads=[vB_b]))
                def fin_b(ch=ch, qs=qs):
                    t0, t1, t2 = tmp
                    self.powact(t0, self.pb_ap[2][:], [self.pb[2]], [tmp_b[0]])
                    self.powact(t1, self.pb_ap[3][:], [self.pb[3]], [tmp_b[1]])
                    self.tt(t0, self.pb_ap[0][:], t0, ALU.mult, [self.pb[0]], [tmp_b[0]])
                    self.tt(t1, self.pb_ap[1][:], t1, ALU.mult, [self.pb[1]], [tmp_b[1]])
                    self.stt(t0, t1, nlam, t0, ALU.mult, ALU.add, [tmp_b[1], sm_b], [tmp_b[0]])
                    self.act(sqb16, t0, AF.Square, [tmp_b[0]], [tmp_b[2]])
                    sbk = 4 + cnt[0] % 4
                    self.mm(self.pb_ap[sbk][:], onesm, sqb16, True, True, [tmp_b[2], self.const_b], [self.pb[sbk]])
                    self.powact(t1, self.pb_ap[sbk][:], [self.pb[sbk], self.const_b], [tmp_b[1]],
                                power=-0.5, scale_in=1.0 / 128, bias=self.eps)
                    self.stt(self.hT[:, ch, qs], t0, gsub, t1, ALU.mult, ALU.mult, [tmp_b[0], tmp_b[1], sm_b], [self.hb[ch]])

                run_blocks(blocks, fin_b)
        flush()

        self.ph_xchg(l)
        kpad, vpad = self.kpad, self.vpad
        self.dma("sp", self.bf(O_TA1, 8320), TAC_d, [], [B["tac"]])
        stg_off = [stg, O_TB]
        stgA_b = [(B["stgk"], B["stgv"]), (Buf("stgk1").alias([B["tb"]]), Buf("stgv1").alias([B["tb"]]))]
        sc_a = 128 ** -0.5

        def a_views(sl):
            o = stg_off[sl]
            return (self.bf(o, 2560),
                    self.bf(o + 5120, 640).rearrange("p (n c) -> p n c", n=5),
                    self.bf(o + 6400, 1024).rearrange("p (r n c) -> p r n c", r=4, n=2),
                    self.bf(o + 8448, 2048).rearrange("p (r c) -> p r c", r=16),
                    self.bf(o + 12544, 2048).rearrange("p (r c) -> p r c", r=16))

        def a_loads(h, qb, sl):
            KAw, VA1s, VA4s, VA16a, VA16b = a_views(sl)
            kb_, vb_ = stgA_b[sl]
            q0 = qb * 512
            di = h // 3
            self.dma("sp", KAw, kpad[h][:, q0:q0 + 2560], self.dkpad[di], [kb_])
            self.dma("sp", VA1s, vpad[h][960 + q0:960 + q0 + 640].rearrange("(n p) c -> p n c", p=128),
                     self.dvpad[di], [vb_])
            for r in range(4):
                st4 = 768 + r + q0
                self.dma("sp", VA4s[:, r, :, :],
                         vpad[h][st4:st4 + 4 * 255 + 1:4].rearrange("(n p) c -> p n c", p=128),
                         self.dvpad[di], [vb_])
            self.dma("sp", VA16a, vpad[h][q0:q0 + 2048].rearrange("(p r) c -> p r c", r=16), self.dvpad[di], [vb_])
            self.dma("sp", VA16b[0:32], vpad[h][q0 + 2048:q0 + 2560].rearrange("(p r) c -> p r c", r=16),
                     self.dvpad[di], [vb_])

        def a_compute(h, qb, sl, ab, hook=None):
            bn, bd = 2 * ab, 2 * ab + 1
            KAw, VA1s, VA4s, VA16a, VA16b = a_views(sl)
            kb_, vb_ = stgA_b[sl]
            q0 = qb * 512
            coef = -sl_a[h] / sc_a
            qcol = self.QT[:, h, q0:q0 + 512]
            common = dict(kreads=[kb_], qreads=[self.qb[h]], tabb=B["tac"], coef=coef, scale=sc_a, vreads=[vb_])
            blocks = []
            for n in range(5):
                if qb == 0 and n == 0:
                    tab = EA1[:, 0, :]
                elif qb == 1 and n == 4:
                    tab = EA1[:, 1, :]
                else:
                    u0 = 512 - 128 * n
                    tab = TA1[:, u0:u0 + 512]
                kc = 960 + 128 * n
                blocks.append(dict(
                    nk=128, ncol=512, scores=[(0, 512, KAw[:, kc:kc + 128], qcol)], tab=tab,
                    pvs=[(bn, self.pb_ap[bn][:], VA1s[:, n, :], 0, 512, n == 0, False),
                         (bd, self.pb_ap[bd][:], onesm, 0, 512, n == 0, False)], **common))
            for n in range(2):
                scores = []
                pvs = []
                for r in range(4):
                    s_r = 768 + 512 * n + r
                    scores.append((r * 128, (r + 1) * 128, KAw[:, s_r:s_r + 4 * 127 + 1:4], self.QT[:, h, q0 + r:q0 + 512:4]))
                    pvs.append((bn, self.pb_ap[bn][:, r:512:4], VA4s[:, r, n, :], r * 128, (r + 1) * 128, False, False))
                    pvs.append((bd, self.pb_ap[bd][:, r:512:4], onesm, r * 128, (r + 1) * 128, False, False))
                blocks.append(dict(nk=128, ncol=512, scores=scores, tab=TA4[:, 2 * qb + n, :], pvs=pvs, **common))
            for n in range(2):
                nk = 128 if n == 0 else 32
                scores = []
                pvs = []
                for r in range(16):
                    s_r = r if n == 0 else 2048 + r
                    scores.append((r * 32, (r + 1) * 32, KAw[:, s_r:s_r + 16 * (nk - 1) + 1:16], self.QT[:, h, q0 + r:q0 + 512:16]))
                    vt = VA16a[:, r, :] if n == 0 else VA16b[0:32, r, :]
                    last = (n == 1 and r == 15)
                    pvs.append((bn, self.pb_ap[bn][:, r:512:16], vt, r * 32, (r + 1) * 32, False, last))
                    pvs.append((bd, self.pb_ap[bd][:, r:512:16], onesm[0:nk, :], r * 32, (r + 1) * 32, False, last))
                blocks.append(dict(nk=nk, ncol=512, scores=scores, tab=TA16[0:nk, 2 * qb + n, :], pvs=pvs, **common))
            def fin_a(h=h, q0=q0, ab=ab, bn=bn, bd=bd):
                t0 = tmp[ab]
                self.powact(t0, self.pb_ap[bd][:], [self.pb[bd]], [tmp_b[ab]])
                self.tt(self.hT[:, h, q0:q0 + 512], self.pb_ap[bn][:], t0, ALU.mult, [self.pb[bn], tmp_b[ab]], [self.hb[h]])

            run_blocks(blocks, fin_a, hook)

        unitsA = [(h, qb) for h in range(6) for qb in range(2)]
        a_loads(unitsA[0][0], unitsA[0][1], 0)
        for ui, (h, qb) in enumerate(unitsA):
            hk = None
            if ui + 1 < len(unitsA):
                hk = (lambda u=ui + 1: a_loads(unitsA[u][0], unitsA[u][1], u % 2))
            a_compute(h, qb, ui % 2, ui % 2, hk)
        flush()

        KCw = self.bf(stg, 1280)
        VCs = self.bf(stg + 2560, 1280).rearrange("p (n c) -> p n c", n=10)
        sc_c = 128 ** -0.5
        for g in range(2):
            flush()
            self.dma("sp", KCw, kpad[6 + g][:, 896:2176], self.dkpad[2], [B["stgk"]])
            self.dma("sp", VCs, vpad[6 + g][896:2176].rearrange("(n p) c -> p n c", p=128), self.dvpad[2], [B["stgv"]])
            for hh in range(3):
                hq = g * 3 + hh
                ch = 10 + hq
                for qb in range(2):
                    q0 = qb * 512
                    ab = (hq * 2 + qb) % 2
                    bn, bd = 2 * ab, 2 * ab + 1
                    blocks = []
                    for n in range(6):
                        if qb == 0 and n == 0:
                            tab = EC[:, 0, :]
                        elif qb == 1 and n == 5:
                            tab = EC[:, 1, :]
                        else:
                            u0 = 640 - 128 * n
                            tab = TC[:, u0:u0 + 512]
                        kt = 4 * qb + n
                        blocks.append(dict(
                            nk=128, ncol=512,
                            scores=[(0, 512, KCw[:, kt * 128:(kt + 1) * 128], self.QT[:, ch, q0:q0 + 512])],
                            kreads=[B["stgk"]], qreads=[self.qb[ch]], tab=tab, tabb=B["tac"],
                            coef=-sl_c[hq] / sc_c, scale=sc_c,
                            pvs=[(bn, self.pb_ap[bn][:], VCs[:, kt, :], 0, 512, n == 0, n == 5),
                                 (bd, self.pb_ap[bd][:], onesm, 0, 512, n == 0, n == 5)],
                            vreads=[B["stgv"]]))
                    def fin_c(ch=ch, q0=q0, ab=ab, bn=bn, bd=bd, hq=hq):
                        t0 = tmp[ab]
                        self.powact(t0, self.pb_ap[bd][:], [self.pb[bd], sm_b], [tmp_b[ab]], bias=esink[:, hq:hq + 1])
                        self.tt(self.hT[:, ch, q0:q0 + 512], self.pb_ap[bn][:], t0, ALU.mult,
                                [self.pb[bn], tmp_b[ab]], [self.hb[ch]])

                    run_blocks(blocks, fin_c)
        flush()


def _lay_gu(w):
    F = w.shape[1]
    return np.ascontiguousarray(w.reshape(16, 128, F // 128, 128).transpose(2, 1, 0, 3)).reshape(F // 128, 128, 2048)


def _lay_wout(w):
    return np.ascontiguousarray(w.reshape(16, 128, 4, 512).transpose(2, 1, 0, 3)).reshape(4, 128, 16 * 512)


def _gains(inputs):
    g = np.zeros((7, 2048), np.float32)
    for l in range(DEPTH):
        g[l * 3 + 0] = inputs["ffn1_norm"][l]
        g[l * 3 + 1] = inputs["mix_norm"][l]
        g[l * 3 + 2] = inputs["ffn2_norm"][l]
    g[6] = inputs["final_norm"]
    return np.ascontiguousarray(g.reshape(7, 16, 128).transpose(2, 0, 1)).reshape(128, 7 * 16)


_PROG_CACHE = {}


def get_prog(phases):
    key = repr(phases)
    if key not in _PROG_CACHE:
        _PROG_CACHE[key] = Prog(phases)
    return _PROG_CACHE[key]


def _tables(cc):
    i = np.arange(128, dtype=np.int64)[:, None]
    u = np.arange(4992, dtype=np.int64)[None, :]
    TB = np.abs(cc * 1024 + u - 3968 - i).astype(np.float32)

    def band(delta, radius, mult, ok):
        d = np.abs(delta)
        return np.where((d <= radius) & ok, (mult * d).astype(np.float32), np.float32(BIG))

    j = np.arange(512, dtype=np.int64)[None, :]
    u1 = np.arange(1024, dtype=np.int64)[None, :]
    TA1 = band(u1 - 448 - i, 64, 1, True)
    k = -64 + i
    EA1_0 = band(j + 64 - i, 64, 1, (cc * 1024 + k >= 0))
    k = 960 + i
    EA1_1 = band(j - 448 - i, 64, 1, (cc * 1024 + k < SEQ))
    TA4 = []
    for qb in range(2):
        for n in range(2):
            jp = (np.arange(512, dtype=np.int64) % 128)[None, :]
            kg = cc * 256 + qb * 128 - 64 + 128 * n + i
            TA4.append(band(jp + 64 - 128 * n - i, 64, 4, (kg >= 0) & (kg < 1024)))
    TA16 = []
    for qb in range(2):
        for n in range(2):
            jp = (np.arange(512, dtype=np.int64) % 32)[None, :]
            if n == 0:
                delta = jp + 64 - i
                kp = qb * 32 - 64 + i
                ok = np.ones_like(i, dtype=bool)
            else:
                delta = jp - 64 - i
                kp = qb * 32 + 64 + i
                ok = i < 32
            kg = cc * 64 + kp
            TA16.append(band(delta, 64, 16, ok & (kg >= 0) & (kg < 256)))
    uc = np.arange(1152, dtype=np.int64)[None, :]
    TC = band(uc - 512 - i, 128, 1, True)
    k = -128 + i
    EC_0 = band(j + 128 - i, 128, 1, (cc * 1024 + k >= 0))
    k = 1024 + i
    EC_1 = band(j - 512 - i, 128, 1, (cc * 1024 + k < SEQ))
    TAC = np.concatenate([TA1, EA1_0, EA1_1] + TA4 + TA16 + [TC, EC_0, EC_1], axis=1)
    assert TAC.shape == (128, 8320)
    return TB, np.ascontiguousarray(TAC.astype(NPBF))


def _attn_small(inputs, l):
    lamv = np.concatenate([inputs["diff_lambda_q1"][l], inputs["diff_lambda_k1"][l],
                           inputs["diff_lambda_q2"][l], inputs["diff_lambda_k2"][l]]).astype(np.float32)
    return {"lamv_%d" % l: lamv,
            "subln_%d" % l: np.ascontiguousarray(inputs["diff_subln"][l].reshape(128, 1)).astype(np.float32),
            "sink_%d" % l: np.ascontiguousarray(inputs["swa_sink"][l]).astype(np.float32)}


def _run(phases, shared, percore):
    prog = get_prog(phases)
    in_maps = []
    for c in range(NCORES):
        m = dict(shared)
        m.update(percore[c])
        in_maps.append(m)
    res = run_bass_kernel_spmd(prog.nc, in_maps, core_ids=list(range(NCORES)))
    return res.results


PH_FUSED = [("load_x",),
            ("ffn", 0, 1), ("win", 0), ("attn", 0), ("wout", 0), ("ffn", 0, 2),
            ("ffn", 1, 1), ("win", 1), ("attn", 1), ("wout", 1), ("ffn", 1, 2),
            ("final",)]


def kernel(x, ffn1_norm, ffn1_w_gate, ffn1_w_up, ffn1_w_down, mix_norm, w_in, w_out,
           diff_lambda_q1, diff_lambda_k1, diff_lambda_q2, diff_lambda_k2, diff_subln,
           swa_sink, ffn2_norm, ffn2_w_gate, ffn2_w_up, ffn2_w_down, final_norm):
    inputs = dict(x=x, ffn1_norm=ffn1_norm, ffn1_w_gate=ffn1_w_gate, ffn1_w_up=ffn1_w_up,
                  ffn1_w_down=ffn1_w_down, mix_norm=mix_norm, w_in=w_in, w_out=w_out,
                  diff_lambda_q1=diff_lambda_q1, diff_lambda_k1=diff_lambda_k1,
                  diff_lambda_q2=diff_lambda_q2, diff_lambda_k2=diff_lambda_k2, diff_subln=diff_subln,
                  swa_sink=swa_sink, ffn2_norm=ffn2_norm, ffn2_w_gate=ffn2_w_gate, ffn2_w_up=ffn2_w_up,
                  ffn2_w_down=ffn2_w_down, final_norm=final_norm)
    inputs = {k: np.asarray(v) for k, v in inputs.items()}
    x = inputs["x"]
    shared = {"gains": _gains(inputs)}
    ffn_w = {1: (inputs["ffn1_w_gate"], inputs["ffn1_w_up"], inputs["ffn1_w_down"]),
             2: (inputs["ffn2_w_gate"], inputs["ffn2_w_up"], inputs["ffn2_w_down"])}
    for l in range(DEPTH):
        for which in (1, 2):
            wg, wu, wd = ffn_w[which]
            tag = "%d_%d" % (l, which)
            shared["wg_" + tag] = _lay_gu(wg[l])
            shared["wu_" + tag] = _lay_gu(wu[l])
            shared["wd_" + tag] = np.ascontiguousarray(wd[l]).reshape(NF, 128, 2048)
        shared["win_%d" % l] = _lay_gu(inputs["w_in"][l])
        shared["wout_%d" % l] = _lay_wout(inputs["w_out"][l])
        shared.update(_attn_small(inputs, l))
    tabs = [_tables(cc) for cc in range(4)]
    pc = []
    for c in range(NCORES):
        b, cc = c // 4, c % 4
        pc.append({"xT_in": np.ascontiguousarray(x[b, cc * 1024:(cc + 1) * 1024, :].T).reshape(16, 128, 1024),
                   "TB": tabs[cc][0], "TAC": tabs[cc][1]})
    res = _run(PH_FUSED, shared, pc)
    out = np.empty((BATCH, SEQ, D_MODEL), np.float32)
    for c in range(NCORES):
        b, cc = c // 4, c % 4
        out[b, cc * 1024:(cc + 1) * 1024, :] = res[c]["outT"].reshape(2048, 1024).T
    return out
```

```python
import contextlib
import math
import numpy as np
import ml_dtypes
import concourse.bass as bass
import concourse.mybir as mybir
from concourse.bass_utils import run_bass_kernel_spmd

F32 = mybir.dt.float32
BF16 = mybir.dt.bfloat16
ALU = mybir.AluOpType
AF = mybir.ActivationFunctionType
AX = mybir.AxisListType
NPBF = ml_dtypes.bfloat16

D_MODEL = 2048
BATCH = 2
SEQ = 4096
DEPTH = 2
D_FF = 5504
NF = D_FF // 128
NC_ = 16
TL = 1024
NCORES = 8
RMS_EPS = 1e-6
BIG = 30000.0
GSZ = 5
N_DUMMY = 0
DUMMY_N = 128

ENGS = ("pe", "act", "dve", "pool", "sp")


class Buf:
    __slots__ = ("w", "wd", "r", "rd", "name")

    def __init__(self, name=""):
        self.w = {}
        self.wd = []
        self.r = {}
        self.rd = []
        self.name = name

    def alias(self, olds):
        for b in olds:
            for src in (b.w, b.r):
                for e, o in src.items():
                    cur = self.r.get(e)
                    if cur is None or cur.idx < o.idx:
                        self.r[e] = o
            self.rd.extend(b.wd)
            self.rd.extend(b.rd)
        return self


class Op:
    __slots__ = ("eng", "fn", "deps", "needed", "dma", "sem", "semval", "prev_semval", "idx", "inc")

    def __init__(self, eng, fn, dma, idx):
        self.eng = eng
        self.fn = fn
        self.dma = dma
        self.idx = idx
        self.deps = []
        self.needed = False
        self.sem = None
        self.semval = 0
        self.prev_semval = 0


class Sched:
    def __init__(self, n_dma_sems=8):
        self.ops = {e: [] for e in ENGS}
        self.n_dma_sems = n_dma_sems
        self.nops = 0

    def op(self, eng, fn, reads=(), writes=(), dma=False, inc=16):
        o = Op(eng, fn, dma, self.nops)
        o.inc = inc
        deps = {}

        def add(d):
            if (not d.dma) and (not dma) and d.eng == "pe" and eng == "pe":
                return
            deps[id(d)] = d

        for b in reads:
            for d in b.w.values():
                add(d)
            for d in b.wd:
                add(d)
        for b in writes:
            for d in b.w.values():
                add(d)
            for d in b.wd:
                add(d)
            for d in b.r.values():
                add(d)
            for d in b.rd:
                add(d)
        for b in reads:
            if dma:
                b.rd.append(o)
            else:
                b.r[eng] = o
        for b in writes:
            b.r = {}
            b.rd = []
            if dma:
                b.w = {}
                b.wd = [o]
            else:
                b.w = {eng: o}
                b.wd = []
        o.deps = list(deps.values())
        for d in o.deps:
            d.needed = True
        self.ops[eng].append(o)
        self.nops += 1
        return o

    def emit(self, nc, stack):
        esem = {e: stack.enter_context(nc.semaphore("S_" + e)) for e in ENGS}
        dsem = {}
        for e in ENGS:
            for inc in sorted(set(o.inc for o in self.ops[e] if o.dma)):
                dsem[(e, inc)] = [stack.enter_context(nc.semaphore("D_%s%d_%d" % (e, inc, i)))
                                  for i in range(self.n_dma_sems if inc == 16 else 2)]
        for e in ENGS:
            c = 0
            kk = {}
            cn = {}
            for o in self.ops[e]:
                if o.dma:
                    pool = dsem[(e, o.inc)]
                    k = kk.get(o.inc, 0)
                    cnts = cn.setdefault(o.inc, [0] * len(pool))
                    i = k % len(pool)
                    o.sem = pool[i]
                    o.prev_semval = cnts[i]
                    cnts[i] += o.inc
                    o.semval = cnts[i]
                    kk[o.inc] = k + 1
                elif o.needed:
                    c += 1
                    o.sem = esem[e]
                    o.semval = c
        block = stack.enter_context(nc.Block())
        handles = {"pe": block.tensor, "act": block.scalar, "dve": block.vector,
                   "pool": block.gpsimd, "sp": block.sync}
        stats = {}
        for e in ENGS:
            ops = self.ops[e]
            if not ops:
                continue
            nwait = [0]

            def body(eng, ops=ops, nwait=nwait):
                known = {}
                for o in ops:
                    waits = {}
                    for d in o.deps:
                        key = id(d.sem)
                        if known.get(key, 0) >= d.semval:
                            continue
                        if key not in waits or waits[key][1] < d.semval:
                            waits[key] = (d.sem, d.semval)
                    if o.dma and o.prev_semval > 0:
                        key = id(o.sem)
                        if known.get(key, 0) < o.prev_semval:
                            if key not in waits or waits[key][1] < o.prev_semval:
                                waits[key] = (o.sem, o.prev_semval)
                    for key, (s, v) in waits.items():
                        eng.wait_ge(s, v)
                        known[key] = v
                        nwait[0] += 1
                    if o.fn is None:
                        continue
                    inst = o.fn(eng)
                    if o.dma:
                        inst.then_inc(o.sem, o.inc)
                    elif o.needed:
                        inst.then_inc(o.sem, 1)

            handles[e](body)
            stats[e] = (len(ops), nwait[0])
        return stats


O_XT = 0
O_RSTD = 65536
O_GAIN = O_RSTD + 4096
O_CONST = O_GAIN + 512
RB = O_CONST + 512
O_H = RB
O_Q = RB + 32768
O_Z = RB + 65536
O_WGU = O_Q
O_SG = O_Q + 24576
O_SQ_F = O_Q + 28672
O_WD = O_Z
O_ACT = O_Z + 2 * GSZ * 4096
NWR = 6
NWS = 3
O_WIN = O_Z
O_WST = O_Z + NWR * 4096
O_KST = O_WST + NWS * 8192
O_VST = O_KST + 4096
O_SQ_W = O_VST + 12288
O_TB = O_Z
O_TA1 = O_TB + 19968
O_EA1 = O_TA1 + 2048
O_TA4 = O_EA1 + 2048
O_TA16 = O_TA4 + 4096
O_TC = O_TA16 + 4096
O_EC = O_TC + 2304
O_STG = O_EC + 2048
O_SB = O_STG + 16640
O_PT = O_SB + 6144
O_TMP = O_PT + 4096
O_SM = O_TMP + 6144
O_DACC = O_SM + 2048
ATT_END = O_DACC
O_WO = O_Z
ARENA_BYTES = max(ATT_END, O_ACT + 2 * GSZ * 2048, O_WO + 32768)

WIN_ROLE = (["q"] * 6 + ["k"] * 6 + ["v"] * 6 + ["q"] * 4 + ["k"] * 4 + ["v"] * 4
            + ["q"] * 6 + ["k"] * 2 + ["v"] * 2)


def alibi_slopes():
    return [2.0 ** (-8.0 * (i + 1) / 16) for i in range(16)]


class Prog:
    def __init__(self, phases):
        self.phases = phases
        self.nc = nc = bass.Bass("TRN2", target_bir_lowering=False)
        self.S = Sched()
        self.dram = {}
        self.stack = contextlib.ExitStack()
        st = self.stack
        self.arena = st.enter_context(nc.sbuf_tensor("arena", [128, ARENA_BYTES // 2], BF16))
        self.pb_ap = [st.enter_context(nc.psum_tensor("pb%d" % i, [128, 512], F32)) for i in range(8)]
        self.pb = [Buf("pb%d" % i) for i in range(8)]
        self.xT = self.f32(O_XT, 16 * 1024).rearrange("p (c t) -> p c t", c=16)
        self.xb = [[Buf("x%d_%d" % (c, h)) for h in range(2)] for c in range(16)]
        self.rstd = self.f32(O_RSTD, 1024)
        self.rstd_b = Buf("rstd")
        self.gain = self.f32(O_GAIN, 7 * 16).rearrange("p (v c) -> p v c", v=7)
        self.gain_b = Buf("gain")
        self.ones = self.bf(O_CONST, 128)
        self.eps = self.f32(O_CONST + 256, 1)
        self.zeros = self.arena[0:1, (O_CONST + 264) // 2:(O_CONST + 264) // 2 + 120]
        self.const_b = Buf("const")
        self.hT = self.bf(O_H, 16 * 1024).rearrange("p (c t) -> p c t", c=16)
        self.hb = [Buf("h%d" % c) for c in range(16)]
        self.QT = self.bf(O_Q, 16 * 1024).rearrange("p (c t) -> p c t", c=16)
        self.qb = [Buf("q%d" % c) for c in range(16)]
        self.region_bufs = []
        self.out_bufs = []
        self.build()

    def bf(self, off, n):
        return self.arena[:, off // 2: off // 2 + n]

    def f32(self, off, n):
        return self.arena[:, off // 2: off // 2 + 2 * n].bitcast(F32)

    def din(self, name, shape, dt):
        if name not in self.dram:
            self.dram[name] = self.nc.dram_tensor(name, list(shape), dt, kind="ExternalInput").ap()
        return self.dram[name]

    def dout(self, name, shape, dt):
        if name not in self.dram:
            self.dram[name] = self.nc.dram_tensor(name, list(shape), dt, kind="ExternalOutput").ap()
        return self.dram[name]

    def _rank_of(self, e, k):
        if not hasattr(self, "_rk"):
            self._rk = {}
            cc = e.partition_id() % 4
            for kk in (-1, 0, 1):
                self._rk[kk] = e.snap((cc + (kk + 4)) % 4)
        return self._rk[k]

    def dint(self, name, shape, dt):
        if name not in self.dram:
            self.dram[name] = self.nc.dram_tensor(name, list(shape), dt).ap()
        return self.dram[name]

    def newbufs(self, names):
        bs = [Buf(n).alias(self.region_bufs) for n in names]
        return bs

    def mm(self, out, lhsT, rhs, start, stop, reads, writes, skip=False):
        kw = {"skip_group_check": True} if skip else {}
        self.S.op("pe", lambda e: e.matmul(out, lhsT=lhsT, rhs=rhs, start=start, stop=stop, **kw),
                  reads=reads, writes=writes)

    def dma(self, eng, out, in_, reads, writes):
        return self.S.op(eng, lambda e: e.dma_start(out=out, in_=in_), reads=reads, writes=writes, dma=True)

    def act(self, out, in_, func, reads, writes, scale=None, bias=None):
        kw = {}
        if scale is not None:
            kw["scale"] = scale
        if bias is not None:
            kw["bias"] = bias
        self.S.op("act", lambda e: e.activation(out=out, in_=in_, func=func, **kw), reads=reads, writes=writes)

    def stt(self, out, in0, scalar, in1, op0, op1, reads, writes, eng="dve"):
        self.S.op(eng, lambda e: e.scalar_tensor_tensor(out=out, in0=in0, scalar=scalar, in1=in1, op0=op0, op1=op1),
                  reads=reads, writes=writes)

    def tt(self, out, in0, in1, op, reads, writes, eng="dve"):
        self.S.op(eng, lambda e: e.tensor_tensor(out=out, in0=in0, in1=in1, op=op), reads=reads, writes=writes)

    def powact(self, out, in_, reads, writes, power=-1.0, scale_in=None, bias=None):
        self.act(out, in_, AF.Ln, reads, writes, scale=scale_in, bias=bias)
        self.act(out, out, AF.Exp, [], writes, scale=power)

    def recip(self, out, in_, reads, writes):
        self.S.op("dve", lambda e: e.reciprocal(out=out, in_=in_), reads=reads, writes=writes)

    def build(self):
        S = self.S
        S.op("pool", lambda e: e.memset(self.ones, 1.0), writes=[self.const_b])
        S.op("pool", lambda e: e.memset(self.eps, RMS_EPS), writes=[self.const_b])
        S.op("pool", lambda e: e.memset(self.zeros, 0.0), writes=[self.const_b])
        g_in = self.din("gains", [128, 7 * 16], F32)
        self.dma("sp", self.f32(O_GAIN, 7 * 16), g_in, [], [self.gain_b])
        for ph in self.phases:
            kind = ph[0]
            if kind == "load_x":
                self.ph_load_x()
            elif kind == "load_state":
                self.ph_load_state()
            elif kind == "ffn":
                self.ph_ffn(ph[1], ph[2])
            elif kind == "win":
                self.ph_win(ph[1])
            elif kind == "store_state":
                self.ph_store_state()
            elif kind == "attn":
                self.ph_attn(ph[1])
            elif kind == "xchg":
                self.ph_xchg(ph[1])
            elif kind == "wout":
                self.ph_wout(ph[1])
            elif kind == "final":
                self.ph_final()
            elif kind == "dump_h":
                h_out = self.dout("hT_out", [16, 128, 1024], BF16)
                for c in range(16):
                    ob = Buf("ho")
                    self.dma("sp", h_out[c], self.hT[:, c, :], [self.hb[c]], [ob])
                    self.out_bufs.append(ob)
            else:
                raise ValueError(kind)
        S.op("sp", None, reads=self.out_bufs)
        self.stats = S.emit(self.nc, self.stack)

    def ph_load_x(self):
        x_in = self.din("xT_in", [16, 128, 1024], F32)
        for c in range(16):
            self.dma("sp", self.xT[:, c, :], x_in[c], [], [self.xb[c][0], self.xb[c][1]])

    def ph_load_state(self):
        self.ph_load_x()
        q_in = self.din("QT_in", [16, 128, 1024], BF16)
        for b in self.qb:
            b.alias(self.region_bufs)
        self.region_bufs = list(self.qb)
        for c in range(16):
            self.dma("sp", self.QT[:, c, :], q_in[c], [], [self.qb[c]])

    def ph_store_state(self):
        x_out = self.dout("xT_out", [16, 128, 1024], F32)
        q_out = self.dout("QT_out", [16, 128, 1024], BF16)
        for c in range(16):
            ob = Buf("xo")
            self.dma("sp", x_out[c], self.xT[:, c, :], [self.xb[c][0], self.xb[c][1]], [ob])
            self.out_bufs.append(ob)
        for c in range(16):
            ob = Buf("qo")
            self.dma("sp", q_out[c], self.QT[:, c, :], [self.qb[c]], [ob])
            self.out_bufs.append(ob)

    def norm(self, vidx, sq_off, sq_bufs, final_out=None):
        sq = [self.bf(sq_off + i * 2048, 1024) for i in range(2)]
        P6, P7 = self.pb[6], self.pb[7]
        for c in range(16):
            i = c % 2
            self.act(sq[i], self.xT[:, c, :], AF.Square, [self.xb[c][0], self.xb[c][1]], [sq_bufs[i]])
            for h in range(2):
                self.mm(self.pb_ap[6 + h][:], self.ones, sq[i][:, h * 512:(h + 1) * 512], c == 0, c == 15,
                        [sq_bufs[i], self.const_b], [self.pb[6 + h]])
        for h in range(2):
            r = self.rstd[:, h * 512:(h + 1) * 512]
            self.powact(r, self.pb_ap[6 + h][:], [self.pb[6 + h], self.const_b], [self.rstd_b],
                        power=-0.5, scale_in=1.0 / D_MODEL, bias=self.eps)
        for c in range(16):
            if final_out is None:
                self.stt(self.hT[:, c, :], self.xT[:, c, :], self.gain[:, vidx, c:c + 1], self.rstd,
                         ALU.mult, ALU.mult, [self.xb[c][0], self.xb[c][1], self.gain_b, self.rstd_b], [self.hb[c]])
            else:
                self.stt(self.xT[:, c, :], self.xT[:, c, :], self.gain[:, vidx, c:c + 1], self.rstd,
                         ALU.mult, ALU.mult, [self.gain_b, self.rstd_b], [self.xb[c][0], self.xb[c][1]])
                ob = Buf("fo")
                self.dma("sp", final_out[c], self.xT[:, c, :], [self.xb[c][0], self.xb[c][1]], [ob])
                self.out_bufs.append(ob)

    def ph_final(self):
        out = self.dout("outT", [16, 128, 1024], F32)
        sqb = self.newbufs(["sq0", "sq1"])
        self.region_bufs = sqb
        self.norm(6, O_SQ_F, sqb, final_out=out)

    def ph_ffn(self, l, which):
        S = self.S
        tag = "%d_%d" % (l, which)
        wg_d = self.din("wg_" + tag, [NF, 128, 2048], F32)
        wu_d = self.din("wu_" + tag, [NF, 128, 2048], F32)
        wd_d = self.din("wd_" + tag, [NF, 128, 2048], F32)
        names = (["wgu%d" % i for i in range(3)] + ["wd%d" % i for i in range(2)] + ["sg0", "sg1", "sq0", "sq1"]
                 + ["act%d_%d_%d" % (a, j, h) for a in range(2) for j in range(GSZ) for h in range(2)])
        bl = self.newbufs(names)
        wgu_b = bl[0:3]
        wd_b = bl[3:5]
        sg_b = bl[5:7]
        sq_b = bl[7:9]
        act_b = [[[bl[9 + (a * GSZ + j) * 2 + h] for h in range(2)] for j in range(GSZ)] for a in range(2)]
        self.region_bufs = bl
        wg_sb = [self.bf(O_WGU + s * 8192, 2048).rearrange("p (c f) -> p c f", c=16) for s in range(3)]
        wu_sb = [self.bf(O_WGU + s * 8192 + 4096, 2048).rearrange("p (c f) -> p c f", c=16) for s in range(3)]
        wd_sb = [self.bf(O_WD + s * GSZ * 4096, GSZ * 2048).rearrange("p (g d) -> p g d", g=GSZ) for s in range(2)]
        act_sb = [self.bf(O_ACT + s * GSZ * 2048, GSZ * 1024).rearrange("p (g t) -> p g t", g=GSZ) for s in range(2)]
        sg_sb = [self.f32(O_SG + s * 2048, 512) for s in range(2)]
        vidx = l * 3 + (0 if which == 1 else 2)
        self.norm(vidx, O_SQ_F, sq_b)

        groups = []
        j = 0
        while j < NF:
            groups.append(list(range(j, min(j + GSZ, NF))))
            j += GSZ
        ng = len(groups)

        def issue_gu(jj):
            s = jj % 3
            self.dma("pool", wg_sb[s].rearrange("p c f -> p (c f)"), wg_d[jj], [], [wgu_b[s]])
            self.dma("pool", wu_sb[s].rearrange("p c f -> p (c f)"), wu_d[jj], [], [wgu_b[s]])

        def issue_wd(g):
            js = groups[g]
            s = g % 2
            for gi, jj in enumerate(js):
                self.dma("pool", wd_sb[s][:, gi, :], wd_d[jj], [], [wd_b[s]])

        dma_plan = []
        for g in range(ng):
            js = groups[g]
            for jj in js[:3]:
                dma_plan.append(("gu", jj))
            dma_plan.append(("wd", g))
            for jj in js[3:]:
                dma_plan.append(("gu", jj))
        self._dma_plan = dma_plan
        self._dma_pos = 0

        plan_pos = {item: i for i, item in enumerate(dma_plan)}

        def pump(until_gu=None, until_wd=None):
            target = ("gu", until_gu) if until_gu is not None else ("wd", until_wd)
            end = plan_pos[target] if until_gu is not None or until_wd is not None else len(self._dma_plan) - 1
            while self._dma_pos <= end:
                k, v = self._dma_plan[self._dma_pos]
                if k == "gu":
                    issue_gu(v)
                else:
                    issue_wd(v)
                self._dma_pos += 1

        unit = [0]

        def emit_gu(g):
            a = g % 2
            for gi, jj in enumerate(groups[g]):
                pump(until_gu=min(jj + 2, NF - 1))
                s = jj % 3
                for h in range(2):
                    k = unit[0] % 2
                    unit[0] += 1
                    pg, pu = 2 * k, 2 * k + 1
                    for c in range(16):
                        self.mm(self.pb_ap[pg][:], wg_sb[s][:, c, :], self.hT[:, c, h * 512:(h + 1) * 512],
                                c == 0, c == 15, [wgu_b[s], self.hb[c]], [self.pb[pg]])
                    for c in range(16):
                        self.mm(self.pb_ap[pu][:], wu_sb[s][:, c, :], self.hT[:, c, h * 512:(h + 1) * 512],
                                c == 0, c == 15, [wgu_b[s], self.hb[c]], [self.pb[pu]])
                    self.act(sg_sb[k], self.pb_ap[pg][:], AF.Silu, [self.pb[pg]], [sg_b[k]])
                    self.tt(act_sb[a][:, gi, h * 512:(h + 1) * 512], sg_sb[k], self.pb_ap[pu][:], ALU.mult,
                            [sg_b[k], self.pb[pu]], [act_b[a][gi][h]])

        dunit = [0]

        def emit_d(g):
            a = g % 2
            s = g % 2
            js = groups[g]
            n = len(js)
            for c in range(16):
                for h in range(2):
                    bk = 4 + dunit[0] % 4
                    dunit[0] += 1
                    for gi in range(n):
                        self.mm(self.pb_ap[bk][:], wd_sb[s][:, gi, c * 128:(c + 1) * 128],
                                act_sb[a][:, gi, h * 512:(h + 1) * 512], gi == 0, gi == n - 1,
                                [wd_b[s], act_b[a][gi][h]], [self.pb[bk]])
                    xs = self.xT[:, c, h * 512:(h + 1) * 512]
                    self.stt(xs, self.pb_ap[bk][:], 0.5, xs, ALU.mult, ALU.add, [self.pb[bk]], [self.xb[c][h]])

        emit_gu(0)
        for g in range(ng):
            if g + 1 < ng:
                emit_gu(g + 1)
            pump(until_wd=g)
            emit_d(g)

    def ph_win(self, l):
        S = self.S
        w_d = self.din("win_%d" % l, [40, 128, 2048], F32)
        kt_out = self.dint("KT_own_%d" % l, [12, 128, 1024], BF16)
        v_out = self.dint("V_own_%d" % l, [12, 1024, 128], BF16)
        k_all = self.dint("KT_all_%d" % l, [4, 4 * 384, 1024], BF16)
        v_all = self.dint("V_all_%d" % l, [4, 4 * 3072, 128], BF16)
        self.dk_own = [Buf("dk%d" % i) for i in range(12)]
        self.dv_own = [Buf("dv%d" % i) for i in range(12)]
        self.dk_all = [Buf("dkall%d" % g) for g in range(4)]
        self.dv_all = [Buf("dvall%d" % g) for g in range(4)]
        cc_groups = [[0, 1, 2, 3], [4, 5, 6, 7]]
        bl = self.newbufs(["win%d" % i for i in range(NWR)] + ["kst0", "kst1", "vst0", "vst1", "sq0", "sq1"]
                          + ["wst%d" % i for i in range(NWS)])
        for b in self.qb:
            b.alias(self.region_bufs)
        win_b = bl[0:NWR]
        kst_b = bl[NWR:NWR + 2]
        vst_b = bl[NWR + 2:NWR + 4]
        sq_b = bl[NWR + 4:NWR + 6]
        wst_b = bl[NWR + 6:NWR + 6 + NWS]
        wst = [self.f32(O_WST + i * 8192, 2048) for i in range(NWS)]
        self.region_bufs = bl + self.qb
        win_sb = [self.bf(O_WIN + s * 4096, 2048).rearrange("p (c f) -> p c f", c=16) for s in range(NWR)]
        kst = [self.bf(O_KST + s * 2048, 1024) for s in range(2)]
        vst = [self.bf(O_VST + s * 6144, 3072).rearrange("p (t g c) -> p t g c", t=8, g=3) for s in range(2)]
        self.norm(l * 3 + 1, O_SQ_W, sq_b)
        role_idx = {}
        cnt = {"q": 0, "k": 0, "v": 0}
        for j in range(40):
            r = WIN_ROLE[j]
            role_idx[j] = (r, cnt[r])
            cnt[r] += 1
        order = (list(range(22, 26)) + [36, 37] + list(range(26, 30)) + [38, 39]
                 + list(range(6, 12)) + list(range(12, 18))
                 + list(range(18, 22)) + list(range(0, 6)) + list(range(30, 36)))
        assert sorted(order) == list(range(40))
        bank = [0]

        def nb():
            b = bank[0] % 8
            bank[0] += 1
            return b

        issued = [0]

        def pump(upto):
            while issued[0] <= min(upto, 39):
                p_ = issued[0]
                j = order[p_]
                s = p_ % NWR
                ws = p_ % NWS
                self.dma("sp", wst[ws], w_d[j], [], [wst_b[ws]])
                self.S.op("dve", (lambda e, o=win_sb[s].rearrange("p c f -> p (c f)"), i=wst[ws]:
                                  e.tensor_copy(out=o, in_=i)), reads=[wst_b[ws]], writes=[win_b[s]])
                issued[0] += 1

        kdone = set()
        vdone = set()
        nk = nv = 0
        for pos, j in enumerate(order):
            pump(pos + 4)
            s = pos % NWR
            role, idx = role_idx[j]
            if role in ("q", "k"):
                if role == "k":
                    ks = nk % 2
                    nk += 1
                for h in range(2):
                    b = nb()
                    for c in range(16):
                        self.mm(self.pb_ap[b][:], win_sb[s][:, c, :], self.hT[:, c, h * 512:(h + 1) * 512],
                                c == 0, c == 15, [win_b[s], self.hb[c]], [self.pb[b]])
                    if role == "q":
                        self.act(self.QT[:, idx, h * 512:(h + 1) * 512], self.pb_ap[b][:], AF.Copy,
                                 [self.pb[b]], [self.qb[idx]])
                    else:
                        self.act(kst[ks][:, h * 512:(h + 1) * 512], self.pb_ap[b][:], AF.Copy,
                                 [self.pb[b]], [kst_b[ks]])
                if role == "k":
                    self.dma("act", kt_out[idx], kst[ks], [kst_b[ks]], [self.dk_own[idx]])
                    kdone.add(idx)
                    g = idx // 3
                    if all((3 * g + i) in kdone for i in range(3)):
                        kin = kt_out[3 * g:3 * g + 3].rearrange("h p t -> (h p) t")
                        S.op("pool", (lambda e, kin=kin, g=g: e.collective_compute(
                            "AllGather", ALU.bypass, replica_groups=cc_groups, ins=[kin.opt()], outs=[k_all[g].opt()])),
                            reads=self.dk_own[3 * g:3 * g + 3], writes=[self.dk_all[g]], dma=True, inc=1)
            else:
                g = idx // 3
                if idx % 3 != 0:
                    continue
                vs = nv % 2
                nv += 1
                s0 = pos % NWR
                assert s0 + 2 < NWR and [role_idx[order[pos + i]] for i in range(3)] == [("v", idx + i) for i in range(3)]
                w3 = self.bf(O_WIN + s0 * 4096, 3 * 2048).rearrange("p (g c f) -> p g c f", g=3, c=16)
                for tt_ in range(8):
                    b = nb()
                    o3 = self.pb_ap[b][:, 0:384].rearrange("p (g f) -> p g f", g=3)
                    for c in range(16):
                        self.mm(o3, self.hT[:, c, tt_ * 128:(tt_ + 1) * 128], w3[:, :, c, :], c == 0, c == 15,
                                [win_b[s0], win_b[s0 + 1], win_b[s0 + 2], self.hb[c]], [self.pb[b]])
                    self.S.op("dve", (lambda e, o=vst[vs][:, tt_, :, :], i=o3: e.tensor_copy(out=o, in_=i)),
                              reads=[self.pb[b]], writes=[vst_b[vs]])
                for i in range(3):
                    self.dma("act", v_out[3 * g + i].rearrange("(t p) c -> p t c", p=128), vst[vs][:, :, i, :],
                             [vst_b[vs]], [self.dv_own[3 * g + i]])
                vin = v_out[3 * g:3 * g + 3].rearrange("h t c -> (h t) c")
                S.op("pool", (lambda e, vin=vin, g=g: e.collective_compute(
                    "AllGather", ALU.bypass, replica_groups=cc_groups, ins=[vin.opt()], outs=[v_all[g].opt()])),
                    reads=self.dv_own[3 * g:3 * g + 3], writes=[self.dv_all[g]], dma=True, inc=1)
        self.kall, self.vall = k_all, v_all

    def ph_wout(self, l):
        w_d = self.din("wout_%d" % l, [4, 128, 16 * 512], F32)
        bl = self.newbufs(["wo0", "wo1"])
        self.region_bufs = bl
        wo_sb = [self.bf(O_WO + s * 16384, 8192).rearrange("p (c d) -> p c d", c=16) for s in range(2)]
        for dg in range(2):
            self.dma("pool", wo_sb[dg].rearrange("p c d -> p (c d)"), w_d[dg], [], [bl[dg]])
        u = 0
        for dg in range(4):
            s = dg % 2
            for dc in range(4):
                c = dg * 4 + dc
                for h in range(2):
                    b = u % 8
                    u += 1
                    for hc in range(16):
                        self.mm(self.pb_ap[b][:], wo_sb[s][:, hc, dc * 128:(dc + 1) * 128],
                                self.hT[:, hc, h * 512:(h + 1) * 512], hc == 0, hc == 15,
                                [bl[s], self.hb[hc]], [self.pb[b]])
                    xs = self.xT[:, c, h * 512:(h + 1) * 512]
                    self.tt(xs, self.pb_ap[b][:], xs, ALU.add, [self.pb[b]], [self.xb[c][h]])
            if dg + 2 < 4:
                self.dma("pool", wo_sb[s].rearrange("p c d -> p (c d)"), w_d[dg + 2], [], [bl[s]])

    def ph_xchg(self, l):
        S = self.S
        k_all, v_all = self.kall, self.vall
        kpad = self.dint("Kpad_%d" % l, [8, 128, 3072], BF16)
        vpad = self.dint("Vpad_%d" % l, [8, 3072, 128], BF16)
        self.dkpad = [[Buf("dkpad") for _ in range(3)] for _ in range(3)]
        self.dvpad = [[Buf("dvpad") for _ in range(3)] for _ in range(3)]
        k_all4 = k_all.rearrange("g (r q) t -> g r q t", r=4)
        v_all4 = v_all.rearrange("g (r q) c -> g r q c", r=4)
        for k in (-1, 0, 1):
            c0 = (k + 1) * 1024
            for di, (g, hl, nh, dh) in enumerate([(0, 0, 3, 0), (1, 0, 3, 3), (3, 1, 2, 6)]):
                def kdma(e, k=k, c0=c0, g=g, hl=hl, nh=nh, dh=dh):
                    rank = self._rank_of(e, k)
                    src = k_all4[g][bass.ds(rank, 1)][0, hl * 128:(hl + nh) * 128, :]
                    dst = kpad[dh:dh + nh, :, c0:c0 + 1024].rearrange("h p t -> (h p) t")
                    return e.dma_start(out=dst, in_=src)

                def vdma(e, k=k, c0=c0, g=g, hl=hl, nh=nh, dh=dh):
                    rank = self._rank_of(e, k)
                    src = v_all4[g][bass.ds(rank, 1)][0, hl * 1024:(hl + nh) * 1024, :].rearrange("(h t) c -> h t c", h=nh)
                    return e.dma_start(out=vpad[dh:dh + nh, c0:c0 + 1024, :], in_=src)

                S.op("sp", kdma, reads=[self.dk_all[g]], writes=[self.dkpad[di][k + 1]], dma=True)
                S.op("sp", vdma, reads=[self.dv_all[g]], writes=[self.dvpad[di][k + 1]], dma=True)
        self.kpad, self.vpad = kpad, vpad

    def ph_attn(self, l):
        S = self.S
        lam_init = 0.8 - 0.6 * math.exp(-0.3 * l)
        slopes = alibi_slopes()
        sl_c, sl_a, sl_b = slopes[0:6], slopes[6:12], slopes[12:16]
        kall5 = self.kall.rearrange("g (r h p) t -> g r h p t", r=4, h=3)
        vall5 = self.vall.rearrange("g (r h t) c -> g r h t c", r=4, h=3)
        TB_d = self.din("TB", [128, 4992], F32)
        TAC_d = self.din("TAC", [128, 8320], BF16)
        lam_d = self.din("lamv_%d" % l, [256], F32)
        sub_d = self.din("subln_%d" % l, [128, 1], F32)
        sink_d = self.din("sink_%d" % l, [6], F32)
        names = ["tb", "tac", "stgk", "stgv", "sb0", "sb1", "sb2", "pt0", "pt1", "pt2", "pt3", "tmp0", "tmp1", "tmp2", "sm", "dacc0", "dacc1", "onesf",
                 "kb0", "kb1", "zk0", "zk1", "vb0", "vb1", "vb2", "vb3"]
        bl = self.newbufs(names)
        for b in self.qb:
            b.alias([x for x in self.region_bufs if x not in self.qb])
        B = dict(zip(names, bl))
        self.region_bufs = bl + self.qb
        TB = self.f32(O_TB, 4992)
        TA1 = self.bf(O_TA1, 1024)
        EA1 = self.bf(O_EA1, 1024).rearrange("p (a b) -> p a b", a=2)
        TA4 = self.bf(O_TA4, 2048).rearrange("p (a b) -> p a b", a=4)
        TA16 = self.bf(O_TA16, 2048).rearrange("p (a b) -> p a b", a=4)
        TC = self.bf(O_TC, 1152)
        EC = self.bf(O_EC, 1024).rearrange("p (a b) -> p a b", a=2)
        sbuf = [self.f32(O_SB + i * 2048, 512) for i in range(3)]
        sb_b = [B["sb0"], B["sb1"], B["sb2"]]
        ptb = [self.bf(O_PT + i * 1024, 512) for i in range(4)]
        pt_b = [B["pt0"], B["pt1"], B["pt2"], B["pt3"]]
        tmp = [self.f32(O_TMP + i * 2048, 512) for i in range(3)]
        tmp_b = [B["tmp0"], B["tmp1"], B["tmp2"]]
        sqb16 = self.bf(O_TMP + 2 * 2048, 512)
        lam_sb = self.f32(O_SM, 256)
        prod = self.f32(O_SM + 1024, 128)
        scal = self.f32(O_SM + 1536, 16)
        sink_sb = self.f32(O_SM + 1600, 6)
        esink = self.f32(O_SM + 1632, 6)
        gsub = self.f32(O_SM + 1664, 1)
        sm_b = B["sm"]
        dacc = [None, None]
        dacc_b = [B["dacc0"], B["dacc1"]]
        self.dma("sp", TB, TB_d, [], [B["tb"]])
        self.dma("sp", lam_sb, lam_d.partition_broadcast(128), [], [sm_b])
        self.dma("sp", sink_sb, sink_d.partition_broadcast(128), [], [sm_b])
        self.dma("sp", gsub, sub_d, [], [sm_b])
        self.tt(prod[:, 0:64], lam_sb[:, 0:64], lam_sb[:, 64:128], ALU.mult, [sm_b], [sm_b])
        self.tt(prod[:, 64:128], lam_sb[:, 128:192], lam_sb[:, 192:256], ALU.mult, [sm_b], [sm_b])
        S.op("dve", lambda e: e.reduce_sum(out=scal[:, 0:1], in_=prod[:, 0:64], axis=AX.X), reads=[sm_b], writes=[sm_b])
        S.op("dve", lambda e: e.reduce_sum(out=scal[:, 1:2], in_=prod[:, 64:128], axis=AX.X), reads=[sm_b], writes=[sm_b])
        self.act(scal[:, 2:4], scal[:, 0:2], AF.Exp, [sm_b], [sm_b])
        self.tt(scal[:, 4:5], scal[:, 3:4], scal[:, 2:3], ALU.subtract, [sm_b], [sm_b])
        S.op("dve", lambda e: e.tensor_scalar_add(out=scal[:, 5:6], in0=scal[:, 4:5], scalar1=-lam_init),
             reads=[sm_b], writes=[sm_b])
        S.op("dve", lambda e: e.tensor_scalar_mul(out=gsub, in0=gsub, scalar1=1.0 - lam_init), reads=[sm_b], writes=[sm_b])
        self.act(esink, sink_sb, AF.Exp, [sm_b], [sm_b])
        nlam = scal[:, 5:6]
        onesf = prod
        S.op("pool", lambda e: e.memset(onesf, 1.0), reads=[sm_b], writes=[B["onesf"]])
        onesm = self.ones

        cnt = [0]

        LAG = 3
        pending = []

        def pop_one():
            blk, (sbk, si, pi), fin = pending.pop(0)
            nk = blk["nk"]
            for pv in blk["pvs"]:
                (bank, out, lhsT, c0, c1, st_, sp_) = pv[:7]
                nkk = pv[7] if len(pv) > 7 else nk
                self.mm(out, lhsT, ptb[pi][0:nkk, c0:c1], st_, sp_, [pt_b[pi]] + blk["vreads"] + [self.const_b],
                        [self.pb[bank]], skip=True)
            if fin is not None:
                fin()

        def flush():
            while pending:
                pop_one()

        def push(blk, fin=None):
            k = cnt[0]
            cnt[0] += 1
            sbk = 4 + k % 4
            si = k % 3
            pi = k % 4
            nk, ncol = blk["nk"], blk["ncol"]
            for (c0, c1, lhsT, rhs) in blk["scores"]:
                self.mm(self.pb_ap[sbk][0:nk, c0:c1], lhsT, rhs, True, True,
                        blk["kreads"] + blk["qreads"], [self.pb[sbk]])
            self.stt(sbuf[si][0:nk, 0:ncol], blk["tab"], blk["coef"], self.pb_ap[sbk][0:nk, 0:ncol],
                     ALU.mult, ALU.add, [blk["tabb"], self.pb[sbk]], [sb_b[si]])
            self.act(ptb[pi][0:nk, 0:ncol], sbuf[si][0:nk, 0:ncol], AF.Exp, [sb_b[si]], [pt_b[pi]],
                     scale=blk["scale"])
            pending.append((blk, (sbk, si, pi), fin))
            if len(pending) > LAG:
                pop_one()

        def run_blocks(blocks, fin=None, hook=None):
            n = len(blocks)
            for i, blk in enumerate(blocks):
                push(blk, fin if i == n - 1 else None)
                if hook is not None and i == LAG - 1:
                    hook()

        stg = O_STG
        KBm = [self.bf(stg, 4096), self.bf(stg + 8192, 4096)]
        VBs = self.bf(O_TA1, 4096).rearrange("p (n c) -> p n c", n=32)
        kb_b = [B["kb0"], B["kb1"]]
        zk_b = [B["zk0"], B["zk1"]]
        vb_b = [B["vb0"], B["vb1"], B["vb2"], B["vb3"]]
        S.op("pool", lambda e: e.memset(KBm[0][64:128, :], 0.0), writes=[zk_b[0]])
        S.op("pool", lambda e: e.memset(KBm[1][0:64, :], 0.0), writes=[zk_b[1]])
        sc_b = 0.125
        for h in range(4):
            flush()
            gg, hl = (6 + h) // 3, (6 + h) % 3
            for m in range(2):
                rs = slice(64 * m, 64 * m + 64)
                self.dma("sp", KBm[m][rs, :].rearrange("p (r t) -> p r t", r=4),
                         kall5[gg, :, hl, rs, :].rearrange("r p t -> p r t"), [self.dk_all[gg]], [kb_b[m]])
            for r4 in range(4):
                self.dma("sp", VBs[:, r4 * 8:(r4 + 1) * 8, :], vall5[gg, r4, hl].rearrange("(n p) c -> p n c", p=128),
                         [self.dv_all[gg]], [vb_b[r4]])
            ch = 6 + h
            for qb in range(2):
                qs = slice(qb * 512, (qb + 1) * 512)
                blocks = []
                for kb in range(32):
                    for m in range(2):
                        u0 = qb * 512 - kb * 128 + 3968
                        blocks.append(dict(
                            nk=128, ncol=512,
                            scores=[(0, 512, KBm[m][:, kb * 128:(kb + 1) * 128], self.QT[:, ch, qs])],
                            kreads=[kb_b[m], zk_b[m]], qreads=[self.qb[ch]],
                            tab=TB[:, u0:u0 + 512], tabb=B["tb"], coef=-sl_b[h] / sc_b, scale=sc_b,
                            pvs=[(m, self.pb_ap[m][:], VBs[:, kb, :], 0, 512, kb == 0, kb == 31),
                                 (2 + m, self.pb_ap[2 + m][:], onesm, 0, 512, kb == 0, kb == 31)]
                            + [(m, self.pb_ap[m][0:120, 0:DUMMY_N], self.zeros, 0, DUMMY_N, False, False, 1)] * N_DUMMY,
                            vreads=[vb_b[kb // 8]]))
                def fin_b(ch=ch, qs=qs):
                    t0, t1, t2 = tmp
                    self.powact(t0, self.pb_ap[2][:], [self.pb[2]], [tmp_b[0]])
                    self.powact(t1, self.pb_ap[3][:], [self.pb[3]], [tmp_b[1]])
                    self.tt(t0, self.pb_ap[0][:], t0, ALU.mult, [self.pb[0]], [tmp_b[0]])
                    self.tt(t1, self.pb_ap[1][:], t1, ALU.mult, [self.pb[1]], [tmp_b[1]])
                    self.stt(t0, t1, nlam, t0, ALU.mult, ALU.add, [tmp_b[1], sm_b], [tmp_b[0]])
                    self.act(sqb16, t0, AF.Square, [tmp_b[0]], [tmp_b[2]])
                    sbk = 4 + cnt[0] % 4
                    self.mm(self.pb_ap[sbk][:], onesm, sqb16, True, True, [tmp_b[2], self.const_b], [self.pb[sbk]])
                    self.powact(t1, self.pb_ap[sbk][:], [self.pb[sbk], self.const_b], [tmp_b[1]],
                                power=-0.5, scale_in=1.0 / 128, bias=self.eps)
                    self.stt(self.hT[:, ch, qs], t0, gsub, t1, ALU.mult, ALU.mult, [tmp_b[0], tmp_b[1], sm_b], [self.hb[ch]])

                run_blocks(blocks, fin_b)
        flush()

        B["stgk"].alias(kb_b + zk_b)
        B["stgv"].alias(kb_b + zk_b)
        B["tac"].alias(vb_b)
        self.ph_xchg(l)
        kpad, vpad = self.kpad, self.vpad
        self.dma("sp", self.bf(O_TA1, 8320), TAC_d, [], [B["tac"]])
        stg_off = [stg, O_TB]
        stgA_b = [(B["stgk"], B["stgv"]), (Buf("stgk1").alias([B["tb"]]), Buf("stgv1").alias([B["tb"]]))]
        sc_a = 128 ** -0.5

        def a_views(sl):
            o = stg_off[sl]
            return (self.bf(o, 2560),
                    self.bf(o + 5120, 640).rearrange("p (n c) -> p n c", n=5),
                    self.bf(o + 6400, 1024).rearrange("p (r n c) -> p r n c", r=4, n=2),
                    self.bf(o + 8448, 2048).rearrange("p (r c) -> p r c", r=16),
                    self.bf(o + 12544, 2048).rearrange("p (r c) -> p r c", r=16))

        def a_loads(h, qb, sl):
            KAw, VA1s, VA4s, VA16a, VA16b = a_views(sl)
            kb_, vb_ = stgA_b[sl]
            q0 = qb * 512
            di = h // 3
            self.dma("sp", KAw, kpad[h][:, q0:q0 + 2560], self.dkpad[di], [kb_])
            self.dma("sp", VA1s, vpad[h][960 + q0:960 + q0 + 640].rearrange("(n p) c -> p n c", p=128),
                     self.dvpad[di], [vb_])
            for r in range(4):
                st4 = 768 + r + q0
                self.dma("sp", VA4s[:, r, :, :],
                         vpad[h][st4:st4 + 4 * 255 + 1:4].rearrange("(n p) c -> p n c", p=128),
                         self.dvpad[di], [vb_])
            self.dma("sp", VA16a, vpad[h][q0:q0 + 2048].rearrange("(p r) c -> p r c", r=16), self.dvpad[di], [vb_])
            self.dma("sp", VA16b[0:32], vpad[h][q0 + 2048:q0 + 2560].rearrange("(p r) c -> p r c", r=16),
                     self.dvpad[di], [vb_])

        def a_compute(h, qb, sl, ab, hook=None):
            bn, bd = 2 * ab, 2 * ab + 1
            KAw, VA1s, VA4s, VA16a, VA16b = a_views(sl)
            kb_, vb_ = stgA_b[sl]
            q0 = qb * 512
            coef = -sl_a[h] / sc_a
            qcol = self.QT[:, h, q0:q0 + 512]
            common = dict(kreads=[kb_], qreads=[self.qb[h]], tabb=B["tac"], coef=coef, scale=sc_a, vreads=[vb_])
            blocks = []
            for n in range(5):
                if qb == 0 and n == 0:
                    tab = EA1[:, 0, :]
                elif qb == 1 and n == 4:
                    tab = EA1[:, 1, :]
                else:
                    u0 = 512 - 128 * n
                    tab = TA1[:, u0:u0 + 512]
                kc = 960 + 128 * n
                blocks.append(dict(
                    nk=128, ncol=512, scores=[(0, 512, KAw[:, kc:kc + 128], qcol)], tab=tab,
                    pvs=[(bn, self.pb_ap[bn][:], VA1s[:, n, :], 0, 512, n == 0, False),
                         (bd, self.pb_ap[bd][:], onesm, 0, 512, n == 0, False)], **common))
            for n in range(2):
                scores = []
                pvs = []
                for r in range(4):
                    s_r = 768 + 512 * n + r
                    scores.append((r * 128, (r + 1) * 128, KAw[:, s_r:s_r + 4 * 127 + 1:4], self.QT[:, h, q0 + r:q0 + 512:4]))
                    pvs.append((bn, self.pb_ap[bn][:, r:512:4], VA4s[:, r, n, :], r * 128, (r + 1) * 128, False, False))
                    pvs.append((bd, self.pb_ap[bd][:, r:512:4], onesm, r * 128, (r + 1) * 128, False, False))
                blocks.append(dict(nk=128, ncol=512, scores=scores, tab=TA4[:, 2 * qb + n, :], pvs=pvs, **common))
            for n in range(2):
                nk = 128 if n == 0 else 32
                scores = []
                pvs = []
                for r in range(16):
                    s_r = r if n == 0 else 2048 + r
                    scores.append((r * 32, (r + 1) * 32, KAw[:, s_r:s_r + 16 * (nk - 1) + 1:16], self.QT[:, h, q0 + r:q0 + 512:16]))
                    vt = VA16a[:, r, :] if n == 0 else VA16b[0:32, r, :]
                    last = (n == 1 and r == 15)
                    pvs.append((bn, self.pb_ap[bn][:, r:512:16], vt, r * 32, (r + 1) * 32, False, last))
                    pvs.append((bd, self.pb_ap[bd][:, r:512:16], onesm[0:nk, :], r * 32, (r + 1) * 32, False, last))
                blocks.append(dict(nk=nk, ncol=512, scores=scores, tab=TA16[0:nk, 2 * qb + n, :], pvs=pvs, **common))
            def fin_a(h=h, q0=q0, ab=ab, bn=bn, bd=bd):
                t0 = tmp[ab]
                self.powact(t0, self.pb_ap[bd][:], [self.pb[bd]], [tmp_b[ab]])
                self.tt(self.hT[:, h, q0:q0 + 512], self.pb_ap[bn][:], t0, ALU.mult, [self.pb[bn], tmp_b[ab]], [self.hb[h]])

            run_blocks(blocks, fin_a, hook)

        unitsA = [(h, qb) for h in range(6) for qb in range(2)]
        a_loads(unitsA[0][0], unitsA[0][1], 0)
        for ui, (h, qb) in enumerate(unitsA):
            hk = None
            if ui + 1 < len(unitsA):
                hk = (lambda u=ui + 1: a_loads(unitsA[u][0], unitsA[u][1], u % 2))
            a_compute(h, qb, ui % 2, ui % 2, hk)
        flush()

        KCw = self.bf(stg, 1280)
        VCs = self.bf(stg + 2560, 1280).rearrange("p (n c) -> p n c", n=10)
        sc_c = 128 ** -0.5
        for g in range(2):
            flush()
            self.dma("sp", KCw, kpad[6 + g][:, 896:2176], self.dkpad[2], [B["stgk"]])
            self.dma("sp", VCs, vpad[6 + g][896:2176].rearrange("(n p) c -> p n c", p=128), self.dvpad[2], [B["stgv"]])
            for hh in range(3):
                hq = g * 3 + hh
                ch = 10 + hq
                for qb in range(2):
                    q0 = qb * 512
                    ab = (hq * 2 + qb) % 2
                    bn, bd = 2 * ab, 2 * ab + 1
                    blocks = []
                    for n in range(6):
                        if qb == 0 and n == 0:
                            tab = EC[:, 0, :]
                        elif qb == 1 and n == 5:
                            tab = EC[:, 1, :]
                        else:
                            u0 = 640 - 128 * n
                            tab = TC[:, u0:u0 + 512]
                        kt = 4 * qb + n
                        blocks.append(dict(
                            nk=128, ncol=512,
                            scores=[(0, 512, KCw[:, kt * 128:(kt + 1) * 128], self.QT[:, ch, q0:q0 + 512])],
                            kreads=[B["stgk"]], qreads=[self.qb[ch]], tab=tab, tabb=B["tac"],
                            coef=-sl_c[hq] / sc_c, scale=sc_c,
                            pvs=[(bn, self.pb_ap[bn][:], VCs[:, kt, :], 0, 512, n == 0, n == 5),
                                 (bd, self.pb_ap[bd][:], onesm, 0, 512, n == 0, n == 5)],
                            vreads=[B["stgv"]]))
                    def fin_c(ch=ch, q0=q0, ab=ab, bn=bn, bd=bd, hq=hq):
                        t0 = tmp[ab]
                        self.powact(t0, self.pb_ap[bd][:], [self.pb[bd], sm_b], [tmp_b[ab]], bias=esink[:, hq:hq + 1])
                        self.tt(self.hT[:, ch, q0:q0 + 512], self.pb_ap[bn][:], t0, ALU.mult,
                                [self.pb[bn], tmp_b[ab]], [self.hb[ch]])

                    run_blocks(blocks, fin_c)
        flush()


def _lay_gu(w):
    F = w.shape[1]
    return np.ascontiguousarray(w.reshape(16, 128, F // 128, 128).transpose(2, 1, 0, 3)).reshape(F // 128, 128, 2048)


def _lay_wout(w):
    return np.ascontiguousarray(w.reshape(16, 128, 4, 512).transpose(2, 1, 0, 3)).reshape(4, 128, 16 * 512)


def _gains(inputs):
    g = np.zeros((7, 2048), np.float32)
    for l in range(DEPTH):
        g[l * 3 + 0] = inputs["ffn1_norm"][l]
        g[l * 3 + 1] = inputs["mix_norm"][l]
        g[l * 3 + 2] = inputs["ffn2_norm"][l]
    g[6] = inputs["final_norm"]
    return np.ascontiguousarray(g.reshape(7, 16, 128).transpose(2, 0, 1)).reshape(128, 7 * 16)


_PROG_CACHE = {}


def get_prog(phases):
    key = repr(phases)
    if key not in _PROG_CACHE:
        _PROG_CACHE[key] = Prog(phases)
    return _PROG_CACHE[key]


def _tables(cc):
    i = np.arange(128, dtype=np.int64)[:, None]
    u = np.arange(4992, dtype=np.int64)[None, :]
    TB = np.abs(cc * 1024 + u - 3968 - i).astype(np.float32)

    def band(delta, radius, mult, ok):
        d = np.abs(delta)
        return np.where((d <= radius) & ok, (mult * d).astype(np.float32), np.float32(BIG))

    j = np.arange(512, dtype=np.int64)[None, :]
    u1 = np.arange(1024, dtype=np.int64)[None, :]
    TA1 = band(u1 - 448 - i, 64, 1, True)
    k = -64 + i
    EA1_0 = band(j + 64 - i, 64, 1, (cc * 1024 + k >= 0))
    k = 960 + i
    EA1_1 = band(j - 448 - i, 64, 1, (cc * 1024 + k < SEQ))
    TA4 = []
    for qb in range(2):
        for n in range(2):
            jp = (np.arange(512, dtype=np.int64) % 128)[None, :]
            kg = cc * 256 + qb * 128 - 64 + 128 * n + i
            TA4.append(band(jp + 64 - 128 * n - i, 64, 4, (kg >= 0) & (kg < 1024)))
    TA16 = []
    for qb in range(2):
        for n in range(2):
            jp = (np.arange(512, dtype=np.int64) % 32)[None, :]
            if n == 0:
                delta = jp + 64 - i
                kp = qb * 32 - 64 + i
                ok = np.ones_like(i, dtype=bool)
            else:
                delta = jp - 64 - i
                kp = qb * 32 + 64 + i
                ok = i < 32
            kg = cc * 64 + kp
            TA16.append(band(delta, 64, 16, ok & (kg >= 0) & (kg < 256)))
    uc = np.arange(1152, dtype=np.int64)[None, :]
    TC = band(uc - 512 - i, 128, 1, True)
    k = -128 + i
    EC_0 = band(j + 128 - i, 128, 1, (cc * 1024 + k >= 0))
    k = 1024 + i
    EC_1 = band(j - 512 - i, 128, 1, (cc * 1024 + k < SEQ))
    TAC = np.concatenate([TA1, EA1_0, EA1_1] + TA4 + TA16 + [TC, EC_0, EC_1], axis=1)
    assert TAC.shape == (128, 8320)
    return TB, np.ascontiguousarray(TAC.astype(NPBF))


def _attn_small(inputs, l):
    lamv = np.concatenate([inputs["diff_lambda_q1"][l], inputs["diff_lambda_k1"][l],
                           inputs["diff_lambda_q2"][l], inputs["diff_lambda_k2"][l]]).astype(np.float32)
    return {"lamv_%d" % l: lamv,
            "subln_%d" % l: np.ascontiguousarray(inputs["diff_subln"][l].reshape(128, 1)).astype(np.float32),
            "sink_%d" % l: np.ascontiguousarray(inputs["swa_sink"][l]).astype(np.float32)}


def _run(phases, shared, percore):
    prog = get_prog(phases)
    in_maps = []
    for c in range(NCORES):
        m = dict(shared)
        m.update(percore[c])
        in_maps.append(m)
    res = run_bass_kernel_spmd(prog.nc, in_maps, core_ids=list(range(NCORES)))
    return res.results


PH_FUSED = [("load_x",),
            ("ffn", 0, 1), ("win", 0), ("attn", 0), ("wout", 0), ("ffn", 0, 2),
            ("ffn", 1, 1), ("win", 1), ("attn", 1), ("wout", 1), ("ffn", 1, 2),
            ("final",)]


def kernel(x, ffn1_norm, ffn1_w_gate, ffn1_w_up, ffn1_w_down, mix_norm, w_in, w_out,
           diff_lambda_q1, diff_lambda_k1, diff_lambda_q2, diff_lambda_k2, diff_subln,
           swa_sink, ffn2_norm, ffn2_w_gate, ffn2_w_up, ffn2_w_down, final_norm):
    inputs = dict(x=x, ffn1_norm=ffn1_norm, ffn1_w_gate=ffn1_w_gate, ffn1_w_up=ffn1_w_up,
                  ffn1_w_down=ffn1_w_down, mix_norm=mix_norm, w_in=w_in, w_out=w_out,
                  diff_lambda_q1=diff_lambda_q1, diff_lambda_k1=diff_lambda_k1,
                  diff_lambda_q2=diff_lambda_q2, diff_lambda_k2=diff_lambda_k2, diff_subln=diff_subln,
                  swa_sink=swa_sink, ffn2_norm=ffn2_norm, ffn2_w_gate=ffn2_w_gate, ffn2_w_up=ffn2_w_up,
                  ffn2_w_down=ffn2_w_down, final_norm=final_norm)
    inputs = {k: np.asarray(v) for k, v in inputs.items()}
    x = inputs["x"]
    shared = {"gains": _gains(inputs)}
    ffn_w = {1: (inputs["ffn1_w_gate"], inputs["ffn1_w_up"], inputs["ffn1_w_down"]),
             2: (inputs["ffn2_w_gate"], inputs["ffn2_w_up"], inputs["ffn2_w_down"])}
    for l in range(DEPTH):
        for which in (1, 2):
            wg, wu, wd = ffn_w[which]
            tag = "%d_%d" % (l, which)
            shared["wg_" + tag] = _lay_gu(wg[l])
            shared["wu_" + tag] = _lay_gu(wu[l])
            shared["wd_" + tag] = np.ascontiguousarray(wd[l]).reshape(NF, 128, 2048)
        shared["win_%d" % l] = _lay_gu(inputs["w_in"][l])
        shared["wout_%d" % l] = _lay_wout(inputs["w_out"][l])
        shared.update(_attn_small(inputs, l))
    tabs = [_tables(cc) for cc in range(4)]
    pc = []
    for c in range(NCORES):
        b, cc = c // 4, c % 4
        pc.append({"xT_in": np.ascontiguousarray(x[b, cc * 1024:(cc + 1) * 1024, :].T).reshape(16, 128, 1024),
                   "TB": tabs[cc][0], "TAC": tabs[cc][1]})
    res = _run(PH_FUSED, shared, pc)
    out = np.empty((BATCH, SEQ, D_MODEL), np.float32)
    for c in range(NCORES):
        b, cc = c // 4, c % 4
        out[b, cc * 1024:(cc + 1) * 1024, :] = res[c]["outT"].reshape(2048, 1024).T
    return out
```

```python
import contextlib
import math
import numpy as np
import ml_dtypes
import concourse.bass as bass
import concourse.mybir as mybir
from concourse.bass_utils import run_bass_kernel_spmd

F32 = mybir.dt.float32
BF16 = mybir.dt.bfloat16
ALU = mybir.AluOpType
AF = mybir.ActivationFunctionType
AX = mybir.AxisListType
NPBF = ml_dtypes.bfloat16

D_MODEL = 2048
BATCH = 2
SEQ = 4096
DEPTH = 2
D_FF = 5504
NF = D_FF // 128
NC_ = 16
TL = 1024
NCORES = 8
RMS_EPS = 1e-6
BIG = 30000.0
GSZ = 5
N_DUMMY = 0
DUMMY_N = 128

ENGS = ("pe", "act", "dve", "pool", "sp")


class Buf:
    __slots__ = ("w", "wd", "r", "rd", "name")

    def __init__(self, name=""):
        self.w = {}
        self.wd = []
        self.r = {}
        self.rd = []
        self.name = name

    def alias(self, olds):
        for b in olds:
            for src in (b.w, b.r):
                for e, o in src.items():
                    cur = self.r.get(e)
                    if cur is None or cur.idx < o.idx:
                        self.r[e] = o
            self.rd.extend(b.wd)
            self.rd.extend(b.rd)
        return self


class Op:
    __slots__ = ("eng", "fn", "deps", "needed", "dma", "sem", "semval", "prev_semval", "idx", "inc")

    def __init__(self, eng, fn, dma, idx):
        self.eng = eng
        self.fn = fn
        self.dma = dma
        self.idx = idx
        self.deps = []
        self.needed = False
        self.sem = None
        self.semval = 0
        self.prev_semval = 0


class Sched:
    def __init__(self, n_dma_sems=8):
        self.ops = {e: [] for e in ENGS}
        self.n_dma_sems = n_dma_sems
        self.nops = 0

    def op(self, eng, fn, reads=(), writes=(), dma=False, inc=16):
        o = Op(eng, fn, dma, self.nops)
        o.inc = inc
        deps = {}

        def add(d):
            if (not d.dma) and (not dma) and d.eng == "pe" and eng == "pe":
                return
            deps[id(d)] = d

        for b in reads:
            for d in b.w.values():
                add(d)
            for d in b.wd:
                add(d)
        for b in writes:
            for d in b.w.values():
                add(d)
            for d in b.wd:
                add(d)
            for d in b.r.values():
                add(d)
            for d in b.rd:
                add(d)
        for b in reads:
            if dma:
                b.rd.append(o)
            else:
                b.r[eng] = o
        for b in writes:
            b.r = {}
            b.rd = []
            if dma:
                b.w = {}
                b.wd = [o]
            else:
                b.w = {eng: o}
                b.wd = []
        o.deps = list(deps.values())
        for d in o.deps:
            d.needed = True
        self.ops[eng].append(o)
        self.nops += 1
        return o

    def emit(self, nc, stack):
        esem = {e: stack.enter_context(nc.semaphore("S_" + e)) for e in ENGS}
        dsem = {}
        for e in ENGS:
            for inc in sorted(set(o.inc for o in self.ops[e] if o.dma)):
                dsem[(e, inc)] = [stack.enter_context(nc.semaphore("D_%s%d_%d" % (e, inc, i)))
                                  for i in range(self.n_dma_sems if inc == 16 else 2)]
        for e in ENGS:
            c = 0
            kk = {}
            cn = {}
            for o in self.ops[e]:
                if o.dma:
                    pool = dsem[(e, o.inc)]
                    k = kk.get(o.inc, 0)
                    cnts = cn.setdefault(o.inc, [0] * len(pool))
                    i = k % len(pool)
                    o.sem = pool[i]
                    o.prev_semval = cnts[i]
                    cnts[i] += o.inc
                    o.semval = cnts[i]
                    kk[o.inc] = k + 1
                elif o.needed:
                    c += 1
                    o.sem = esem[e]
                    o.semval = c
        block = stack.enter_context(nc.Block())
        handles = {"pe": block.tensor, "act": block.scalar, "dve": block.vector,
                   "pool": block.gpsimd, "sp": block.sync}
        stats = {}
        for e in ENGS:
            ops = self.ops[e]
            if not ops:
                continue
            nwait = [0]

            def body(eng, ops=ops, nwait=nwait):
                known = {}
                for o in ops:
                    waits = {}
                    for d in o.deps:
                        key = id(d.sem)
                        if known.get(key, 0) >= d.semval:
                            continue
                        if key not in waits or waits[key][1] < d.semval:
                            waits[key] = (d.sem, d.semval)
                    if o.dma and o.prev_semval > 0:
                        key = id(o.sem)
                        if known.get(key, 0) < o.prev_semval:
                            if key not in waits or waits[key][1] < o.prev_semval:
                                waits[key] = (o.sem, o.prev_semval)
                    for key, (s, v) in waits.items():
                        eng.wait_ge(s, v)
                        known[key] = v
                        nwait[0] += 1
                    if o.fn is None:
                        continue
                    inst = o.fn(eng)
                    if o.dma:
                        inst.then_inc(o.sem, o.inc)
                    elif o.needed:
                        inst.then_inc(o.sem, 1)

            handles[e](body)
            stats[e] = (len(ops), nwait[0])
        return stats


O_XT = 0
O_RSTD = 65536
O_GAIN = O_RSTD + 4096
O_CONST = O_GAIN + 512
RB = O_CONST + 512
O_H = RB
O_Q = RB + 32768
O_Z = RB + 65536
O_WGU = O_Q
O_SG = O_Q + 24576
O_SQ_F = O_Q + 28672
O_WD = O_Z
O_ACT = O_Z + 2 * GSZ * 4096
NWR = 6
NWS = 3
O_WIN = O_Z
O_WST = O_Z + NWR * 4096
O_KST = O_WST + NWS * 8192
O_VST = O_KST + 4096
O_SQ_W = O_VST + 12288
O_TB = O_Z
O_TA1 = O_TB + 19968
O_EA1 = O_TA1 + 2048
O_TA4 = O_EA1 + 2048
O_TA16 = O_TA4 + 4096
O_TC = O_TA16 + 4096
O_EC = O_TC + 2304
O_STG = O_EC + 2048
O_SB = O_STG + 16640
O_PT = O_SB + 6144
O_TMP = O_PT + 4096
O_SM = O_TMP + 6144
O_DACC = O_SM + 2048
ATT_END = O_DACC
O_WO = O_Z
ARENA_BYTES = max(ATT_END, O_ACT + 2 * GSZ * 2048, O_WO + 32768)

WIN_ROLE = (["q"] * 6 + ["k"] * 6 + ["v"] * 6 + ["q"] * 4 + ["k"] * 4 + ["v"] * 4
            + ["q"] * 6 + ["k"] * 2 + ["v"] * 2)


def alibi_slopes():
    return [2.0 ** (-8.0 * (i + 1) / 16) for i in range(16)]


class Prog:
    def __init__(self, phases):
        self.phases = phases
        self.nc = nc = bass.Bass("TRN2", target_bir_lowering=False)
        self.S = Sched()
        self.dram = {}
        self.stack = contextlib.ExitStack()
        st = self.stack
        self.arena = st.enter_context(nc.sbuf_tensor("arena", [128, ARENA_BYTES // 2], BF16))
        self.pb_ap = [st.enter_context(nc.psum_tensor("pb%d" % i, [128, 512], F32)) for i in range(8)]
        self.pb = [Buf("pb%d" % i) for i in range(8)]
        self.xT = self.f32(O_XT, 16 * 1024).rearrange("p (c t) -> p c t", c=16)
        self.xb = [[Buf("x%d_%d" % (c, h)) for h in range(2)] for c in range(16)]
        self.rstd = self.f32(O_RSTD, 1024)
        self.rstd_b = Buf("rstd")
        self.gain = self.f32(O_GAIN, 7 * 16).rearrange("p (v c) -> p v c", v=7)
        self.gain_b = Buf("gain")
        self.ones = self.bf(O_CONST, 128)
        self.eps = self.f32(O_CONST + 256, 1)
        self.zeros = self.arena[0:1, (O_CONST + 264) // 2:(O_CONST + 264) // 2 + 120]
        self.const_b = Buf("const")
        self.hT = self.bf(O_H, 16 * 1024).rearrange("p (c t) -> p c t", c=16)
        self.hb = [Buf("h%d" % c) for c in range(16)]
        self.QT = self.bf(O_Q, 16 * 1024).rearrange("p (c t) -> p c t", c=16)
        self.qb = [Buf("q%d" % c) for c in range(16)]
        self.region_bufs = []
        self.out_bufs = []
        self.build()

    def bf(self, off, n):
        return self.arena[:, off // 2: off // 2 + n]

    def f32(self, off, n):
        return self.arena[:, off // 2: off // 2 + 2 * n].bitcast(F32)

    def din(self, name, shape, dt):
        if name not in self.dram:
            self.dram[name] = self.nc.dram_tensor(name, list(shape), dt, kind="ExternalInput").ap()
        return self.dram[name]

    def dout(self, name, shape, dt):
        if name not in self.dram:
            self.dram[name] = self.nc.dram_tensor(name, list(shape), dt, kind="ExternalOutput").ap()
        return self.dram[name]

    def _rank_of(self, e, k):
        if not hasattr(self, "_rk"):
            self._rk = {}
            cc = e.partition_id() % 4
            for kk in (-1, 0, 1):
                self._rk[kk] = e.snap((cc + (kk + 4)) % 4)
        return self._rk[k]

    def dint(self, name, shape, dt):
        if name not in self.dram:
            self.dram[name] = self.nc.dram_tensor(name, list(shape), dt).ap()
        return self.dram[name]

    def newbufs(self, names):
        bs = [Buf(n).alias(self.region_bufs) for n in names]
        return bs

    def mm(self, out, lhsT, rhs, start, stop, reads, writes, skip=False):
        kw = {"skip_group_check": True} if skip else {}
        self.S.op("pe", lambda e: e.matmul(out, lhsT=lhsT, rhs=rhs, start=start, stop=stop, **kw),
                  reads=reads, writes=writes)

    def dma(self, eng, out, in_, reads, writes):
        return self.S.op(eng, lambda e: e.dma_start(out=out, in_=in_), reads=reads, writes=writes, dma=True)

    def act(self, out, in_, func, reads, writes, scale=None, bias=None):
        kw = {}
        if scale is not None:
            kw["scale"] = scale
        if bias is not None:
            kw["bias"] = bias
        self.S.op("act", lambda e: e.activation(out=out, in_=in_, func=func, **kw), reads=reads, writes=writes)

    def stt(self, out, in0, scalar, in1, op0, op1, reads, writes, eng="dve"):
        self.S.op(eng, lambda e: e.scalar_tensor_tensor(out=out, in0=in0, scalar=scalar, in1=in1, op0=op0, op1=op1),
                  reads=reads, writes=writes)

    def tt(self, out, in0, in1, op, reads, writes, eng="dve"):
        self.S.op(eng, lambda e: e.tensor_tensor(out=out, in0=in0, in1=in1, op=op), reads=reads, writes=writes)

    def powact(self, out, in_, reads, writes, power=-1.0, scale_in=None, bias=None):
        self.act(out, in_, AF.Ln, reads, writes, scale=scale_in, bias=bias)
        self.act(out, out, AF.Exp, [], writes, scale=power)

    def recip(self, out, in_, reads, writes):
        self.S.op("dve", lambda e: e.reciprocal(out=out, in_=in_), reads=reads, writes=writes)

    def build(self):
        S = self.S
        S.op("pool", lambda e: e.memset(self.ones, 1.0), writes=[self.const_b])
        S.op("pool", lambda e: e.memset(self.eps, RMS_EPS), writes=[self.const_b])
        S.op("pool", lambda e: e.memset(self.zeros, 0.0), writes=[self.const_b])
        g_in = self.din("gains", [128, 7 * 16], F32)
        self.dma("sp", self.f32(O_GAIN, 7 * 16), g_in, [], [self.gain_b])
        for ph in self.phases:
            kind = ph[0]
            if kind == "load_x":
                self.ph_load_x()
            elif kind == "load_state":
                self.ph_load_state()
            elif kind == "ffn":
                self.ph_ffn(ph[1], ph[2])
            elif kind == "win":
                self.ph_win(ph[1])
            elif kind == "store_state":
                self.ph_store_state()
            elif kind == "attn":
                self.ph_attn(ph[1])
            elif kind == "xchg":
                self.ph_xchg(ph[1])
            elif kind == "wout":
                self.ph_wout(ph[1])
            elif kind == "final":
                self.ph_final()
            elif kind == "dump_h":
                h_out = self.dout("hT_out", [16, 128, 1024], BF16)
                for c in range(16):
                    ob = Buf("ho")
                    self.dma("sp", h_out[c], self.hT[:, c, :], [self.hb[c]], [ob])
                    self.out_bufs.append(ob)
            else:
                raise ValueError(kind)
        S.op("sp", None, reads=self.out_bufs)
        self.stats = S.emit(self.nc, self.stack)

    def ph_load_x(self):
        x_in = self.din("xT_in", [16, 128, 1024], F32)
        for c in range(16):
            self.dma("sp", self.xT[:, c, :], x_in[c], [], [self.xb[c][0], self.xb[c][1]])

    def ph_load_state(self):
        self.ph_load_x()
        q_in = self.din("QT_in", [16, 128, 1024], BF16)
        for b in self.qb:
            b.alias(self.region_bufs)
        self.region_bufs = list(self.qb)
        for c in range(16):
            self.dma("sp", self.QT[:, c, :], q_in[c], [], [self.qb[c]])

    def ph_store_state(self):
        x_out = self.dout("xT_out", [16, 128, 1024], F32)
        q_out = self.dout("QT_out", [16, 128, 1024], BF16)
        for c in range(16):
            ob = Buf("xo")
            self.dma("sp", x_out[c], self.xT[:, c, :], [self.xb[c][0], self.xb[c][1]], [ob])
            self.out_bufs.append(ob)
        for c in range(16):
            ob = Buf("qo")
            self.dma("sp", q_out[c], self.QT[:, c, :], [self.qb[c]], [ob])
            self.out_bufs.append(ob)

    def norm(self, vidx, sq_off, sq_bufs, final_out=None):
        sq = [self.bf(sq_off + i * 2048, 1024) for i in range(2)]
        P6, P7 = self.pb[6], self.pb[7]
        for c in range(16):
            i = c % 2
            self.act(sq[i], self.xT[:, c, :], AF.Square, [self.xb[c][0], self.xb[c][1]], [sq_bufs[i]])
            for h in range(2):
                self.mm(self.pb_ap[6 + h][:], self.ones, sq[i][:, h * 512:(h + 1) * 512], c == 0, c == 15,
                        [sq_bufs[i], self.const_b], [self.pb[6 + h]])
        for h in range(2):
            r = self.rstd[:, h * 512:(h + 1) * 512]
            self.powact(r, self.pb_ap[6 + h][:], [self.pb[6 + h], self.const_b], [self.rstd_b],
                        power=-0.5, scale_in=1.0 / D_MODEL, bias=self.eps)
        for c in range(16):
            if final_out is None:
                self.stt(self.hT[:, c, :], self.xT[:, c, :], self.gain[:, vidx, c:c + 1], self.rstd,
                         ALU.mult, ALU.mult, [self.xb[c][0], self.xb[c][1], self.gain_b, self.rstd_b], [self.hb[c]])
            else:
                self.stt(self.xT[:, c, :], self.xT[:, c, :], self.gain[:, vidx, c:c + 1], self.rstd,
                         ALU.mult, ALU.mult, [self.gain_b, self.rstd_b], [self.xb[c][0], self.xb[c][1]])
                ob = Buf("fo")
                self.dma("sp", final_out[c], self.xT[:, c, :], [self.xb[c][0], self.xb[c][1]], [ob])
                self.out_bufs.append(ob)

    def ph_final(self):
        out = self.dout("outT", [16, 128, 1024], F32)
        sqb = self.newbufs(["sq0", "sq1"])
        self.region_bufs = sqb
        self.norm(6, O_SQ_F, sqb, final_out=out)

    def ph_ffn(self, l, which):
        S = self.S
        tag = "%d_%d" % (l, which)
        wg_d = self.din("wg_" + tag, [NF, 128, 2048], F32)
        wu_d = self.din("wu_" + tag, [NF, 128, 2048], F32)
        wd_d = self.din("wd_" + tag, [NF, 128, 2048], F32)
        names = (["wgu%d" % i for i in range(3)] + ["wd%d" % i for i in range(2)] + ["sg0", "sg1", "sq0", "sq1"]
                 + ["act%d_%d_%d" % (a, j, h) for a in range(2) for j in range(GSZ) for h in range(2)])
        bl = self.newbufs(names)
        wgu_b = bl[0:3]
        wd_b = bl[3:5]
        sg_b = bl[5:7]
        sq_b = bl[7:9]
        act_b = [[[bl[9 + (a * GSZ + j) * 2 + h] for h in range(2)] for j in range(GSZ)] for a in range(2)]
        self.region_bufs = bl
        wg_sb = [self.bf(O_WGU + s * 8192, 2048).rearrange("p (c f) -> p c f", c=16) for s in range(3)]
        wu_sb = [self.bf(O_WGU + s * 8192 + 4096, 2048).rearrange("p (c f) -> p c f", c=16) for s in range(3)]
        wd_sb = [self.bf(O_WD + s * GSZ * 4096, GSZ * 2048).rearrange("p (g d) -> p g d", g=GSZ) for s in range(2)]
        act_sb = [self.bf(O_ACT + s * GSZ * 2048, GSZ * 1024).rearrange("p (g t) -> p g t", g=GSZ) for s in range(2)]
        sg_sb = [self.f32(O_SG + s * 2048, 512) for s in range(2)]
        vidx = l * 3 + (0 if which == 1 else 2)
        self.norm(vidx, O_SQ_F, sq_b)

        groups = []
        j = 0
        while j < NF:
            groups.append(list(range(j, min(j + GSZ, NF))))
            j += GSZ
        ng = len(groups)

        def issue_gu(jj):
            s = jj % 3
            self.dma("pool", wg_sb[s].rearrange("p c f -> p (c f)"), wg_d[jj], [], [wgu_b[s]])
            self.dma("pool", wu_sb[s].rearrange("p c f -> p (c f)"), wu_d[jj], [], [wgu_b[s]])

        def issue_wd(g):
            js = groups[g]
            s = g % 2
            for gi, jj in enumerate(js):
                self.dma("pool", wd_sb[s][:, gi, :], wd_d[jj], [], [wd_b[s]])

        dma_plan = []
        for g in range(ng):
            js = groups[g]
            for jj in js[:3]:
                dma_plan.append(("gu", jj))
            dma_plan.append(("wd", g))
            for jj in js[3:]:
                dma_plan.append(("gu", jj))
        self._dma_plan = dma_plan
        self._dma_pos = 0

        plan_pos = {item: i for i, item in enumerate(dma_plan)}

        def pump(until_gu=None, until_wd=None):
            target = ("gu", until_gu) if until_gu is not None else ("wd", until_wd)
            end = plan_pos[target] if until_gu is not None or until_wd is not None else len(self._dma_plan) - 1
            while self._dma_pos <= end:
                k, v = self._dma_plan[self._dma_pos]
                if k == "gu":
                    issue_gu(v)
                else:
                    issue_wd(v)
                self._dma_pos += 1

        unit = [0]

        def emit_gu(g):
            a = g % 2
            for gi, jj in enumerate(groups[g]):
                pump(until_gu=min(jj + 2, NF - 1))
                s = jj % 3
                for h in range(2):
                    k = unit[0] % 2
                    unit[0] += 1
                    pg, pu = 2 * k, 2 * k + 1
                    for c in range(16):
                        self.mm(self.pb_ap[pg][:], wg_sb[s][:, c, :], self.hT[:, c, h * 512:(h + 1) * 512],
                                c == 0, c == 15, [wgu_b[s], self.hb[c]], [self.pb[pg]])
                    for c in range(16):
                        self.mm(self.pb_ap[pu][:], wu_sb[s][:, c, :], self.hT[:, c, h * 512:(h + 1) * 512],
                                c == 0, c == 15, [wgu_b[s], self.hb[c]], [self.pb[pu]])
                    self.act(sg_sb[k], self.pb_ap[pg][:], AF.Silu, [self.pb[pg]], [sg_b[k]])
                    self.tt(act_sb[a][:, gi, h * 512:(h + 1) * 512], sg_sb[k], self.pb_ap[pu][:], ALU.mult,
                            [sg_b[k], self.pb[pu]], [act_b[a][gi][h]])

        dunit = [0]

        def emit_d(g):
            a = g % 2
            s = g % 2
            js = groups[g]
            n = len(js)
            for c in range(16):
                for h in range(2):
                    bk = 4 + dunit[0] % 4
                    dunit[0] += 1
                    for gi in range(n):
                        self.mm(self.pb_ap[bk][:], wd_sb[s][:, gi, c * 128:(c + 1) * 128],
                                act_sb[a][:, gi, h * 512:(h + 1) * 512], gi == 0, gi == n - 1,
                                [wd_b[s], act_b[a][gi][h]], [self.pb[bk]])
                    xs = self.xT[:, c, h * 512:(h + 1) * 512]
                    self.stt(xs, self.pb_ap[bk][:], 0.5, xs, ALU.mult, ALU.add, [self.pb[bk]], [self.xb[c][h]])

        emit_gu(0)
        for g in range(ng):
            if g + 1 < ng:
                emit_gu(g + 1)
            pump(until_wd=g)
            emit_d(g)

    def ph_win(self, l):
        S = self.S
        w_d = self.din("win_%d" % l, [40, 128, 2048], F32)
        kt_out = self.dint("KT_own_%d" % l, [12, 128, 1024], BF16)
        v_out = self.dint("V_own_%d" % l, [12, 1024, 128], BF16)
        k_all = self.dint("KT_all_%d" % l, [4, 4 * 384, 1024], BF16)
        v_all = self.dint("V_all_%d" % l, [4, 4 * 3072, 128], BF16)
        self.dk_own = [Buf("dk%d" % i) for i in range(12)]
        self.dv_own = [Buf("dv%d" % i) for i in range(12)]
        self.dk_all = [Buf("dkall%d" % g) for g in range(4)]
        self.dv_all = [Buf("dvall%d" % g) for g in range(4)]
        cc_groups = [[0, 1, 2, 3], [4, 5, 6, 7]]
        bl = self.newbufs(["win%d" % i for i in range(NWR)] + ["kst0", "kst1", "vst0", "vst1", "sq0", "sq1"]
                          + ["wst%d" % i for i in range(NWS)])
        for b in self.qb:
            b.alias(self.region_bufs)
        win_b = bl[0:NWR]
        kst_b = bl[NWR:NWR + 2]
        vst_b = bl[NWR + 2:NWR + 4]
        sq_b = bl[NWR + 4:NWR + 6]
        wst_b = bl[NWR + 6:NWR + 6 + NWS]
        wst = [self.f32(O_WST + i * 8192, 2048) for i in range(NWS)]
        self.region_bufs = bl + self.qb
        win_sb = [self.bf(O_WIN + s * 4096, 2048).rearrange("p (c f) -> p c f", c=16) for s in range(NWR)]
        kst = [self.bf(O_KST + s * 2048, 1024) for s in range(2)]
        vst = [self.bf(O_VST + s * 6144, 3072).rearrange("p (t g c) -> p t g c", t=8, g=3) for s in range(2)]
        self.norm(l * 3 + 1, O_SQ_W, sq_b)
        role_idx = {}
        cnt = {"q": 0, "k": 0, "v": 0}
        for j in range(40):
            r = WIN_ROLE[j]
            role_idx[j] = (r, cnt[r])
            cnt[r] += 1
        order = (list(range(22, 26)) + [36, 37] + list(range(26, 30)) + [38, 39]
                 + list(range(6, 12)) + list(range(12, 18))
                 + list(range(18, 22)) + list(range(0, 6)) + list(range(30, 36)))
        assert sorted(order) == list(range(40))
        bank = [0]

        def nb():
            b = bank[0] % 8
            bank[0] += 1
            return b

        issued = [0]

        def pump(upto):
            while issued[0] <= min(upto, 39):
                p_ = issued[0]
                j = order[p_]
                s = p_ % NWR
                ws = p_ % NWS
                self.dma("sp", wst[ws], w_d[j], [], [wst_b[ws]])
                self.S.op("dve", (lambda e, o=win_sb[s].rearrange("p c f -> p (c f)"), i=wst[ws]:
                                  e.tensor_copy(out=o, in_=i)), reads=[wst_b[ws]], writes=[win_b[s]])
                issued[0] += 1

        kdone = set()
        vdone = set()
        nk = nv = 0
        for pos, j in enumerate(order):
            pump(pos + 4)
            s = pos % NWR
            role, idx = role_idx[j]
            if role in ("q", "k"):
                if role == "k":
                    ks = nk % 2
                    nk += 1
                for h in range(2):
                    b = nb()
                    for c in range(16):
                        self.mm(self.pb_ap[b][:], win_sb[s][:, c, :], self.hT[:, c, h * 512:(h + 1) * 512],
                                c == 0, c == 15, [win_b[s], self.hb[c]], [self.pb[b]])
                    if role == "q":
                        self.act(self.QT[:, idx, h * 512:(h + 1) * 512], self.pb_ap[b][:], AF.Copy,
                                 [self.pb[b]], [self.qb[idx]])
                    else:
                        self.act(kst[ks][:, h * 512:(h + 1) * 512], self.pb_ap[b][:], AF.Copy,
                                 [self.pb[b]], [kst_b[ks]])
                if role == "k":
                    self.dma("act", kt_out[idx], kst[ks], [kst_b[ks]], [self.dk_own[idx]])
                    kdone.add(idx)
                    g = idx // 3
                    if all((3 * g + i) in kdone for i in range(3)):
                        kin = kt_out[3 * g:3 * g + 3].rearrange("h p t -> (h p) t")
                        S.op("pool", (lambda e, kin=kin, g=g: e.collective_compute(
                            "AllGather", ALU.bypass, replica_groups=cc_groups, ins=[kin.opt()], outs=[k_all[g].opt()])),
                            reads=self.dk_own[3 * g:3 * g + 3], writes=[self.dk_all[g]], dma=True, inc=1)
            else:
                g = idx // 3
                if idx % 3 != 0:
                    continue
                vs = nv % 2
                nv += 1
                s0 = pos % NWR
                assert s0 + 2 < NWR and [role_idx[order[pos + i]] for i in range(3)] == [("v", idx + i) for i in range(3)]
                w3 = self.bf(O_WIN + s0 * 4096, 3 * 2048).rearrange("p (g c f) -> p g c f", g=3, c=16)
                for tt_ in range(8):
                    b = nb()
                    o3 = self.pb_ap[b][:, 0:384].rearrange("p (g f) -> p g f", g=3)
                    for c in range(16):
                        self.mm(o3, self.hT[:, c, tt_ * 128:(tt_ + 1) * 128], w3[:, :, c, :], c == 0, c == 15,
                                [win_b[s0], win_b[s0 + 1], win_b[s0 + 2], self.hb[c]], [self.pb[b]])
                    self.S.op("dve", (lambda e, o=vst[vs][:, tt_, :, :], i=o3: e.tensor_copy(out=o, in_=i)),
                              reads=[self.pb[b]], writes=[vst_b[vs]])
                for i in range(3):
                    self.dma("act", v_out[3 * g + i].rearrange("(t p) c -> p t c", p=128), vst[vs][:, :, i, :],
                             [vst_b[vs]], [self.dv_own[3 * g + i]])
                vin = v_out[3 * g:3 * g + 3].rearrange("h t c -> (h t) c")
                S.op("pool", (lambda e, vin=vin, g=g: e.collective_compute(
                    "AllGather", ALU.bypass, replica_groups=cc_groups, ins=[vin.opt()], outs=[v_all[g].opt()])),
                    reads=self.dv_own[3 * g:3 * g + 3], writes=[self.dv_all[g]], dma=True, inc=1)
        self.kall, self.vall = k_all, v_all

    def ph_wout(self, l):
        w_d = self.din("wout_%d" % l, [4, 128, 16 * 512], F32)
        bl = self.newbufs(["wo0", "wo1"])
        self.region_bufs = bl
        wo_sb = [self.bf(O_WO + s * 16384, 8192).rearrange("p (c d) -> p c d", c=16) for s in range(2)]
        for dg in range(2):
            self.dma("pool", wo_sb[dg].rearrange("p c d -> p (c d)"), w_d[dg], [], [bl[dg]])
        u = 0
        for dg in range(4):
            s = dg % 2
            for dc in range(4):
                c = dg * 4 + dc
                for h in range(2):
                    b = u % 8
                    u += 1
                    for hc in range(16):
                        self.mm(self.pb_ap[b][:], wo_sb[s][:, hc, dc * 128:(dc + 1) * 128],
                                self.hT[:, hc, h * 512:(h + 1) * 512], hc == 0, hc == 15,
                                [bl[s], self.hb[hc]], [self.pb[b]])
                    xs = self.xT[:, c, h * 512:(h + 1) * 512]
                    self.tt(xs, self.pb_ap[b][:], xs, ALU.add, [self.pb[b]], [self.xb[c][h]])
            if dg + 2 < 4:
                self.dma("pool", wo_sb[s].rearrange("p c d -> p (c d)"), w_d[dg + 2], [], [bl[s]])

    def ph_xchg(self, l):
        S = self.S
        k_all, v_all = self.kall, self.vall
        kpad = self.dint("Kpad_%d" % l, [8, 128, 3072], BF16)
        vpad = self.dint("Vpad_%d" % l, [8, 3072, 128], BF16)
        self.dkpad = [[Buf("dkpad") for _ in range(3)] for _ in range(3)]
        self.dvpad = [[Buf("dvpad") for _ in range(3)] for _ in range(3)]
        k_all4 = k_all.rearrange("g (r q) t -> g r q t", r=4)
        v_all4 = v_all.rearrange("g (r q) c -> g r q c", r=4)
        for k in (-1, 0, 1):
            c0 = (k + 1) * 1024
            for di, (g, hl, nh, dh) in enumerate([(0, 0, 3, 0), (1, 0, 3, 3), (3, 1, 2, 6)]):
                def kdma(e, k=k, c0=c0, g=g, hl=hl, nh=nh, dh=dh):
                    rank = self._rank_of(e, k)
                    src = k_all4[g][bass.ds(rank, 1)][0, hl * 128:(hl + nh) * 128, :]
                    dst = kpad[dh:dh + nh, :, c0:c0 + 1024].rearrange("h p t -> (h p) t")
                    return e.dma_start(out=dst, in_=src)

                def vdma(e, k=k, c0=c0, g=g, hl=hl, nh=nh, dh=dh):
                    rank = self._rank_of(e, k)
                    src = v_all4[g][bass.ds(rank, 1)][0, hl * 1024:(hl + nh) * 1024, :].rearrange("(h t) c -> h t c", h=nh)
                    return e.dma_start(out=vpad[dh:dh + nh, c0:c0 + 1024, :], in_=src)

                S.op("sp", kdma, reads=[self.dk_all[g]], writes=[self.dkpad[di][k + 1]], dma=True)
                S.op("sp", vdma, reads=[self.dv_all[g]], writes=[self.dvpad[di][k + 1]], dma=True)
        self.kpad, self.vpad = kpad, vpad

    def ph_attn(self, l):
        S = self.S
        lam_init = 0.8 - 0.6 * math.exp(-0.3 * l)
        slopes = alibi_slopes()
        sl_c, sl_a, sl_b = slopes[0:6], slopes[6:12], slopes[12:16]
        kall5 = self.kall.rearrange("g (r h p) t -> g r h p t", r=4, h=3)
        vall5 = self.vall.rearrange("g (r h t) c -> g r h t c", r=4, h=3)
        TB_d = self.din("TB", [128, 4992], F32)
        TAC_d = self.din("TAC", [128, 8320], BF16)
        lam_d = self.din("lamv_%d" % l, [256], F32)
        sub_d = self.din("subln_%d" % l, [128, 1], F32)
        sink_d = self.din("sink_%d" % l, [6], F32)
        names = ["tb", "tac", "stgk", "stgv", "sb0", "sb1", "sb2", "pt0", "pt1", "pt2", "pt3", "tmp0", "tmp1", "tmp2", "sm", "dacc0", "dacc1", "onesf",
                 "kb0", "kb1", "zk0", "zk1", "vb0", "vb1", "vb2", "vb3"]
        bl = self.newbufs(names)
        for b in self.qb:
            b.alias([x for x in self.region_bufs if x not in self.qb])
        B = dict(zip(names, bl))
        self.region_bufs = bl + self.qb
        TB = self.f32(O_TB, 4992)
        TA1 = self.bf(O_TA1, 1024)
        EA1 = self.bf(O_EA1, 1024).rearrange("p (a b) -> p a b", a=2)
        TA4 = self.bf(O_TA4, 2048).rearrange("p (a b) -> p a b", a=4)
        TA16 = self.bf(O_TA16, 2048).rearrange("p (a b) -> p a b", a=4)
        TC = self.bf(O_TC, 1152)
        EC = self.bf(O_EC, 1024).rearrange("p (a b) -> p a b", a=2)
        sbuf = [self.f32(O_SB + i * 2048, 512) for i in range(3)]
        sb_b = [B["sb0"], B["sb1"], B["sb2"]]
        ptb = [self.bf(O_PT + i * 1024, 512) for i in range(4)]
        pt_b = [B["pt0"], B["pt1"], B["pt2"], B["pt3"]]
        tmp = [self.f32(O_TMP + i * 2048, 512) for i in range(3)]
        tmp_b = [B["tmp0"], B["tmp1"], B["tmp2"]]
        sqb16 = self.bf(O_TMP + 2 * 2048, 512)
        lam_sb = self.f32(O_SM, 256)
        prod = self.f32(O_SM + 1024, 128)
        scal = self.f32(O_SM + 1536, 16)
        sink_sb = self.f32(O_SM + 1600, 6)
        esink = self.f32(O_SM + 1632, 6)
        gsub = self.f32(O_SM + 1664, 1)
        sm_b = B["sm"]
        dacc = [None, None]
        dacc_b = [B["dacc0"], B["dacc1"]]
        self.dma("sp", TB, TB_d, [], [B["tb"]])
        self.dma("sp", lam_sb, lam_d.partition_broadcast(128), [], [sm_b])
        self.dma("sp", sink_sb, sink_d.partition_broadcast(128), [], [sm_b])
        self.dma("sp", gsub, sub_d, [], [sm_b])
        self.tt(prod[:, 0:64], lam_sb[:, 0:64], lam_sb[:, 64:128], ALU.mult, [sm_b], [sm_b])
        self.tt(prod[:, 64:128], lam_sb[:, 128:192], lam_sb[:, 192:256], ALU.mult, [sm_b], [sm_b])
        S.op("dve", lambda e: e.reduce_sum(out=scal[:, 0:1], in_=prod[:, 0:64], axis=AX.X), reads=[sm_b], writes=[sm_b])
        S.op("dve", lambda e: e.reduce_sum(out=scal[:, 1:2], in_=prod[:, 64:128], axis=AX.X), reads=[sm_b], writes=[sm_b])
        self.act(scal[:, 2:4], scal[:, 0:2], AF.Exp, [sm_b], [sm_b])
        self.tt(scal[:, 4:5], scal[:, 3:4], scal[:, 2:3], ALU.subtract, [sm_b], [sm_b])
        S.op("dve", lambda e: e.tensor_scalar_add(out=scal[:, 5:6], in0=scal[:, 4:5], scalar1=-lam_init),
             reads=[sm_b], writes=[sm_b])
        S.op("dve", lambda e: e.tensor_scalar_mul(out=gsub, in0=gsub, scalar1=1.0 - lam_init), reads=[sm_b], writes=[sm_b])
        self.act(esink, sink_sb, AF.Exp, [sm_b], [sm_b])
        nlam = scal[:, 5:6]
        onesf = prod
        S.op("pool", lambda e: e.memset(onesf, 1.0), reads=[sm_b], writes=[B["onesf"]])
        onesm = self.ones

        cnt = [0]

        LAG = 3
        pending = []

        def pop_one():
            blk, (sbk, si, pi), fin = pending.pop(0)
            nk = blk["nk"]
            for pv in blk["pvs"]:
                (bank, out, lhsT, c0, c1, st_, sp_) = pv[:7]
                nkk = pv[7] if len(pv) > 7 else nk
                self.mm(out, lhsT, ptb[pi][0:nkk, c0:c1], st_, sp_, [pt_b[pi]] + blk["vreads"] + [self.const_b],
                        [self.pb[bank]], skip=True)
            if fin is not None:
                fin()

        def flush():
            while pending:
                pop_one()

        def push(blk, fin=None):
            k = cnt[0]
            cnt[0] += 1
            sbk = 4 + k % 4
            si = k % 3
            pi = k % 4
            nk, ncol = blk["nk"], blk["ncol"]
            for (c0, c1, lhsT, rhs) in blk["scores"]:
                self.mm(self.pb_ap[sbk][0:nk, c0:c1], lhsT, rhs, True, True,
                        blk["kreads"] + blk["qreads"], [self.pb[sbk]])
            self.stt(sbuf[si][0:nk, 0:ncol], blk["tab"], blk["coef"], self.pb_ap[sbk][0:nk, 0:ncol],
                     ALU.mult, ALU.add, [blk["tabb"], self.pb[sbk]], [sb_b[si]])
            self.act(ptb[pi][0:nk, 0:ncol], sbuf[si][0:nk, 0:ncol], AF.Exp, [sb_b[si]], [pt_b[pi]],
                     scale=blk["scale"])
            pending.append((blk, (sbk, si, pi), fin))
            if len(pending) > LAG:
                pop_one()

        def run_blocks(blocks, fin=None, hook=None):
            n = len(blocks)
            for i, blk in enumerate(blocks):
                push(blk, fin if i == n - 1 else None)
                if hook is not None and i == LAG - 1:
                    hook()

        stg = O_STG
        KBm = [self.bf(stg, 4096), self.bf(stg + 8192, 4096)]
        VBs = self.bf(O_TA1, 4096).rearrange("p (n c) -> p n c", n=32)
        kb_b = [B["kb0"], B["kb1"]]
        zk_b = [B["zk0"], B["zk1"]]
        vb_b = [B["vb0"], B["vb1"], B["vb2"], B["vb3"]]
        S.op("pool", lambda e: e.memset(KBm[0][64:128, :], 0.0), writes=[zk_b[0]])
        S.op("pool", lambda e: e.memset(KBm[1][0:64, :], 0.0), writes=[zk_b[1]])
        sc_b = 0.125
        for h in range(4):
            flush()
            gg, hl = (6 + h) // 3, (6 + h) % 3
            for m in range(2):
                rs = slice(64 * m, 64 * m + 64)
                self.dma("sp", KBm[m][rs, :].rearrange("p (r t) -> p r t", r=4),
                         kall5[gg, :, hl, rs, :].rearrange("r p t -> p r t"), [self.dk_all[gg]], [kb_b[m]])
            for r4 in range(4):
                self.dma("sp", VBs[:, r4 * 8:(r4 + 1) * 8, :], vall5[gg, r4, hl].rearrange("(n p) c -> p n c", p=128),
                         [self.dv_all[gg]], [vb_b[r4]])
            ch = 6 + h
            for qb in range(2):
                qs = slice(qb * 512, (qb + 1) * 512)
                blocks = []
                for kb in range(32):
                    for m in range(2):
                        u0 = qb * 512 - kb * 128 + 3968
                        blocks.append(dict(
                            nk=128, ncol=512,
                            scores=[(0, 512, KBm[m][:, kb * 128:(kb + 1) * 128], self.QT[:, ch, qs])],
                            kreads=[kb_b[m], zk_b[m]], qreads=[self.qb[ch]],
                            tab=TB[:, u0:u0 + 512], tabb=B["tb"], coef=-sl_b[h] / sc_b, scale=sc_b,
                            pvs=[(m, self.pb_ap[m][:], VBs[:, kb, :], 0, 512, kb == 0, kb == 31),
                                 (2 + m, self.pb_ap[2 + m][:], onesm, 0, 512, kb == 0, kb == 31)]
                            + [(m, self.pb_ap[m][0:120, 0:DUMMY_N], self.zeros, 0, DUMMY_N, False, False, 1)] * N_DUMMY,
                            vreads=[vb_b[kb // 8]]))
                def fin_b(ch=ch, qs=qs):
                    t0, t1, t2 = tmp
                    self.powact(t0, self.pb_ap[2][:], [self.pb[2]], [tmp_b[0]])
                    self.powact(t1, self.pb_ap[3][:], [self.pb[3]], [tmp_b[1]])
                    self.tt(t0, self.pb_ap[0][:], t0, ALU.mult, [self.pb[0]], [tmp_b[0]])
                    self.tt(t1, self.pb_ap[1][:], t1, ALU.mult, [self.pb[1]], [tmp_b[1]])
                    self.stt(t0, t1, nlam, t0, ALU.mult, ALU.add, [tmp_b[1], sm_b], [tmp_b[0]])
                    self.act(sqb16, t0, AF.Square, [tmp_b[0]], [tmp_b[2]])
                    sbk = 4 + cnt[0] % 4
                    self.mm(self.pb_ap[sbk][:], onesm, sqb16, True, True, [tmp_b[2], self.const_b], [self.pb[sbk]])
                    self.powact(t1, self.pb_ap[sbk][:], [self.pb[sbk], self.const_b], [tmp_b[1]],
                                power=-0.5, scale_in=1.0 / 128, bias=self.eps)
                    self.stt(self.hT[:, ch, qs], t0, gsub, t1, ALU.mult, ALU.mult, [tmp_b[0], tmp_b[1], sm_b], [self.hb[ch]])

                run_blocks(blocks, fin_b)
        flush()

        B["stgk"].alias(kb_b + zk_b)
        B["stgv"].alias(kb_b + zk_b)
        B["tac"].alias(vb_b)
        self.ph_xchg(l)
        kpad, vpad = self.kpad, self.vpad
        self.dma("sp", self.bf(O_TA1, 8320), TAC_d, [], [B["tac"]])
        stg_off = [stg, O_TB]
        A_KEYS = ["k", "v1", "v4_0", "v4_1", "v4_2", "v4_3", "v16a", "v16b"]
        stgA_b = [{kk: Buf("a0" + kk).alias([B["stgk"], B["stgv"]]) for kk in A_KEYS},
                  {kk: Buf("a1" + kk).alias([B["tb"]]) for kk in A_KEYS}]
        sc_a = 128 ** -0.5

        def a_views(sl):
            o = stg_off[sl]
            return (self.bf(o, 2560),
                    self.bf(o + 5120, 640).rearrange("p (n c) -> p n c", n=5),
                    self.bf(o + 6400, 1024).rearrange("p (r n c) -> p r n c", r=4, n=2),
                    self.bf(o + 8448, 2048).rearrange("p (r c) -> p r c", r=16),
                    self.bf(o + 12544, 2048).rearrange("p (r c) -> p r c", r=16))

        def a_loads(h, qb, sl):
            KAw, VA1s, VA4s, VA16a, VA16b = a_views(sl)
            bb = stgA_b[sl]
            q0 = qb * 512
            di = h // 3
            self.dma("sp", KAw, kpad[h][:, q0:q0 + 2560], self.dkpad[di], [bb["k"]])
            self.dma("sp", VA1s, vpad[h][960 + q0:960 + q0 + 640].rearrange("(n p) c -> p n c", p=128),
                     self.dvpad[di], [bb["v1"]])
            for r in range(4):
                st4 = 768 + r + q0
                self.dma("sp", VA4s[:, r, :, :],
                         vpad[h][st4:st4 + 4 * 255 + 1:4].rearrange("(n p) c -> p n c", p=128),
                         self.dvpad[di], [bb["v4_%d" % r]])
            self.dma("sp", VA16a, vpad[h][q0:q0 + 2048].rearrange("(p r) c -> p r c", r=16), self.dvpad[di], [bb["v16a"]])
            self.dma("sp", VA16b[0:32], vpad[h][q0 + 2048:q0 + 2560].rearrange("(p r) c -> p r c", r=16),
                     self.dvpad[di], [bb["v16b"]])

        def a_compute(h, qb, sl, ab, hook=None):
            bn, bd = 2 * ab, 2 * ab + 1
            KAw, VA1s, VA4s, VA16a, VA16b = a_views(sl)
            bb = stgA_b[sl]
            q0 = qb * 512
            coef = -sl_a[h] / sc_a
            qcol = self.QT[:, h, q0:q0 + 512]
            common = dict(kreads=[bb["k"]], qreads=[self.qb[h]], tabb=B["tac"], coef=coef, scale=sc_a)
            blocks = []
            for n in range(5):
                if qb == 0 and n == 0:
                    tab = EA1[:, 0, :]
                elif qb == 1 and n == 4:
                    tab = EA1[:, 1, :]
                else:
                    u0 = 512 - 128 * n
                    tab = TA1[:, u0:u0 + 512]
                kc = 960 + 128 * n
                blocks.append(dict(
                    nk=128, ncol=512, scores=[(0, 512, KAw[:, kc:kc + 128], qcol)], tab=tab,
                    pvs=[(bn, self.pb_ap[bn][:], VA1s[:, n, :], 0, 512, n == 0, False),
                         (bd, self.pb_ap[bd][:], onesm, 0, 512, n == 0, False)], vreads=[bb["v1"]], **common))
            for n in range(2):
                scores = []
                pvs = []
                for r in range(4):
                    s_r = 768 + 512 * n + r
                    scores.append((r * 128, (r + 1) * 128, KAw[:, s_r:s_r + 4 * 127 + 1:4], self.QT[:, h, q0 + r:q0 + 512:4]))
                    pvs.append((bn, self.pb_ap[bn][:, r:512:4], VA4s[:, r, n, :], r * 128, (r + 1) * 128, False, False))
                    pvs.append((bd, self.pb_ap[bd][:, r:512:4], onesm, r * 128, (r + 1) * 128, False, False))
                blocks.append(dict(nk=128, ncol=512, scores=scores, tab=TA4[:, 2 * qb + n, :], pvs=pvs,
                                   vreads=[bb["v4_%d" % r] for r in range(4)], **common))
            for n in range(2):
                nk = 128 if n == 0 else 32
                scores = []
                pvs = []
                for r in range(16):
                    s_r = r if n == 0 else 2048 + r
                    scores.append((r * 32, (r + 1) * 32, KAw[:, s_r:s_r + 16 * (nk - 1) + 1:16], self.QT[:, h, q0 + r:q0 + 512:16]))
                    vt = VA16a[:, r, :] if n == 0 else VA16b[0:32, r, :]
                    last = (n == 1 and r == 15)
                    pvs.append((bn, self.pb_ap[bn][:, r:512:16], vt, r * 32, (r + 1) * 32, False, last))
                    pvs.append((bd, self.pb_ap[bd][:, r:512:16], onesm[0:nk, :], r * 32, (r + 1) * 32, False, last))
                blocks.append(dict(nk=nk, ncol=512, scores=scores, tab=TA16[0:nk, 2 * qb + n, :], pvs=pvs,
                                   vreads=[bb["v16a"] if n == 0 else bb["v16b"]], **common))
            def fin_a(h=h, q0=q0, ab=ab, bn=bn, bd=bd):
                t0 = tmp[ab]
                self.powact(t0, self.pb_ap[bd][:], [self.pb[bd]], [tmp_b[ab]])
                self.tt(self.hT[:, h, q0:q0 + 512], self.pb_ap[bn][:], t0, ALU.mult, [self.pb[bn], tmp_b[ab]], [self.hb[h]])

            run_blocks(blocks, fin_a, hook)

        unitsA = [(h, qb) for h in range(6) for qb in range(2)]
        a_loads(unitsA[0][0], unitsA[0][1], 0)
        for ui, (h, qb) in enumerate(unitsA):
            hk = None
            if ui + 1 < len(unitsA):
                hk = (lambda u=ui + 1: a_loads(unitsA[u][0], unitsA[u][1], u % 2))
            a_compute(h, qb, ui % 2, ui % 2, hk)
        flush()

        B["stgk"].alias(list(stgA_b[0].values()))
        B["stgv"].alias(list(stgA_b[0].values()))
        KCw = self.bf(stg, 1280)
        VCs = self.bf(stg + 2560, 1280).rearrange("p (n c) -> p n c", n=10)
        sc_c = 128 ** -0.5
        for g in range(2):
            flush()
            self.dma("sp", KCw, kpad[6 + g][:, 896:2176], self.dkpad[2], [B["stgk"]])
            self.dma("sp", VCs, vpad[6 + g][896:2176].rearrange("(n p) c -> p n c", p=128), self.dvpad[2], [B["stgv"]])
            for hh in range(3):
                hq = g * 3 + hh
                ch = 10 + hq
                for qb in range(2):
                    q0 = qb * 512
                    ab = (hq * 2 + qb) % 2
                    bn, bd = 2 * ab, 2 * ab + 1
                    blocks = []
                    for n in range(6):
                        if qb == 0 and n == 0:
                            tab = EC[:, 0, :]
                        elif qb == 1 and n == 5:
                            tab = EC[:, 1, :]
                        else:
                            u0 = 640 - 128 * n
                            tab = TC[:, u0:u0 + 512]
                        kt = 4 * qb + n
                        blocks.append(dict(
                            nk=128, ncol=512,
                            scores=[(0, 512, KCw[:, kt * 128:(kt + 1) * 128], self.QT[:, ch, q0:q0 + 512])],
                            kreads=[B["stgk"]], qreads=[self.qb[ch]], tab=tab, tabb=B["tac"],
                            coef=-sl_c[hq] / sc_c, scale=sc_c,
                            pvs=[(bn, self.pb_ap[bn][:], VCs[:, kt, :], 0, 512, n == 0, n == 5),
                                 (bd, self.pb_ap[bd][:], onesm, 0, 512, n == 0, n == 5)],
                            vreads=[B["stgv"]]))
                    def fin_c(ch=ch, q0=q0, ab=ab, bn=bn, bd=bd, hq=hq):
                        t0 = tmp[ab]
                        self.powact(t0, self.pb_ap[bd][:], [self.pb[bd], sm_b], [tmp_b[ab]], bias=esink[:, hq:hq + 1])
                        self.tt(self.hT[:, ch, q0:q0 + 512], self.pb_ap[bn][:], t0, ALU.mult,
                                [self.pb[bn], tmp_b[ab]], [self.hb[ch]])

                    run_blocks(blocks, fin_c)
        flush()


def _lay_gu(w):
    F = w.shape[1]
    return np.ascontiguousarray(w.reshape(16, 128, F // 128, 128).transpose(2, 1, 0, 3)).reshape(F // 128, 128, 2048)


def _lay_wout(w):
    return np.ascontiguousarray(w.reshape(16, 128, 4, 512).transpose(2, 1, 0, 3)).reshape(4, 128, 16 * 512)


def _gains(inputs):
    g = np.zeros((7, 2048), np.float32)
    for l in range(DEPTH):
        g[l * 3 + 0] = inputs["ffn1_norm"][l]
        g[l * 3 + 1] = inputs["mix_norm"][l]
        g[l * 3 + 2] = inputs["ffn2_norm"][l]
    g[6] = inputs["final_norm"]
    return np.ascontiguousarray(g.reshape(7, 16, 128).transpose(2, 0, 1)).reshape(128, 7 * 16)


_PROG_CACHE = {}


def get_prog(phases):
    key = repr(phases)
    if key not in _PROG_CACHE:
        _PROG_CACHE[key] = Prog(phases)
    return _PROG_CACHE[key]


def _tables(cc):
    i = np.arange(128, dtype=np.int64)[:, None]
    u = np.arange(4992, dtype=np.int64)[None, :]
    TB = np.abs(cc * 1024 + u - 3968 - i).astype(np.float32)

    def band(delta, radius, mult, ok):
        d = np.abs(delta)
        return np.where((d <= radius) & ok, (mult * d).astype(np.float32), np.float32(BIG))

    j = np.arange(512, dtype=np.int64)[None, :]
    u1 = np.arange(1024, dtype=np.int64)[None, :]
    TA1 = band(u1 - 448 - i, 64, 1, True)
    k = -64 + i
    EA1_0 = band(j + 64 - i, 64, 1, (cc * 1024 + k >= 0))
    k = 960 + i
    EA1_1 = band(j - 448 - i, 64, 1, (cc * 1024 + k < SEQ))
    TA4 = []
    for qb in range(2):
        for n in range(2):
            jp = (np.arange(512, dtype=np.int64) % 128)[None, :]
            kg = cc * 256 + qb * 128 - 64 + 128 * n + i
            TA4.append(band(jp + 64 - 128 * n - i, 64, 4, (kg >= 0) & (kg < 1024)))
    TA16 = []
    for qb in range(2):
        for n in range(2):
            jp = (np.arange(512, dtype=np.int64) % 32)[None, :]
            if n == 0:
                delta = jp + 64 - i
                kp = qb * 32 - 64 + i
                ok = np.ones_like(i, dtype=bool)
            else:
                delta = jp - 64 - i
                kp = qb * 32 + 64 + i
                ok = i < 32
            kg = cc * 64 + kp
            TA16.append(band(delta, 64, 16, ok & (kg >= 0) & (kg < 256)))
    uc = np.arange(1152, dtype=np.int64)[None, :]
    TC = band(uc - 512 - i, 128, 1, True)
    k = -128 + i
    EC_0 = band(j + 128 - i, 128, 1, (cc * 1024 + k >= 0))
    k = 1024 + i
    EC_1 = band(j - 512 - i, 128, 1, (cc * 1024 + k < SEQ))
    TAC = np.concatenate([TA1, EA1_0, EA1_1] + TA4 + TA16 + [TC, EC_0, EC_1], axis=1)
    assert TAC.shape == (128, 8320)
    return TB, np.ascontiguousarray(TAC.astype(NPBF))


def _attn_small(inputs, l):
    lamv = np.concatenate([inputs["diff_lambda_q1"][l], inputs["diff_lambda_k1"][l],
                           inputs["diff_lambda_q2"][l], inputs["diff_lambda_k2"][l]]).astype(np.float32)
    return {"lamv_%d" % l: lamv,
            "subln_%d" % l: np.ascontiguousarray(inputs["diff_subln"][l].reshape(128, 1)).astype(np.float32),
            "sink_%d" % l: np.ascontiguousarray(inputs["swa_sink"][l]).astype(np.float32)}


def _run(phases, shared, percore):
    prog = get_prog(phases)
    in_maps = []
    for c in range(NCORES):
        m = dict(shared)
        m.update(percore[c])
        in_maps.append(m)
    res = run_bass_kernel_spmd(prog.nc, in_maps, core_ids=list(range(NCORES)))
    return res.results


PH_FUSED = [("load_x",),
            ("ffn", 0, 1), ("win", 0), ("attn", 0), ("wout", 0), ("ffn", 0, 2),
            ("ffn", 1, 1), ("win", 1), ("attn", 1), ("wout", 1), ("ffn", 1, 2),
            ("final",)]


def kernel(x, ffn1_norm, ffn1_w_gate, ffn1_w_up, ffn1_w_down, mix_norm, w_in, w_out,
           diff_lambda_q1, diff_lambda_k1, diff_lambda_q2, diff_lambda_k2, diff_subln,
           swa_sink, ffn2_norm, ffn2_w_gate, ffn2_w_up, ffn2_w_down, final_norm):
    inputs = dict(x=x, ffn1_norm=ffn1_norm, ffn1_w_gate=ffn1_w_gate, ffn1_w_up=ffn1_w_up,
                  ffn1_w_down=ffn1_w_down, mix_norm=mix_norm, w_in=w_in, w_out=w_out,
                  diff_lambda_q1=diff_lambda_q1, diff_lambda_k1=diff_lambda_k1,
                  diff_lambda_q2=diff_lambda_q2, diff_lambda_k2=diff_lambda_k2, diff_subln=diff_subln,
                  swa_sink=swa_sink, ffn2_norm=ffn2_norm, ffn2_w_gate=ffn2_w_gate, ffn2_w_up=ffn2_w_up,
                  ffn2_w_down=ffn2_w_down, final_norm=final_norm)
    inputs = {k: np.asarray(v) for k, v in inputs.items()}
    x = inputs["x"]
    shared = {"gains": _gains(inputs)}
    ffn_w = {1: (inputs["ffn1_w_gate"], inputs["ffn1_w_up"], inputs["ffn1_w_down"]),
             2: (inputs["ffn2_w_gate"], inputs["ffn2_w_up"], inputs["ffn2_w_down"])}
    for l in range(DEPTH):
        for which in (1, 2):
            wg, wu, wd = ffn_w[which]
            tag = "%d_%d" % (l, which)
            shared["wg_" + tag] = _lay_gu(wg[l])
            shared["wu_" + tag] = _lay_gu(wu[l])
            shared["wd_" + tag] = np.ascontiguousarray(wd[l]).reshape(NF, 128, 2048)
        shared["win_%d" % l] = _lay_gu(inputs["w_in"][l])
        shared["wout_%d" % l] = _lay_wout(inputs["w_out"][l])
        shared.update(_attn_small(inputs, l))
    tabs = [_tables(cc) for cc in range(4)]
    pc = []
    for c in range(NCORES):
        b, cc = c // 4, c % 4
        pc.append({"xT_in": np.ascontiguousarray(x[b, cc * 1024:(cc + 1) * 1024, :].T).reshape(16, 128, 1024),
                   "TB": tabs[cc][0], "TAC": tabs[cc][1]})
    res = _run(PH_FUSED, shared, pc)
    out = np.empty((BATCH, SEQ, D_MODEL), np.float32)
    for c in range(NCORES):
        b, cc = c // 4, c % 4
        out[b, cc * 1024:(cc + 1) * 1024, :] = res[c]["outT"].reshape(2048, 1024).T
    return out
```

```python
import contextlib
import math
import numpy as np
import ml_dtypes
import concourse.bass as bass
import concourse.mybir as mybir
from concourse.bass_utils import run_bass_kernel_spmd

F32 = mybir.dt.float32
BF16 = mybir.dt.bfloat16
ALU = mybir.AluOpType
AF = mybir.ActivationFunctionType
AX = mybir.AxisListType
NPBF = ml_dtypes.bfloat16

D_MODEL = 2048
BATCH = 2
SEQ = 4096
DEPTH = 2
D_FF = 5504
NF = D_FF // 128
NC_ = 16
TL = 1024
NCORES = 8
RMS_EPS = 1e-6
BIG = 30000.0
GSZ = 5
N_DUMMY = 0
DUMMY_N = 128

ENGS = ("pe", "act", "dve", "pool", "sp")


class Buf:
    __slots__ = ("w", "wd", "r", "rd", "name")

    def __init__(self, name=""):
        self.w = {}
        self.wd = []
        self.r = {}
        self.rd = []
        self.name = name

    def alias(self, olds):
        for b in olds:
            for src in (b.w, b.r):
                for e, o in src.items():
                    cur = self.r.get(e)
                    if cur is None or cur.idx < o.idx:
                        self.r[e] = o
            self.rd.extend(b.wd)
            self.rd.extend(b.rd)
        return self


class Op:
    __slots__ = ("eng", "fn", "deps", "needed", "dma", "sem", "semval", "prev_semval", "idx", "inc")

    def __init__(self, eng, fn, dma, idx):
        self.eng = eng
        self.fn = fn
        self.dma = dma
        self.idx = idx
        self.deps = []
        self.needed = False
        self.sem = None
        self.semval = 0
        self.prev_semval = 0


class Sched:
    def __init__(self, n_dma_sems=8):
        self.ops = {e: [] for e in ENGS}
        self.n_dma_sems = n_dma_sems
        self.nops = 0

    def op(self, eng, fn, reads=(), writes=(), dma=False, inc=16):
        o = Op(eng, fn, dma, self.nops)
        o.inc = inc
        deps = {}

        def add(d):
            if (not d.dma) and (not dma) and d.eng == "pe" and eng == "pe":
                return
            deps[id(d)] = d

        for b in reads:
            for d in b.w.values():
                add(d)
            for d in b.wd:
                add(d)
        for b in writes:
            for d in b.w.values():
                add(d)
            for d in b.wd:
                add(d)
            for d in b.r.values():
                add(d)
            for d in b.rd:
                add(d)
        for b in reads:
            if dma:
                b.rd.append(o)
            else:
                b.r[eng] = o
        for b in writes:
            b.r = {}
            b.rd = []
            if dma:
                b.w = {}
                b.wd = [o]
            else:
                b.w = {eng: o}
                b.wd = []
        o.deps = list(deps.values())
        for d in o.deps:
            d.needed = True
        self.ops[eng].append(o)
        self.nops += 1
        return o

    def emit(self, nc, stack):
        esem = {e: stack.enter_context(nc.semaphore("S_" + e)) for e in ENGS}
        dsem = {}
        for e in ENGS:
            for inc in sorted(set(o.inc for o in self.ops[e] if o.dma)):
                dsem[(e, inc)] = [stack.enter_context(nc.semaphore("D_%s%d_%d" % (e, inc, i)))
                                  for i in range(self.n_dma_sems if inc == 16 else 2)]
        for e in ENGS:
            c = 0
            kk = {}
            cn = {}
            for o in self.ops[e]:
                if o.dma:
                    pool = dsem[(e, o.inc)]
                    k = kk.get(o.inc, 0)
                    cnts = cn.setdefault(o.inc, [0] * len(pool))
                    i = k % len(pool)
                    o.sem = pool[i]
                    o.prev_semval = cnts[i]
                    cnts[i] += o.inc
                    o.semval = cnts[i]
                    kk[o.inc] = k + 1
                elif o.needed:
                    c += 1
                    o.sem = esem[e]
                    o.semval = c
        block = stack.enter_context(nc.Block())
        handles = {"pe": block.tensor, "act": block.scalar, "dve": block.vector,
                   "pool": block.gpsimd, "sp": block.sync}
        stats = {}
        for e in ENGS:
            ops = self.ops[e]
            if not ops:
                continue
            nwait = [0]

            def body(eng, ops=ops, nwait=nwait):
                known = {}
                for o in ops:
                    waits = {}
                    for d in o.deps:
                        key = id(d.sem)
                        if known.get(key, 0) >= d.semval:
                            continue
                        if key not in waits or waits[key][1] < d.semval:
                            waits[key] = (d.sem, d.semval)
                    if o.dma and o.prev_semval > 0:
                        key = id(o.sem)
                        if known.get(key, 0) < o.prev_semval:
                            if key not in waits or waits[key][1] < o.prev_semval:
                                waits[key] = (o.sem, o.prev_semval)
                    for key, (s, v) in waits.items():
                        eng.wait_ge(s, v)
                        known[key] = v
                        nwait[0] += 1
                    if o.fn is None:
                        continue
                    inst = o.fn(eng)
                    if o.dma:
                        inst.then_inc(o.sem, o.inc)
                    elif o.needed:
                        inst.then_inc(o.sem, 1)

            handles[e](body)
            stats[e] = (len(ops), nwait[0])
        return stats


O_XT = 0
O_RSTD = 65536
O_GAIN = O_RSTD + 4096
O_CONST = O_GAIN + 512
RB = O_CONST + 512
O_H = RB
O_Q = RB + 32768
O_Z = RB + 65536
O_WGU = O_Q
O_SG = O_Q + 24576
O_SQ_F = O_Q + 28672
O_WD = O_Z
O_ACT = O_Z + 2 * GSZ * 4096
NWR = 6
NWS = 3
O_WIN = O_Z
O_WST = O_Z + NWR * 4096
O_KST = O_WST + NWS * 8192
O_VST = O_KST + 4096
O_SQ_W = O_VST + 12288
O_TB = O_Z
O_TA1 = O_TB + 19968
O_EA1 = O_TA1 + 2048
O_TA4 = O_EA1 + 2048
O_TA16 = O_TA4 + 4096
O_TC = O_TA16 + 4096
O_EC = O_TC + 2304
O_STG = O_EC + 2048
O_SB = O_STG + 16640
O_PT = O_SB + 6144
O_TMP = O_PT + 4096
O_SM = O_TMP + 6144
O_DACC = O_SM + 2048
ATT_END = O_DACC
O_WO = O_Z
ARENA_BYTES = max(ATT_END, O_ACT + 2 * GSZ * 2048, O_WO + 32768)

WIN_ROLE = (["q"] * 6 + ["k"] * 6 + ["v"] * 6 + ["q"] * 4 + ["k"] * 4 + ["v"] * 4
            + ["q"] * 6 + ["k"] * 2 + ["v"] * 2)


def alibi_slopes():
    return [2.0 ** (-8.0 * (i + 1) / 16) for i in range(16)]


class Prog:
    def __init__(self, phases):
        self.phases = phases
        self.nc = nc = bass.Bass("TRN2", target_bir_lowering=False)
        self.S = Sched()
        self.dram = {}
        self.stack = contextlib.ExitStack()
        st = self.stack
        self.arena = st.enter_context(nc.sbuf_tensor("arena", [128, ARENA_BYTES // 2], BF16))
        self.pb_ap = [st.enter_context(nc.psum_tensor("pb%d" % i, [128, 512], F32)) for i in range(8)]
        self.pb = [Buf("pb%d" % i) for i in range(8)]
        self.xT = self.f32(O_XT, 16 * 1024).rearrange("p (c t) -> p c t", c=16)
        self.xb = [[Buf("x%d_%d" % (c, h)) for h in range(2)] for c in range(16)]
        self.rstd = self.f32(O_RSTD, 1024)
        self.rstd_b = Buf("rstd")
        self.gain = self.f32(O_GAIN, 7 * 16).rearrange("p (v c) -> p v c", v=7)
        self.gain_b = Buf("gain")
        self.ones = self.bf(O_CONST, 128)
        self.eps = self.f32(O_CONST + 256, 1)
        self.zeros = self.arena[0:1, (O_CONST + 264) // 2:(O_CONST + 264) // 2 + 120]
        self.const_b = Buf("const")
        self.hT = self.bf(O_H, 16 * 1024).rearrange("p (c t) -> p c t", c=16)
        self.hb = [Buf("h%d" % c) for c in range(16)]
        self.QT = self.bf(O_Q, 16 * 1024).rearrange("p (c t) -> p c t", c=16)
        self.qb = [Buf("q%d" % c) for c in range(16)]
        self.region_bufs = []
        self.out_bufs = []
        self.build()

    def bf(self, off, n):
        return self.arena[:, off // 2: off // 2 + n]

    def f32(self, off, n):
        return self.arena[:, off // 2: off // 2 + 2 * n].bitcast(F32)

    def din(self, name, shape, dt):
        if name not in self.dram:
            self.dram[name] = self.nc.dram_tensor(name, list(shape), dt, kind="ExternalInput").ap()
        return self.dram[name]

    def dout(self, name, shape, dt):
        if name not in self.dram:
            self.dram[name] = self.nc.dram_tensor(name, list(shape), dt, kind="ExternalOutput").ap()
        return self.dram[name]

    def _rank_of(self, e, k):
        if not hasattr(self, "_rk"):
            self._rk = {}
            cc = e.partition_id() % 4
            for kk in (-1, 0, 1):
                self._rk[kk] = e.snap((cc + (kk + 4)) % 4)
        return self._rk[k]

    def dint(self, name, shape, dt):
        if name not in self.dram:
            self.dram[name] = self.nc.dram_tensor(name, list(shape), dt).ap()
        return self.dram[name]

    def newbufs(self, names):
        bs = [Buf(n).alias(self.region_bufs) for n in names]
        return bs

    def mm(self, out, lhsT, rhs, start, stop, reads, writes, skip=False):
        kw = {"skip_group_check": True} if skip else {}
        self.S.op("pe", lambda e: e.matmul(out, lhsT=lhsT, rhs=rhs, start=start, stop=stop, **kw),
                  reads=reads, writes=writes)

    def dma(self, eng, out, in_, reads, writes):
        return self.S.op(eng, lambda e: e.dma_start(out=out, in_=in_), reads=reads, writes=writes, dma=True)

    def act(self, out, in_, func, reads, writes, scale=None, bias=None):
        kw = {}
        if scale is not None:
            kw["scale"] = scale
        if bias is not None:
            kw["bias"] = bias
        self.S.op("act", lambda e: e.activation(out=out, in_=in_, func=func, **kw), reads=reads, writes=writes)

    def stt(self, out, in0, scalar, in1, op0, op1, reads, writes, eng="dve"):
        self.S.op(eng, lambda e: e.scalar_tensor_tensor(out=out, in0=in0, scalar=scalar, in1=in1, op0=op0, op1=op1),
                  reads=reads, writes=writes)

    def tt(self, out, in0, in1, op, reads, writes, eng="dve"):
        self.S.op(eng, lambda e: e.tensor_tensor(out=out, in0=in0, in1=in1, op=op), reads=reads, writes=writes)

    def powact(self, out, in_, reads, writes, power=-1.0, scale_in=None, bias=None):
        self.act(out, in_, AF.Ln, reads, writes, scale=scale_in, bias=bias)
        self.act(out, out, AF.Exp, [], writes, scale=power)

    def recip(self, out, in_, reads, writes):
        self.S.op("dve", lambda e: e.reciprocal(out=out, in_=in_), reads=reads, writes=writes)

    def build(self):
        S = self.S
        S.op("pool", lambda e: e.memset(self.ones, 1.0), writes=[self.const_b])
        S.op("pool", lambda e: e.memset(self.eps, RMS_EPS), writes=[self.const_b])
        S.op("pool", lambda e: e.memset(self.zeros, 0.0), writes=[self.const_b])
        g_in = self.din("gains", [128, 7 * 16], F32)
        self.dma("sp", self.f32(O_GAIN, 7 * 16), g_in, [], [self.gain_b])
        for ph in self.phases:
            kind = ph[0]
            if kind == "load_x":
                self.ph_load_x()
            elif kind == "load_state":
                self.ph_load_state()
            elif kind == "ffn":
                self.ph_ffn(ph[1], ph[2])
            elif kind == "win":
                self.ph_win(ph[1])
            elif kind == "store_state":
                self.ph_store_state()
            elif kind == "attn":
                self.ph_attn(ph[1])
            elif kind == "xchg":
                self.ph_xchg(ph[1])
            elif kind == "wout":
                self.ph_wout(ph[1])
            elif kind == "final":
                self.ph_final()
            elif kind == "dump_h":
                h_out = self.dout("hT_out", [16, 128, 1024], BF16)
                for c in range(16):
                    ob = Buf("ho")
                    self.dma("sp", h_out[c], self.hT[:, c, :], [self.hb[c]], [ob])
                    self.out_bufs.append(ob)
            else:
                raise ValueError(kind)
        S.op("sp", None, reads=self.out_bufs)
        self.stats = S.emit(self.nc, self.stack)

    def ph_load_x(self):
        x_in = self.din("xT_in", [16, 128, 1024], F32)
        for c in range(16):
            self.dma("sp", self.xT[:, c, :], x_in[c], [], [self.xb[c][0], self.xb[c][1]])

    def ph_load_state(self):
        self.ph_load_x()
        q_in = self.din("QT_in", [16, 128, 1024], BF16)
        for b in self.qb:
            b.alias(self.region_bufs)
        self.region_bufs = list(self.qb)
        for c in range(16):
            self.dma("sp", self.QT[:, c, :], q_in[c], [], [self.qb[c]])

    def ph_store_state(self):
        x_out = self.dout("xT_out", [16, 128, 1024], F32)
        q_out = self.dout("QT_out", [16, 128, 1024], BF16)
        for c in range(16):
            ob = Buf("xo")
            self.dma("sp", x_out[c], self.xT[:, c, :], [self.xb[c][0], self.xb[c][1]], [ob])
            self.out_bufs.append(ob)
        for c in range(16):
            ob = Buf("qo")
            self.dma("sp", q_out[c], self.QT[:, c, :], [self.qb[c]], [ob])
            self.out_bufs.append(ob)

    def norm(self, vidx, sq_off, sq_bufs, final_out=None):
        sq = [self.bf(sq_off + i * 2048, 1024) for i in range(2)]
        P6, P7 = self.pb[6], self.pb[7]
        for c in range(16):
            i = c % 2
            self.act(sq[i], self.xT[:, c, :], AF.Square, [self.xb[c][0], self.xb[c][1]], [sq_bufs[i]])
            for h in range(2):
                self.mm(self.pb_ap[6 + h][:], self.ones, sq[i][:, h * 512:(h + 1) * 512], c == 0, c == 15,
                        [sq_bufs[i], self.const_b], [self.pb[6 + h]])
        for h in range(2):
            r = self.rstd[:, h * 512:(h + 1) * 512]
            self.powact(r, self.pb_ap[6 + h][:], [self.pb[6 + h], self.const_b], [self.rstd_b],
                        power=-0.5, scale_in=1.0 / D_MODEL, bias=self.eps)
        for c in range(16):
            if final_out is None:
                self.stt(self.hT[:, c, :], self.xT[:, c, :], self.gain[:, vidx, c:c + 1], self.rstd,
                         ALU.mult, ALU.mult, [self.xb[c][0], self.xb[c][1], self.gain_b, self.rstd_b], [self.hb[c]])
            else:
                self.stt(self.xT[:, c, :], self.xT[:, c, :], self.gain[:, vidx, c:c + 1], self.rstd,
                         ALU.mult, ALU.mult, [self.gain_b, self.rstd_b], [self.xb[c][0], self.xb[c][1]])
                ob = Buf("fo")
                self.dma("sp", final_out[c], self.xT[:, c, :], [self.xb[c][0], self.xb[c][1]], [ob])
                self.out_bufs.append(ob)

    def ph_final(self):
        out = self.dout("outT", [16, 128, 1024], F32)
        sqb = self.newbufs(["sq0", "sq1"])
        self.region_bufs = sqb
        self.norm(6, O_SQ_F, sqb, final_out=out)

    def ph_ffn(self, l, which):
        S = self.S
        tag = "%d_%d" % (l, which)
        wg_d = self.din("wg_" + tag, [NF, 128, 2048], F32)
        wu_d = self.din("wu_" + tag, [NF, 128, 2048], F32)
        wd_d = self.din("wd_" + tag, [NF, 128, 2048], F32)
        names = (["wgu%d" % i for i in range(3)] + ["wd%d" % i for i in range(2)] + ["sg0", "sg1", "sq0", "sq1"]
                 + ["act%d_%d_%d" % (a, j, h) for a in range(2) for j in range(GSZ) for h in range(2)]
                 + ["wu%d" % i for i in range(3)] + ["wdc%d_%d" % (a, j) for a in range(2) for j in range(GSZ)])
        bl = self.newbufs(names)
        wgu_b = bl[0:3]
        wd_b = bl[3:5]
        nb0 = 9 + 2 * GSZ * 2
        wu_b = bl[nb0:nb0 + 3]
        wdc_b = [[bl[nb0 + 3 + a * GSZ + j] for j in range(GSZ)] for a in range(2)]
        sg_b = bl[5:7]
        sq_b = bl[7:9]
        act_b = [[[bl[9 + (a * GSZ + j) * 2 + h] for h in range(2)] for j in range(GSZ)] for a in range(2)]
        self.region_bufs = bl
        wg_sb = [self.bf(O_WGU + s * 8192, 2048).rearrange("p (c f) -> p c f", c=16) for s in range(3)]
        wu_sb = [self.bf(O_WGU + s * 8192 + 4096, 2048).rearrange("p (c f) -> p c f", c=16) for s in range(3)]
        wd_sb = [self.bf(O_WD + s * GSZ * 4096, GSZ * 2048).rearrange("p (g d) -> p g d", g=GSZ) for s in range(2)]
        act_sb = [self.bf(O_ACT + s * GSZ * 2048, GSZ * 1024).rearrange("p (g t) -> p g t", g=GSZ) for s in range(2)]
        sg_sb = [self.f32(O_SG + s * 2048, 512) for s in range(2)]
        vidx = l * 3 + (0 if which == 1 else 2)
        self.norm(vidx, O_SQ_F, sq_b)

        groups = []
        j = 0
        while j < NF:
            groups.append(list(range(j, min(j + GSZ, NF))))
            j += GSZ
        ng = len(groups)

        def issue_gu(jj):
            s = jj % 3
            self.dma("pool", wg_sb[s].rearrange("p c f -> p (c f)"), wg_d[jj], [], [wgu_b[s]])
            self.dma("pool", wu_sb[s].rearrange("p c f -> p (c f)"), wu_d[jj], [], [wu_b[s]])

        def issue_wd(g):
            js = groups[g]
            s = g % 2
            for gi, jj in enumerate(js):
                self.dma("pool", wd_sb[s][:, gi, :], wd_d[jj], [], [wdc_b[s][gi]])

        dma_plan = []
        for g in range(ng):
            js = groups[g]
            for jj in js[:3]:
                dma_plan.append(("gu", jj))
            dma_plan.append(("wd", g))
            for jj in js[3:]:
                dma_plan.append(("gu", jj))
        self._dma_plan = dma_plan
        self._dma_pos = 0

        plan_pos = {item: i for i, item in enumerate(dma_plan)}

        def pump(until_gu=None, until_wd=None):
            target = ("gu", until_gu) if until_gu is not None else ("wd", until_wd)
            end = plan_pos[target] if until_gu is not None or until_wd is not None else len(self._dma_plan) - 1
            while self._dma_pos <= end:
                k, v = self._dma_plan[self._dma_pos]
                if k == "gu":
                    issue_gu(v)
                else:
                    issue_wd(v)
                self._dma_pos += 1

        unit = [0]

        def emit_gu(g):
            a = g % 2
            for gi, jj in enumerate(groups[g]):
                pump(until_gu=min(jj + 2, NF - 1))
                s = jj % 3
                for h in range(2):
                    k = unit[0] % 2
                    unit[0] += 1
                    pg, pu = 2 * k, 2 * k + 1
                    for c in range(16):
                        self.mm(self.pb_ap[pg][:], wg_sb[s][:, c, :], self.hT[:, c, h * 512:(h + 1) * 512],
                                c == 0, c == 15, [wgu_b[s], self.hb[c]], [self.pb[pg]])
                    for c in range(16):
                        self.mm(self.pb_ap[pu][:], wu_sb[s][:, c, :], self.hT[:, c, h * 512:(h + 1) * 512],
                                c == 0, c == 15, [wu_b[s], self.hb[c]], [self.pb[pu]])
                    self.act(sg_sb[k], self.pb_ap[pg][:], AF.Silu, [self.pb[pg]], [sg_b[k]])
                    self.tt(act_sb[a][:, gi, h * 512:(h + 1) * 512], sg_sb[k], self.pb_ap[pu][:], ALU.mult,
                            [sg_b[k], self.pb[pu]], [act_b[a][gi][h]])

        dunit = [0]

        def emit_d(g):
            a = g % 2
            s = g % 2
            js = groups[g]
            n = len(js)
            for c in range(16):
                for h in range(2):
                    bk = 4 + dunit[0] % 4
                    dunit[0] += 1
                    for gi in range(n):
                        self.mm(self.pb_ap[bk][:], wd_sb[s][:, gi, c * 128:(c + 1) * 128],
                                act_sb[a][:, gi, h * 512:(h + 1) * 512], gi == 0, gi == n - 1,
                                [wdc_b[s][gi], act_b[a][gi][h]], [self.pb[bk]])
                    xs = self.xT[:, c, h * 512:(h + 1) * 512]
                    self.stt(xs, self.pb_ap[bk][:], 0.5, xs, ALU.mult, ALU.add, [self.pb[bk]], [self.xb[c][h]])

        emit_gu(0)
        for g in range(ng):
            if g + 1 < ng:
                emit_gu(g + 1)
            pump(until_wd=g)
            emit_d(g)

    def ph_win(self, l):
        S = self.S
        w_d = self.din("win_%d" % l, [40, 128, 2048], F32)
        kt_out = self.dint("KT_own_%d" % l, [12, 128, 1024], BF16)
        v_out = self.dint("V_own_%d" % l, [12, 1024, 128], BF16)
        k_all = self.dint("KT_all_%d" % l, [4, 4 * 384, 1024], BF16)
        v_all = self.dint("V_all_%d" % l, [4, 4 * 3072, 128], BF16)
        self.dk_own = [Buf("dk%d" % i) for i in range(12)]
        self.dv_own = [Buf("dv%d" % i) for i in range(12)]
        self.dk_all = [Buf("dkall%d" % g) for g in range(4)]
        self.dv_all = [Buf("dvall%d" % g) for g in range(4)]
        cc_groups = [[0, 1, 2, 3], [4, 5, 6, 7]]
        bl = self.newbufs(["win%d" % i for i in range(NWR)] + ["kst0", "kst1", "vst0", "vst1", "sq0", "sq1"]
                          + ["wst%d" % i for i in range(NWS)])
        for b in self.qb:
            b.alias(self.region_bufs)
        win_b = bl[0:NWR]
        kst_b = bl[NWR:NWR + 2]
        vst_b = bl[NWR + 2:NWR + 4]
        sq_b = bl[NWR + 4:NWR + 6]
        wst_b = bl[NWR + 6:NWR + 6 + NWS]
        wst = [self.f32(O_WST + i * 8192, 2048) for i in range(NWS)]
        self.region_bufs = bl + self.qb
        win_sb = [self.bf(O_WIN + s * 4096, 2048).rearrange("p (c f) -> p c f", c=16) for s in range(NWR)]
        kst = [self.bf(O_KST + s * 2048, 1024) for s in range(2)]
        vst = [self.bf(O_VST + s * 6144, 3072).rearrange("p (t g c) -> p t g c", t=8, g=3) for s in range(2)]
        self.norm(l * 3 + 1, O_SQ_W, sq_b)
        role_idx = {}
        cnt = {"q": 0, "k": 0, "v": 0}
        for j in range(40):
            r = WIN_ROLE[j]
            role_idx[j] = (r, cnt[r])
            cnt[r] += 1
        order = (list(range(22, 26)) + [36, 37] + list(range(26, 30)) + [38, 39]
                 + list(range(6, 12)) + list(range(12, 18))
                 + list(range(18, 22)) + list(range(0, 6)) + list(range(30, 36)))
        assert sorted(order) == list(range(40))
        bank = [0]

        def nb():
            b = bank[0] % 8
            bank[0] += 1
            return b

        issued = [0]

        def pump(upto):
            while issued[0] <= min(upto, 39):
                p_ = issued[0]
                j = order[p_]
                s = p_ % NWR
                ws = p_ % NWS
                self.dma("sp", wst[ws], w_d[j], [], [wst_b[ws]])
                self.S.op("dve", (lambda e, o=win_sb[s].rearrange("p c f -> p (c f)"), i=wst[ws]:
                                  e.tensor_copy(out=o, in_=i)), reads=[wst_b[ws]], writes=[win_b[s]])
                issued[0] += 1

        kdone = set()
        vdone = set()
        nk = nv = 0
        for pos, j in enumerate(order):
            pump(pos + 4)
            s = pos % NWR
            role, idx = role_idx[j]
            if role in ("q", "k"):
                if role == "k":
                    ks = nk % 2
                    nk += 1
                for h in range(2):
                    b = nb()
                    for c in range(16):
                        self.mm(self.pb_ap[b][:], win_sb[s][:, c, :], self.hT[:, c, h * 512:(h + 1) * 512],
                                c == 0, c == 15, [win_b[s], self.hb[c]], [self.pb[b]])
                    if role == "q":
                        self.act(self.QT[:, idx, h * 512:(h + 1) * 512], self.pb_ap[b][:], AF.Copy,
                                 [self.pb[b]], [self.qb[idx]])
                    else:
                        self.act(kst[ks][:, h * 512:(h + 1) * 512], self.pb_ap[b][:], AF.Copy,
                                 [self.pb[b]], [kst_b[ks]])
                if role == "k":
                    self.dma("act", kt_out[idx], kst[ks], [kst_b[ks]], [self.dk_own[idx]])
                    kdone.add(idx)
                    g = idx // 3
                    if all((3 * g + i) in kdone for i in range(3)):
                        kin = kt_out[3 * g:3 * g + 3].rearrange("h p t -> (h p) t")
                        S.op("pool", (lambda e, kin=kin, g=g: e.collective_compute(
                            "AllGather", ALU.bypass, replica_groups=cc_groups, ins=[kin.opt()], outs=[k_all[g].opt()])),
                            reads=self.dk_own[3 * g:3 * g + 3], writes=[self.dk_all[g]], dma=True, inc=1)
            else:
                g = idx // 3
                if idx % 3 != 0:
                    continue
                vs = nv % 2
                nv += 1
                s0 = pos % NWR
                assert s0 + 2 < NWR and [role_idx[order[pos + i]] for i in range(3)] == [("v", idx + i) for i in range(3)]
                w3 = self.bf(O_WIN + s0 * 4096, 3 * 2048).rearrange("p (g c f) -> p g c f", g=3, c=16)
                for tt_ in range(8):
                    b = nb()
                    o3 = self.pb_ap[b][:, 0:384].rearrange("p (g f) -> p g f", g=3)
                    for c in range(16):
                        self.mm(o3, self.hT[:, c, tt_ * 128:(tt_ + 1) * 128], w3[:, :, c, :], c == 0, c == 15,
                                [win_b[s0], win_b[s0 + 1], win_b[s0 + 2], self.hb[c]], [self.pb[b]])
                    self.S.op("dve", (lambda e, o=vst[vs][:, tt_, :, :], i=o3: e.tensor_copy(out=o, in_=i)),
                              reads=[self.pb[b]], writes=[vst_b[vs]])
                for i in range(3):
                    self.dma("act", v_out[3 * g + i].rearrange("(t p) c -> p t c", p=128), vst[vs][:, :, i, :],
                             [vst_b[vs]], [self.dv_own[3 * g + i]])
                vin = v_out[3 * g:3 * g + 3].rearrange("h t c -> (h t) c")
                S.op("pool", (lambda e, vin=vin, g=g: e.collective_compute(
                    "AllGather", ALU.bypass, replica_groups=cc_groups, ins=[vin.opt()], outs=[v_all[g].opt()])),
                    reads=self.dv_own[3 * g:3 * g + 3], writes=[self.dv_all[g]], dma=True, inc=1)
        self.kall, self.vall = k_all, v_all

    def ph_wout(self, l):
        w_d = self.din("wout_%d" % l, [4, 128, 16 * 512], F32)
        bl = self.newbufs(["wo0", "wo1"])
        self.region_bufs = bl
        wo_sb = [self.bf(O_WO + s * 16384, 8192).rearrange("p (c d) -> p c d", c=16) for s in range(2)]
        for dg in range(2):
            self.dma("pool", wo_sb[dg].rearrange("p c d -> p (c d)"), w_d[dg], [], [bl[dg]])
        u = 0
        for dg in range(4):
            s = dg % 2
            for dc in range(4):
                c = dg * 4 + dc
                for h in range(2):
                    b = u % 8
                    u += 1
                    for hc in range(16):
                        self.mm(self.pb_ap[b][:], wo_sb[s][:, hc, dc * 128:(dc + 1) * 128],
                                self.hT[:, hc, h * 512:(h + 1) * 512], hc == 0, hc == 15,
                                [bl[s], self.hb[hc]], [self.pb[b]])
                    xs = self.xT[:, c, h * 512:(h + 1) * 512]
                    self.tt(xs, self.pb_ap[b][:], xs, ALU.add, [self.pb[b]], [self.xb[c][h]])
            if dg + 2 < 4:
                self.dma("pool", wo_sb[s].rearrange("p c d -> p (c d)"), w_d[dg + 2], [], [bl[s]])

    def ph_xchg(self, l):
        S = self.S
        k_all, v_all = self.kall, self.vall
        kpad = self.dint("Kpad_%d" % l, [8, 128, 3072], BF16)
        vpad = self.dint("Vpad_%d" % l, [8, 3072, 128], BF16)
        self.dkpad = [[Buf("dkpad") for _ in range(3)] for _ in range(3)]
        self.dvpad = [[Buf("dvpad") for _ in range(3)] for _ in range(3)]
        k_all4 = k_all.rearrange("g (r q) t -> g r q t", r=4)
        v_all4 = v_all.rearrange("g (r q) c -> g r q c", r=4)
        for k in (-1, 0, 1):
            c0 = (k + 1) * 1024
            for di, (g, hl, nh, dh) in enumerate([(0, 0, 3, 0), (1, 0, 3, 3), (3, 1, 2, 6)]):
                def kdma(e, k=k, c0=c0, g=g, hl=hl, nh=nh, dh=dh):
                    rank = self._rank_of(e, k)
                    src = k_all4[g][bass.ds(rank, 1)][0, hl * 128:(hl + nh) * 128, :]
                    dst = kpad[dh:dh + nh, :, c0:c0 + 1024].rearrange("h p t -> (h p) t")
                    return e.dma_start(out=dst, in_=src)

                def vdma(e, k=k, c0=c0, g=g, hl=hl, nh=nh, dh=dh):
                    rank = self._rank_of(e, k)
                    src = v_all4[g][bass.ds(rank, 1)][0, hl * 1024:(hl + nh) * 1024, :].rearrange("(h t) c -> h t c", h=nh)
                    return e.dma_start(out=vpad[dh:dh + nh, c0:c0 + 1024, :], in_=src)

                S.op("sp", kdma, reads=[self.dk_all[g]], writes=[self.dkpad[di][k + 1]], dma=True)
                S.op("sp", vdma, reads=[self.dv_all[g]], writes=[self.dvpad[di][k + 1]], dma=True)
        self.kpad, self.vpad = kpad, vpad

    def ph_attn(self, l):
        S = self.S
        lam_init = 0.8 - 0.6 * math.exp(-0.3 * l)
        slopes = alibi_slopes()
        sl_c, sl_a, sl_b = slopes[0:6], slopes[6:12], slopes[12:16]
        kall5 = self.kall.rearrange("g (r h p) t -> g r h p t", r=4, h=3)
        vall5 = self.vall.rearrange("g (r h t) c -> g r h t c", r=4, h=3)
        TB_d = self.din("TB", [128, 4992], F32)
        TAC_d = self.din("TAC", [128, 8320], BF16)
        lam_d = self.din("lamv_%d" % l, [256], F32)
        sub_d = self.din("subln_%d" % l, [128, 1], F32)
        sink_d = self.din("sink_%d" % l, [6], F32)
        names = ["tb", "tac", "stgk", "stgv", "sb0", "sb1", "sb2", "pt0", "pt1", "pt2", "pt3", "tmp0", "tmp1", "tmp2", "sm", "dacc0", "dacc1", "onesf",
                 "kb0", "kb1", "zk0", "zk1", "vb0", "vb1", "vb2", "vb3"]
        bl = self.newbufs(names)
        for b in self.qb:
            b.alias([x for x in self.region_bufs if x not in self.qb])
        B = dict(zip(names, bl))
        self.region_bufs = bl + self.qb
        TB = self.f32(O_TB, 4992)
        TA1 = self.bf(O_TA1, 1024)
        EA1 = self.bf(O_EA1, 1024).rearrange("p (a b) -> p a b", a=2)
        TA4 = self.bf(O_TA4, 2048).rearrange("p (a b) -> p a b", a=4)
        TA16 = self.bf(O_TA16, 2048).rearrange("p (a b) -> p a b", a=4)
        TC = self.bf(O_TC, 1152)
        EC = self.bf(O_EC, 1024).rearrange("p (a b) -> p a b", a=2)
        sbuf = [self.f32(O_SB + i * 2048, 512) for i in range(3)]
        sb_b = [B["sb0"], B["sb1"], B["sb2"]]
        ptb = [self.bf(O_PT + i * 1024, 512) for i in range(4)]
        pt_b = [B["pt0"], B["pt1"], B["pt2"], B["pt3"]]
        tmp = [self.f32(O_TMP + i * 2048, 512) for i in range(3)]
        tmp_b = [B["tmp0"], B["tmp1"], B["tmp2"]]
        sqb16 = self.bf(O_TMP + 2 * 2048, 512)
        lam_sb = self.f32(O_SM, 256)
        prod = self.f32(O_SM + 1024, 128)
        scal = self.f32(O_SM + 1536, 16)
        sink_sb = self.f32(O_SM + 1600, 6)
        esink = self.f32(O_SM + 1632, 6)
        gsub = self.f32(O_SM + 1664, 1)
        sm_b = B["sm"]
        dacc = [None, None]
        dacc_b = [B["dacc0"], B["dacc1"]]
        self.dma("sp", TB, TB_d, [], [B["tb"]])
        self.dma("sp", lam_sb, lam_d.partition_broadcast(128), [], [sm_b])
        self.dma("sp", sink_sb, sink_d.partition_broadcast(128), [], [sm_b])
        self.dma("sp", gsub, sub_d, [], [sm_b])
        self.tt(prod[:, 0:64], lam_sb[:, 0:64], lam_sb[:, 64:128], ALU.mult, [sm_b], [sm_b])
        self.tt(prod[:, 64:128], lam_sb[:, 128:192], lam_sb[:, 192:256], ALU.mult, [sm_b], [sm_b])
        S.op("dve", lambda e: e.reduce_sum(out=scal[:, 0:1], in_=prod[:, 0:64], axis=AX.X), reads=[sm_b], writes=[sm_b])
        S.op("dve", lambda e: e.reduce_sum(out=scal[:, 1:2], in_=prod[:, 64:128], axis=AX.X), reads=[sm_b], writes=[sm_b])
        self.act(scal[:, 2:4], scal[:, 0:2], AF.Exp, [sm_b], [sm_b])
        self.tt(scal[:, 4:5], scal[:, 3:4], scal[:, 2:3], ALU.subtract, [sm_b], [sm_b])
        S.op("dve", lambda e: e.tensor_scalar_add(out=scal[:, 5:6], in0=scal[:, 4:5], scalar1=-lam_init),
             reads=[sm_b], writes=[sm_b])
        S.op("dve", lambda e: e.tensor_scalar_mul(out=gsub, in0=gsub, scalar1=1.0 - lam_init), reads=[sm_b], writes=[sm_b])
        self.act(esink, sink_sb, AF.Exp, [sm_b], [sm_b])
        nlam = scal[:, 5:6]
        onesf = prod
        S.op("pool", lambda e: e.memset(onesf, 1.0), reads=[sm_b], writes=[B["onesf"]])
        onesm = self.ones

        cnt = [0]

        LAG = 3
        pending = []

        def pop_one():
            blk, (sbk, si, pi), fin = pending.pop(0)
            nk = blk["nk"]
            for pv in blk["pvs"]:
                (bank, out, lhsT, c0, c1, st_, sp_) = pv[:7]
                nkk = pv[7] if len(pv) > 7 else nk
                self.mm(out, lhsT, ptb[pi][0:nkk, c0:c1], st_, sp_, [pt_b[pi]] + blk["vreads"] + [self.const_b],
                        [self.pb[bank]], skip=True)
            if fin is not None:
                fin()

        def flush():
            while pending:
                pop_one()

        def push(blk, fin=None):
            k = cnt[0]
            cnt[0] += 1
            sbk = 4 + k % 4
            si = k % 3
            pi = k % 4
            nk, ncol = blk["nk"], blk["ncol"]
            for (c0, c1, lhsT, rhs) in blk["scores"]:
                self.mm(self.pb_ap[sbk][0:nk, c0:c1], lhsT, rhs, True, True,
                        blk["kreads"] + blk["qreads"], [self.pb[sbk]])
            self.stt(sbuf[si][0:nk, 0:ncol], blk["tab"], blk["coef"], self.pb_ap[sbk][0:nk, 0:ncol],
                     ALU.mult, ALU.add, [blk["tabb"], self.pb[sbk]], [sb_b[si]])
            self.act(ptb[pi][0:nk, 0:ncol], sbuf[si][0:nk, 0:ncol], AF.Exp, [sb_b[si]], [pt_b[pi]],
                     scale=blk["scale"])
            pending.append((blk, (sbk, si, pi), fin))
            if len(pending) > LAG:
                pop_one()

        def run_blocks(blocks, fin=None, hook=None):
            n = len(blocks)
            for i, blk in enumerate(blocks):
                push(blk, fin if i == n - 1 else None)
                if hook is not None and i == LAG - 1:
                    hook()

        stg = O_STG
        KBm = [self.bf(stg, 4096), self.bf(stg + 8192, 4096)]
        VBs = self.bf(O_TA1, 4096).rearrange("p (n c) -> p n c", n=32)
        kb_b = [B["kb0"], B["kb1"]]
        zk_b = [B["zk0"], B["zk1"]]
        vb_b = [B["vb0"], B["vb1"], B["vb2"], B["vb3"]]
        S.op("pool", lambda e: e.memset(KBm[0][64:128, :], 0.0), writes=[zk_b[0]])
        S.op("pool", lambda e: e.memset(KBm[1][0:64, :], 0.0), writes=[zk_b[1]])
        sc_b = 0.125
        for h in range(4):
            flush()
            gg, hl = (6 + h) // 3, (6 + h) % 3
            for m in range(2):
                rs = slice(64 * m, 64 * m + 64)
                self.dma("sp", KBm[m][rs, :].rearrange("p (r t) -> p r t", r=4),
                         kall5[gg, :, hl, rs, :].rearrange("r p t -> p r t"), [self.dk_all[gg]], [kb_b[m]])
            for r4 in range(4):
                self.dma("sp", VBs[:, r4 * 8:(r4 + 1) * 8, :], vall5[gg, r4, hl].rearrange("(n p) c -> p n c", p=128),
                         [self.dv_all[gg]], [vb_b[r4]])
            ch = 6 + h
            for qb in range(2):
                qs = slice(qb * 512, (qb + 1) * 512)
                blocks = []
                for kb in range(32):
                    for m in range(2):
                        u0 = qb * 512 - kb * 128 + 3968
                        blocks.append(dict(
                            nk=128, ncol=512,
                            scores=[(0, 512, KBm[m][:, kb * 128:(kb + 1) * 128], self.QT[:, ch, qs])],
                            kreads=[kb_b[m], zk_b[m]], qreads=[self.qb[ch]],
                            tab=TB[:, u0:u0 + 512], tabb=B["tb"], coef=-sl_b[h] / sc_b, scale=sc_b,
                            pvs=[(m, self.pb_ap[m][:], VBs[:, kb, :], 0, 512, kb == 0, kb == 31),
                                 (2 + m, self.pb_ap[2 + m][:], onesm, 0, 512, kb == 0, kb == 31)]
                            + [(m, self.pb_ap[m][0:120, 0:DUMMY_N], self.zeros, 0, DUMMY_N, False, False, 1)] * N_DUMMY,
                            vreads=[vb_b[kb // 8]]))
                def fin_b(ch=ch, qs=qs):
                    t0, t1, t2 = tmp
                    self.powact(t0, self.pb_ap[2][:], [self.pb[2]], [tmp_b[0]])
                    self.powact(t1, self.pb_ap[3][:], [self.pb[3]], [tmp_b[1]])
                    self.tt(t0, self.pb_ap[0][:], t0, ALU.mult, [self.pb[0]], [tmp_b[0]])
                    self.tt(t1, self.pb_ap[1][:], t1, ALU.mult, [self.pb[1]], [tmp_b[1]])
                    self.stt(t0, t1, nlam, t0, ALU.mult, ALU.add, [tmp_b[1], sm_b], [tmp_b[0]])
                    self.act(sqb16, t0, AF.Square, [tmp_b[0]], [tmp_b[2]])
                    sbk = 4 + cnt[0] % 4
                    self.mm(self.pb_ap[sbk][:], onesm, sqb16, True, True, [tmp_b[2], self.const_b], [self.pb[sbk]])
                    self.powact(t1, self.pb_ap[sbk][:], [self.pb[sbk], self.const_b], [tmp_b[1]],
                                power=-0.5, scale_in=1.0 / 128, bias=self.eps)
                    self.stt(self.hT[:, ch, qs], t0, gsub, t1, ALU.mult, ALU.mult, [tmp_b[0], tmp_b[1], sm_b], [self.hb[ch]])

                run_blocks(blocks, fin_b)
        flush()

        B["stgk"].alias(kb_b + zk_b)
        B["stgv"].alias(kb_b + zk_b)
        B["tac"].alias(vb_b)
        self.ph_xchg(l)
        kpad, vpad = self.kpad, self.vpad
        self.dma("sp", self.bf(O_TA1, 8320), TAC_d, [], [B["tac"]])
        stg_off = [stg, O_TB]
        A_KEYS = ["k", "v1", "v4_0", "v4_1", "v4_2", "v4_3", "v16a", "v16b"]
        stgA_b = [{kk: Buf("a0" + kk).alias([B["stgk"], B["stgv"]]) for kk in A_KEYS},
                  {kk: Buf("a1" + kk).alias([B["tb"]]) for kk in A_KEYS}]
        sc_a = 128 ** -0.5

        def a_views(sl):
            o = stg_off[sl]
            return (self.bf(o, 2560),
                    self.bf(o + 5120, 640).rearrange("p (n c) -> p n c", n=5),
                    self.bf(o + 6400, 1024).rearrange("p (r n c) -> p r n c", r=4, n=2),
                    self.bf(o + 8448, 2048).rearrange("p (r c) -> p r c", r=16),
                    self.bf(o + 12544, 2048).rearrange("p (r c) -> p r c", r=16))

        def a_loads(h, qb, sl):
            KAw, VA1s, VA4s, VA16a, VA16b = a_views(sl)
            bb = stgA_b[sl]
            q0 = qb * 512
            di = h // 3
            self.dma("sp", KAw, kpad[h][:, q0:q0 + 2560], self.dkpad[di], [bb["k"]])
            self.dma("sp", VA1s, vpad[h][960 + q0:960 + q0 + 640].rearrange("(n p) c -> p n c", p=128),
                     self.dvpad[di], [bb["v1"]])
            for r in range(4):
                st4 = 768 + r + q0
                self.dma("sp", VA4s[:, r, :, :],
                         vpad[h][st4:st4 + 4 * 255 + 1:4].rearrange("(n p) c -> p n c", p=128),
                         self.dvpad[di], [bb["v4_%d" % r]])
            self.dma("sp", VA16a, vpad[h][q0:q0 + 2048].rearrange("(p r) c -> p r c", r=16), self.dvpad[di], [bb["v16a"]])
            self.dma("sp", VA16b[0:32], vpad[h][q0 + 2048:q0 + 2560].rearrange("(p r) c -> p r c", r=16),
                     self.dvpad[di], [bb["v16b"]])

        def a_compute(h, qb, sl, ab, hook=None):
            bn, bd = 2 * ab, 2 * ab + 1
            KAw, VA1s, VA4s, VA16a, VA16b = a_views(sl)
            bb = stgA_b[sl]
            q0 = qb * 512
            coef = -sl_a[h] / sc_a
            qcol = self.QT[:, h, q0:q0 + 512]
            common = dict(kreads=[bb["k"]], qreads=[self.qb[h]], tabb=B["tac"], coef=coef, scale=sc_a)
            blocks = []
            for n in range(5):
                if qb == 0 and n == 0:
                    tab = EA1[:, 0, :]
                elif qb == 1 and n == 4:
                    tab = EA1[:, 1, :]
                else:
                    u0 = 512 - 128 * n
                    tab = TA1[:, u0:u0 + 512]
                kc = 960 + 128 * n
                blocks.append(dict(
                    nk=128, ncol=512, scores=[(0, 512, KAw[:, kc:kc + 128], qcol)], tab=tab,
                    pvs=[(bn, self.pb_ap[bn][:], VA1s[:, n, :], 0, 512, n == 0, False),
                         (bd, self.pb_ap[bd][:], onesm, 0, 512, n == 0, False)], vreads=[bb["v1"]], **common))
            for n in range(2):
                scores = []
                pvs = []
                for r in range(4):
                    s_r = 768 + 512 * n + r
                    scores.append((r * 128, (r + 1) * 128, KAw[:, s_r:s_r + 4 * 127 + 1:4], self.QT[:, h, q0 + r:q0 + 512:4]))
                    pvs.append((bn, self.pb_ap[bn][:, r:512:4], VA4s[:, r, n, :], r * 128, (r + 1) * 128, False, False))
                    pvs.append((bd, self.pb_ap[bd][:, r:512:4], onesm, r * 128, (r + 1) * 128, False, False))
                blocks.append(dict(nk=128, ncol=512, scores=scores, tab=TA4[:, 2 * qb + n, :], pvs=pvs,
                                   vreads=[bb["v4_%d" % r] for r in range(4)], **common))
            for n in range(2):
                nk = 128 if n == 0 else 32
                scores = []
                pvs = []
                for r in range(16):
                    s_r = r if n == 0 else 2048 + r
                    scores.append((r * 32, (r + 1) * 32, KAw[:, s_r:s_r + 16 * (nk - 1) + 1:16], self.QT[:, h, q0 + r:q0 + 512:16]))
                    vt = VA16a[:, r, :] if n == 0 else VA16b[0:32, r, :]
                    last = (n == 1 and r == 15)
                    pvs.append((bn, self.pb_ap[bn][:, r:512:16], vt, r * 32, (r + 1) * 32, False, last))
                    pvs.append((bd, self.pb_ap[bd][:, r:512:16], onesm[0:nk, :], r * 32, (r + 1) * 32, False, last))
                blocks.append(dict(nk=nk, ncol=512, scores=scores, tab=TA16[0:nk, 2 * qb + n, :], pvs=pvs,
                                   vreads=[bb["v16a"] if n == 0 else bb["v16b"]], **common))
            def fin_a(h=h, q0=q0, ab=ab, bn=bn, bd=bd):
                t0 = tmp[ab]
                self.powact(t0, self.pb_ap[bd][:], [self.pb[bd]], [tmp_b[ab]])
                self.tt(self.hT[:, h, q0:q0 + 512], self.pb_ap[bn][:], t0, ALU.mult, [self.pb[bn], tmp_b[ab]], [self.hb[h]])

            run_blocks(blocks, fin_a, hook)

        unitsA = [(h, qb) for h in range(6) for qb in range(2)]
        a_loads(unitsA[0][0], unitsA[0][1], 0)
        for ui, (h, qb) in enumerate(unitsA):
            hk = None
            if ui + 1 < len(unitsA):
                hk = (lambda u=ui + 1: a_loads(unitsA[u][0], unitsA[u][1], u % 2))
            a_compute(h, qb, ui % 2, ui % 2, hk)
        flush()

        B["stgk"].alias(list(stgA_b[0].values()))
        B["stgv"].alias(list(stgA_b[0].values()))
        KCw = self.bf(stg, 1280)
        VCs = self.bf(stg + 2560, 1280).rearrange("p (n c) -> p n c", n=10)
        sc_c = 128 ** -0.5
        for g in range(2):
            flush()
            self.dma("sp", KCw, kpad[6 + g][:, 896:2176], self.dkpad[2], [B["stgk"]])
            self.dma("sp", VCs, vpad[6 + g][896:2176].rearrange("(n p) c -> p n c", p=128), self.dvpad[2], [B["stgv"]])
            for hh in range(3):
                hq = g * 3 + hh
                ch = 10 + hq
                for qb in range(2):
                    q0 = qb * 512
                    ab = (hq * 2 + qb) % 2
                    bn, bd = 2 * ab, 2 * ab + 1
                    blocks = []
                    for n in range(6):
                        if qb == 0 and n == 0:
                            tab = EC[:, 0, :]
                        elif qb == 1 and n == 5:
                            tab = EC[:, 1, :]
                        else:
                            u0 = 640 - 128 * n
                            tab = TC[:, u0:u0 + 512]
                        kt = 4 * qb + n
                        blocks.append(dict(
                            nk=128, ncol=512,
                            scores=[(0, 512, KCw[:, kt * 128:(kt + 1) * 128], self.QT[:, ch, q0:q0 + 512])],
                            kreads=[B["stgk"]], qreads=[self.qb[ch]], tab=tab, tabb=B["tac"],
                            coef=-sl_c[hq] / sc_c, scale=sc_c,
                            pvs=[(bn, self.pb_ap[bn][:], VCs[:, kt, :], 0, 512, n == 0, n == 5),
                                 (bd, self.pb_ap[bd][:], onesm, 0, 512, n == 0, n == 5)],
                            vreads=[B["stgv"]]))
                    def fin_c(ch=ch, q0=q0, ab=ab, bn=bn, bd=bd, hq=hq):
                        t0 = tmp[ab]
                        self.powact(t0, self.pb_ap[bd][:], [self.pb[bd], sm_b], [tmp_b[ab]], bias=esink[:, hq:hq + 1])
                        self.tt(self.hT[:, ch, q0:q0 + 512], self.pb_ap[bn][:], t0, ALU.mult,
                                [self.pb[bn], tmp_b[ab]], [self.hb[ch]])

                    run_blocks(blocks, fin_c)
        flush()


def _lay_gu(w):
    F = w.shape[1]
    return np.ascontiguousarray(w.reshape(16, 128, F // 128, 128).transpose(2, 1, 0, 3)).reshape(F // 128, 128, 2048)


def _lay_wout(w):
    return np.ascontiguousarray(w.reshape(16, 128, 4, 512).transpose(2, 1, 0, 3)).reshape(4, 128, 16 * 512)


def _gains(inputs):
    g = np.zeros((7, 2048), np.float32)
    for l in range(DEPTH):
        g[l * 3 + 0] = inputs["ffn1_norm"][l]
        g[l * 3 + 1] = inputs["mix_norm"][l]
        g[l * 3 + 2] = inputs["ffn2_norm"][l]
    g[6] = inputs["final_norm"]
    return np.ascontiguousarray(g.reshape(7, 16, 128).transpose(2, 0, 1)).reshape(128, 7 * 16)


_PROG_CACHE = {}


def get_prog(phases):
    key = repr(phases)
    if key not in _PROG_CACHE:
        _PROG_CACHE[key] = Prog(phases)
    return _PROG_CACHE[key]


def _tables(cc):
    i = np.arange(128, dtype=np.int64)[:, None]
    u = np.arange(4992, dtype=np.int64)[None, :]
    TB = np.abs(cc * 1024 + u - 3968 - i).astype(np.float32)

    def band(delta, radius, mult, ok):
        d = np.abs(delta)
        return np.where((d <= radius) & ok, (mult * d).astype(np.float32), np.float32(BIG))

    j = np.arange(512, dtype=np.int64)[None, :]
    u1 = np.arange(1024, dtype=np.int64)[None, :]
    TA1 = band(u1 - 448 - i, 64, 1, True)
    k = -64 + i
    EA1_0 = band(j + 64 - i, 64, 1, (cc * 1024 + k >= 0))
    k = 960 + i
    EA1_1 = band(j - 448 - i, 64, 1, (cc * 1024 + k < SEQ))
    TA4 = []
    for qb in range(2):
        for n in range(2):
            jp = (np.arange(512, dtype=np.int64) % 128)[None, :]
            kg = cc * 256 + qb * 128 - 64 + 128 * n + i
            TA4.append(band(jp + 64 - 128 * n - i, 64, 4, (kg >= 0) & (kg < 1024)))
    TA16 = []
    for qb in range(2):
        for n in range(2):
            jp = (np.arange(512, dtype=np.int64) % 32)[None, :]
            if n == 0:
                delta = jp + 64 - i
                kp = qb * 32 - 64 + i
                ok = np.ones_like(i, dtype=bool)
            else:
                delta = jp - 64 - i
                kp = qb * 32 + 64 + i
                ok = i < 32
            kg = cc * 64 + kp
            TA16.append(band(delta, 64, 16, ok & (kg >= 0) & (kg < 256)))
    uc = np.arange(1152, dtype=np.int64)[None, :]
    TC = band(uc - 512 - i, 128, 1, True)
    k = -128 + i
    EC_0 = band(j + 128 - i, 128, 1, (cc * 1024 + k >= 0))
    k = 1024 + i
    EC_1 = band(j - 512 - i, 128, 1, (cc * 1024 + k < SEQ))
    TAC = np.concatenate([TA1, EA1_0, EA1_1] + TA4 + TA16 + [TC, EC_0, EC_1], axis=1)
    assert TAC.shape == (128, 8320)
    return TB, np.ascontiguousarray(TAC.astype(NPBF))


def _attn_small(inputs, l):
    lamv = np.concatenate([inputs["diff_lambda_q1"][l], inputs["diff_lambda_k1"][l],
                           inputs["diff_lambda_q2"][l], inputs["diff_lambda_k2"][l]]).astype(np.float32)
    return {"lamv_%d" % l: lamv,
            "subln_%d" % l: np.ascontiguousarray(inputs["diff_subln"][l].reshape(128, 1)).astype(np.float32),
            "sink_%d" % l: np.ascontiguousarray(inputs["swa_sink"][l]).astype(np.float32)}


def _run(phases, shared, percore):
    prog = get_prog(phases)
    in_maps = []
    for c in range(NCORES):
        m = dict(shared)
        m.update(percore[c])
        in_maps.append(m)
    res = run_bass_kernel_spmd(prog.nc, in_maps, core_ids=list(range(NCORES)))
    return res.results


PH_FUSED = [("load_x",),
            ("ffn", 0, 1), ("win", 0), ("attn", 0), ("wout", 0), ("ffn", 0, 2),
            ("ffn", 1, 1), ("win", 1), ("attn", 1), ("wout", 1), ("ffn", 1, 2),
            ("final",)]


def kernel(x, ffn1_norm, ffn1_w_gate, ffn1_w_up, ffn1_w_down, mix_norm, w_in, w_out,
           diff_lambda_q1, diff_lambda_k1, diff_lambda_q2, diff_lambda_k2, diff_subln,
           swa_sink, ffn2_norm, ffn2_w_gate, ffn2_w_up, ffn2_w_down, final_norm):
    inputs = dict(x=x, ffn1_norm=ffn1_norm, ffn1_w_gate=ffn1_w_gate, ffn1_w_up=ffn1_w_up,
                  ffn1_w_down=ffn1_w_down, mix_norm=mix_norm, w_in=w_in, w_out=w_out,
                  diff_lambda_q1=diff_lambda_q1, diff_lambda_k1=diff_lambda_k1,
                  diff_lambda_q2=diff_lambda_q2, diff_lambda_k2=diff_lambda_k2, diff_subln=diff_subln,
                  swa_sink=swa_sink, ffn2_norm=ffn2_norm, ffn2_w_gate=ffn2_w_gate, ffn2_w_up=ffn2_w_up,
                  ffn2_w_down=ffn2_w_down, final_norm=final_norm)
    inputs = {k: np.asarray(v) for k, v in inputs.items()}
    x = inputs["x"]
    shared = {"gains": _gains(inputs)}
    ffn_w = {1: (inputs["ffn1_w_gate"], inputs["ffn1_w_up"], inputs["ffn1_w_down"]),
             2: (inputs["ffn2_w_gate"], inputs["ffn2_w_up"], inputs["ffn2_w_down"])}
    for l in range(DEPTH):
        for which in (1, 2):
            wg, wu, wd = ffn_w[which]
            tag = "%d_%d" % (l, which)
            shared["wg_" + tag] = _lay_gu(wg[l])
            shared["wu_" + tag] = _lay_gu(wu[l])
            shared["wd_" + tag] = np.ascontiguousarray(wd[l]).reshape(NF, 128, 2048)
        shared["win_%d" % l] = _lay_gu(inputs["w_in"][l])
        shared["wout_%d" % l] = _lay_wout(inputs["w_out"][l])
        shared.update(_attn_small(inputs, l))
    tabs = [_tables(cc) for cc in range(4)]
    pc = []
    for c in range(NCORES):
        b, cc = c // 4, c % 4
        pc.append({"xT_in": np.ascontiguousarray(x[b, cc * 1024:(cc + 1) * 1024, :].T).reshape(16, 128, 1024),
                   "TB": tabs[cc][0], "TAC": tabs[cc][1]})
    res = _run(PH_FUSED, shared, pc)
    out = np.empty((BATCH, SEQ, D_MODEL), np.float32)
    for c in range(NCORES):
        b, cc = c // 4, c % 4
        out[b, cc * 1024:(cc + 1) * 1024, :] = res[c]["outT"].reshape(2048, 1024).T
    return out
```
